# Optimizing a Trainium2 kernel written in Bass

```python
import math
import jax, jax.numpy as jnp
from jax import lax
import numpy as np

D_MODEL = 1024
BATCH = 2
SEQ = 8192
DEPTH = 2

N_MIXERS = 2
EPS = 1e-6
POOL_WINDOWS = (2, 4, 8, 16)
N_POOL_GROUPS = len(POOL_WINDOWS)
POOL_GROUP_DIM = D_MODEL // N_POOL_GROUPS
HEAD_DIM = 64
N_HEADS = D_MODEL // HEAD_DIM
D_ATTN = N_HEADS * HEAD_DIM
ATTN_PATTERNS = ((128, 1), (512, 4), (2048, 16))
N_ATTN_GROUPS = len(ATTN_PATTERNS)
HEAD_GROUPS = tuple(N_HEADS // N_ATTN_GROUPS + (1 if g < N_HEADS % N_ATTN_GROUPS else 0) for g in range(N_ATTN_GROUPS))
ROPE_THETA = 10000.0
D_FF = -(-8 * D_MODEL // (3 * 256)) * 256
N_POOL_LAYERS = (DEPTH + 1) // 2
N_ATTN_LAYERS = DEPTH // 2
NEG_INF = -1e30

kernel_name = "hybrid_pool_dilated_swa_swiglu"


def rmsnorm(x, g):
    xf = x.astype(jnp.float32)
    y = xf * lax.rsqrt(jnp.mean(xf * xf, axis=-1, keepdims=True) + EPS)
    return (y * g.astype(jnp.float32)).astype(x.dtype)


def rope(t):
    S, hd = t.shape[1], t.shape[-1]
    inv_freq = 1.0 / (ROPE_THETA ** (jnp.arange(0, hd, 2, dtype=jnp.float32) / hd))
    ang = jnp.arange(S, dtype=jnp.float32)[:, None] * inv_freq[None, :]
    ang = jnp.concatenate([ang, ang], axis=-1)[None, :, None, :]
    tf = t.astype(jnp.float32)
    t1, t2 = tf[..., : hd // 2], tf[..., hd // 2 :]
    rot = jnp.concatenate([-t2, t1], axis=-1)
    return (tf * jnp.cos(ang) + rot * jnp.sin(ang)).astype(t.dtype)


def pool_mixer(h, w_in, w_group, scale, w_out):
    B, S, _ = h.shape
    u = jnp.einsum('bsd,de->bse', h, w_in).reshape(B, S, N_POOL_GROUPS, POOL_GROUP_DIM)
    cs = jnp.cumsum(u.astype(jnp.float32), axis=1)
    pos = jnp.arange(S)
    outs = []
    for g, w in enumerate(POOL_WINDOWS):
        c = cs[:, :, g]
        lag = jnp.pad(c, ((0, 0), (w, 0), (0, 0)))[:, :S]
        cnt = jnp.minimum(pos + 1, w).astype(jnp.float32)[None, :, None]
        outs.append((c - lag) / cnt - u[:, :, g].astype(jnp.float32))
    p = jnp.stack(outs, axis=2).astype(h.dtype)
    z = jnp.einsum('bsgc,gce->bsge', p, w_group).reshape(B, S, D_MODEL) * scale
    return jnp.einsum('bsd,de->bse', z, w_out)


def dilated_window_attention(q, k, v, window, dilation):
    B, S, H, hd = q.shape
    w = window // dilation
    L = S // dilation
    nb = -(-L // w)
    Lp = nb * w

    def to_blocks(t):
        t = t.reshape(B, L, dilation, H, hd)
        t = jnp.pad(t, ((0, 0), (0, Lp - L), (0, 0), (0, 0), (0, 0)))
        return t.reshape(B, nb, w, dilation, H, hd)

    qb, kb, vb = to_blocks(q), to_blocks(k), to_blocks(v)

    def with_prev(t):
        prev = jnp.concatenate([jnp.zeros_like(t[:, :1]), t[:, :-1]], axis=1)
        return jnp.concatenate([prev, t], axis=2)

    kc, vc = with_prev(kb), with_prev(vb)
    s = jnp.einsum('bnqrhd,bnkrhd->bnrhqk', qb, kc).astype(jnp.float32)
    qi = jnp.arange(w)[:, None]
    kj = jnp.arange(2 * w)[None, :]
    dist = w + qi - kj
    key_idx = jnp.arange(nb)[:, None, None] * w - w + kj[None]
    mask = (dist >= 0)[None] & (dist <= w)[None] & (key_idx >= 0)
    s = jnp.where(mask[None, :, None, None], s, NEG_INF)
    m = jnp.max(s, axis=-1, keepdims=True)
    e = jnp.exp(s - m)
    den = jnp.sum(e, axis=-1, keepdims=True)
    p = (e / den).astype(v.dtype)
    o = jnp.einsum('bnrhqk,bnkrhd->bnqrhd', p, vc)
    lse = (m + jnp.log(den))[..., 0]
    lse = jnp.transpose(lse, (0, 1, 4, 2, 3))
    o = o.reshape(B, Lp, dilation, H, hd)[:, :L].reshape(B, S, H, hd)
    lse = lse.reshape(B, Lp, dilation, H)[:, :L].reshape(B, S, H)
    return o, lse


def attn_mixer(h, w_qkv, w_out):
    B, S, _ = h.shape
    qkv = jnp.einsum('bsd,de->bse', h, w_qkv).reshape(B, S, 3, N_HEADS, HEAD_DIM)
    q = rope(qkv[:, :, 0]) * jnp.asarray(HEAD_DIM ** -0.5, dtype=h.dtype)
    k = rope(qkv[:, :, 1])
    v = qkv[:, :, 2]
    outs, lses = [], []
    start = 0
    for (window, dilation), n_g in zip(ATTN_PATTERNS, HEAD_GROUPS):
        sl = slice(start, start + n_g)
        o_g, lse_g = dilated_window_attention(q[:, :, sl], k[:, :, sl], v[:, :, sl], window, dilation)
        outs.append(o_g)
        lses.append(jax.nn.logsumexp(lse_g, axis=-1) - math.log(n_g))
        start += n_g
    alpha = jax.nn.softmax(jnp.stack(lses, axis=-1), axis=-1)
    merged = jnp.concatenate(
        [o_g * (N_ATTN_GROUPS * alpha[:, :, g]).astype(o_g.dtype)[:, :, None, None] for g, o_g in enumerate(outs)],
        axis=2,
    ).reshape(B, S, D_ATTN)
    return jnp.einsum('bse,ed->bsd', merged, w_out)


def swiglu(h, w_gate, w_up, w_down):
    g = jnp.einsum('bsd,df->bsf', h, w_gate)
    u = jnp.einsum('bsd,df->bsf', h, w_up)
    return jnp.einsum('bsf,fd->bsd', jax.nn.silu(g) * u, w_down)


def setup_inputs(seed: int = 0) -> dict:
    key = jax.random.key(seed)
    ks = jax.random.split(key, 14)
    f32 = jnp.float32
    nrm = lambda k, shape, fan_in: jax.random.normal(k, shape, f32) * (fan_in ** -0.5)
    return {
        "x": jax.random.normal(ks[0], (BATCH, SEQ, D_MODEL), f32),
        "norm_mix": 1.0 + 0.05 * jax.random.normal(ks[1], (DEPTH, D_MODEL), f32),
        "norm_ffn": 1.0 + 0.05 * jax.random.normal(ks[2], (DEPTH, D_MODEL), f32),
        "norm_final": 1.0 + 0.05 * jax.random.normal(ks[3], (D_MODEL,), f32),
        "pool_w_in": nrm(ks[4], (N_POOL_LAYERS, D_MODEL, D_MODEL), D_MODEL),
        "pool_w_group": nrm(ks[5], (N_POOL_LAYERS, N_POOL_GROUPS, POOL_GROUP_DIM, POOL_GROUP_DIM), POOL_GROUP_DIM),
        "pool_scale": 1.0 + 0.1 * jax.random.normal(ks[6], (N_POOL_LAYERS, D_MODEL), f32),
        "pool_w_out": nrm(ks[7], (N_POOL_LAYERS, D_MODEL, D_MODEL), D_MODEL),
        "attn_w_qkv": nrm(ks[8], (N_ATTN_LAYERS, D_MODEL, 3 * D_ATTN), D_MODEL),
        "attn_w_out": nrm(ks[9], (N_ATTN_LAYERS, D_ATTN, D_MODEL), D_ATTN),
        "ffn_w_gate": nrm(ks[10], (DEPTH, D_MODEL, D_FF), D_MODEL),
        "ffn_w_up": nrm(ks[11], (DEPTH, D_MODEL, D_FF), D_MODEL),
        "ffn_w_down": nrm(ks[12], (DEPTH, D_FF, D_MODEL), D_FF),
    }


def reference(x, norm_mix, norm_ffn, norm_final, pool_w_in, pool_w_group, pool_scale, pool_w_out,
              attn_w_qkv, attn_w_out, ffn_w_gate, ffn_w_up, ffn_w_down):
    for i in range(DEPTH):
        h = rmsnorm(x, norm_mix[i])
        j = i // N_MIXERS
        if i % N_MIXERS == 0:
            y = pool_mixer(h, pool_w_in[j], pool_w_group[j], pool_scale[j], pool_w_out[j])
        else:
            y = attn_mixer(h, attn_w_qkv[j], attn_w_out[j])
        x = x + y
        h = rmsnorm(x, norm_ffn[i])
        x = x + swiglu(h, ffn_w_gate[i], ffn_w_up[i], ffn_w_down[i])
    return rmsnorm(x, norm_final)
```

```python
import numpy as np
from contextlib import ExitStack
import concourse.bass as bass
import concourse.mybir as mybir
from concourse.bass_utils import run_bass_kernel_spmd

F32 = mybir.dt.float32
BF16 = mybir.dt.bfloat16
ALU = mybir.AluOpType
AF = mybir.ActivationFunctionType
DSZ = {F32: 4, BF16: 2}

ENGS = ("pe", "act", "dve", "pool", "sp")
ENG_ATTR = {"pe": "tensor", "act": "scalar", "dve": "vector", "pool": "gpsimd", "sp": "sync"}


class _Op:
    __slots__ = ("fn", "waits", "dma", "dsem", "inc")

    def __init__(self, fn, waits, dma, dsem, inc=16):
        self.fn, self.waits, self.dma, self.dsem, self.inc = fn, waits, dma, dsem, inc


class Sched:
    def __init__(self, nc, n_dma_sems=48):
        self.nc = nc
        self.ops = {e: [] for e in ENGS}
        self.count = {e: 0 for e in ENGS}
        self.waited = {e: {} for e in ENGS}
        self.acc = {}
        self.n_dma = n_dma_sems
        self.dma_val = [0] * n_dma_sems
        half = n_dma_sems // 2
        self.dma_pools = {"sw": list(range(0, half - 2)), "hw": list(range(half, n_dma_sems)),
                          "cc": list(range(half - 2, half))}
        self.dma_rr = {"sw": 0, "hw": 0, "cc": 0}

    @staticmethod
    def rng(ap):
        t = ap.tensor
        name = t.name
        esz = DSZ[ap.dtype]
        pat = ap.ap
        off = ap.offset
        shape = list(t.shape)
        if type(t).__name__.startswith("DRam") or len(shape) < 2 or "DRam" in type(t).__name__:
            lo = off
            hi = off + sum((c - 1) * abs(s) for s, c in pat) + 1
            return (name, 0, 1, lo * esz, hi * esz)
        F = 1
        for s in shape[1:]:
            F *= s
        p0 = off // F
        c0 = off % F
        pcount = pat[0][1]
        hi = c0 + sum((c - 1) * abs(s) for s, c in pat[1:]) + 1
        if name == "ps":
            lo_b = (c0 * esz) // 2048
            hi_b = (hi * esz + 2047) // 2048
            return (name, 0, 128, lo_b * 2048, hi_b * 2048)
        return (name, p0, p0 + pcount, c0 * esz, hi * esz)

    def _deps(self, reads, writes):
        toks = []
        for ap in reads:
            n, p0, p1, lo, hi = self.rng(ap)
            a = self.acc.setdefault(n, {"w": [], "r": []})
            for (q0, q1, l, h, tok) in a["w"]:
                if q0 < p1 and p0 < q1 and l < hi and lo < h:
                    toks.append(tok)
        for ap in writes:
            n, p0, p1, lo, hi = self.rng(ap)
            a = self.acc.setdefault(n, {"w": [], "r": []})
            for (q0, q1, l, h, tok) in a["w"]:
                if q0 < p1 and p0 < q1 and l < hi and lo < h:
                    toks.append(("W",) + tok)
            for (q0, q1, l, h, tok) in a["r"]:
                if q0 < p1 and p0 < q1 and l < hi and lo < h:
                    toks.append(("W",) + tok)
        return toks

    def _record(self, reads, writes, tok):
        for ap in writes:
            n, p0, p1, lo, hi = self.rng(ap)
            a = self.acc.setdefault(n, {"w": [], "r": []})
            a["w"] = [e for e in a["w"] if not (p0 <= e[0] and e[1] <= p1 and lo <= e[2] and e[3] <= hi)]
            a["r"] = [e for e in a["r"] if not (p0 <= e[0] and e[1] <= p1 and lo <= e[2] and e[3] <= hi)]
            a["w"].append((p0, p1, lo, hi, tok))
        for ap in reads:
            n, p0, p1, lo, hi = self.rng(ap)
            a = self.acc.setdefault(n, {"w": [], "r": []})
            a["r"] = [e for e in a["r"] if not (e[4][0] == tok[0] and tok[0] != "dma" and p0 <= e[0]
                                                and e[1] <= p1 and lo <= e[2] and e[3] <= hi)]
            a["r"].append((p0, p1, lo, hi, tok))

    def add(self, eng, fn, reads=(), writes=(), dma=False, extra_waits=(), cc=False):
        ps_reads = [ap for ap in reads if ap.tensor.name == "ps"]
        if ps_reads:
            reads = [ap for ap in reads if ap.tensor.name != "ps"]
            writes = list(writes) + ps_reads
        toks = self._deps(reads, writes) + list(extra_waits)
        waits = []
        wd = self.waited[eng]
        for tok in toks:
            war = False
            if tok[0] == "W":
                war = True
                tok = tok[1:]
            if tok[0] == "dma":
                key, val = ("dma", tok[1]), tok[2]
            else:
                key, val = tok[0], tok[1]
                if key == eng:
                    if eng in ("pe", "sp"):
                        continue
            if wd.get(key, 0) >= val:
                continue
            wd[key] = val
            waits.append((key, val))
        dsem = None
        if dma:
            pk_ = "cc" if cc else ("sw" if eng == "pool" else "hw")
            pool_ = self.dma_pools[pk_]
            dsem = pool_[self.dma_rr[pk_] % len(pool_)]
            self.dma_rr[pk_] += 1
            prev = self.dma_val[dsem]
            key = ("dma", dsem)
            if prev > 0 and wd.get(key, 0) < prev:
                wd[key] = prev
                waits.append((key, prev))
            inc_ = 1 if cc else 16
            self.dma_val[dsem] = prev + inc_
            tok = ("dma", dsem, prev + inc_)
        else:
            self.count[eng] += 1
            tok = (eng, self.count[eng])
        best = {}
        for k, v in waits:
            best[k] = max(best.get(k, 0), v)
        self.ops[eng].append(_Op(fn, list(best.items()), dma, dsem, 1 if cc else 16))
        self._record(reads, writes, tok)
        return tok

    def barrier_tokens(self):
        toks = [(e, self.count[e]) for e in ENGS if self.count[e] > 0]
        toks += [("dma", i, v) for i, v in enumerate(self.dma_val) if v > 0]
        return toks

    def emit(self, es, final_waits):
        nc = self.nc
        sems = {e: es.enter_context(nc.semaphore("s_" + e)) for e in ENGS}
        dsems = [es.enter_context(nc.semaphore("d%d" % i)) for i in range(self.n_dma)]
        block = es.enter_context(nc.Block())

        def semobj(key):
            return dsems[key[1]] if isinstance(key, tuple) else sems[key]

        for e in ENGS:
            deco = getattr(block, ENG_ATTR[e])

            def body(eng, e=e):
                for op in self.ops[e]:
                    for key, val in op.waits:
                        eng.wait_ge(semobj(key), val)
                    ins = op.fn(eng)
                    if op.dma:
                        ins.then_inc(dsems[op.dsem], op.inc)
                    else:
                        ins.then_inc(sems[e], 1)
                if e == "sp":
                    for tok in final_waits:
                        if tok[0] == "dma":
                            eng.wait_ge(dsems[tok[1]], tok[2])
                        else:
                            eng.wait_ge(sems[tok[0]], tok[1])

            deco(body)


NCORES = 8
BATCH, SEQ, D = 2, 8192, 1024
NT = 2048
TT = 512
NTT = NT // TT
DC = D // 128
FF = 2816
FC = FF // 128
FH = 11
HALO = 16
EPS = 1e-6
NH, HD = 16, 64
GROUP_HEADS = ((0, 6), (6, 11), (11, 16))
DIL = (1, 4, 16)
V_GMIX0, V_GFFN0, V_PSC, V_GMIX1, V_GFFN1, V_GFIN = 0, 8, 16, 24, 32, 40

ARENA_BYTES = 212480


class Arena:
    def __init__(self, nc, es, name="arena", nbytes=ARENA_BYTES):
        self.t = es.enter_context(nc.sbuf_tensor(name, [128, nbytes // 2], BF16))
        self.nbytes = nbytes

    def view(self, off, shape, dtype):
        n = 1
        for s in shape:
            n *= s
        esz = DSZ[dtype]
        assert off % 4 == 0 and off + n * esz <= self.nbytes, (off, shape, dtype)
        ap = self.t[:, off // 2: off // 2 + n * esz // 2]
        if dtype != BF16:
            ap = ap.bitcast(dtype)
        if len(shape) == 2:
            ap = ap.rearrange("p (a b) -> p a b", a=shape[0])
        elif len(shape) == 3:
            ap = ap.rearrange("p (a b c) -> p a b c", a=shape[0], b=shape[1])
        return ap


class Ctx:
    def __init__(self, nc, es):
        self.nc = nc
        self.es = es
        self.S = Sched(nc)
        self.ps = es.enter_context(nc.psum_tensor("ps", [128, 8, 512], F32))
        self.bank = 0

    def psum(self):
        b = self.bank
        self.bank = (self.bank + 1) % 8
        return self.ps[:, b, :]

    def mm(self, out, pairs):
        pairs = list(pairs)
        n = len(pairs)

        def fn(e):
            ins = None
            for i, (l, r) in enumerate(pairs):
                ins = e.matmul(out, lhsT=l, rhs=r, start=(i == 0), stop=(i == n - 1))
            return ins
        reads = [p[0] for p in pairs] + [p[1] for p in pairs]
        return self.S.add("pe", fn, reads=reads, writes=[out])

    def dma(self, q, out, in_):
        return self.S.add(q, lambda e: e.dma_start(out=out, in_=in_), reads=[in_], writes=[out], dma=True)

    def act(self, out, in_, func, reads=None, **kw):
        rd = [in_] + [v for v in kw.values() if not isinstance(v, (int, float))]
        return self.S.add("act", lambda e: e.activation(out=out, in_=in_, func=func, **kw), reads=rd, writes=[out])

    def tt(self, eng, out, in0, in1, op):
        return self.S.add(eng, lambda e: e.tensor_tensor(out=out, in0=in0, in1=in1, op=op),
                          reads=[in0, in1], writes=[out])

    def stt(self, eng, out, in0, scalar, in1, op0, op1):
        rd = [in0, in1] + ([] if isinstance(scalar, (int, float)) else [scalar])
        return self.S.add(eng, lambda e: e.scalar_tensor_tensor(out=out, in0=in0, scalar=scalar, in1=in1,
                                                                op0=op0, op1=op1), reads=rd, writes=[out])

    def ts(self, eng, out, in0, scalar1, op0, scalar2=None, op1=None):
        rd = [in0] + [s for s in (scalar1, scalar2) if s is not None and not isinstance(s, (int, float))]
        if op1 is None:
            return self.S.add(eng, lambda e: e.tensor_scalar(out=out, in0=in0, scalar1=scalar1, scalar2=None,
                                                             op0=op0), reads=rd, writes=[out])
        return self.S.add(eng, lambda e: e.tensor_scalar(out=out, in0=in0, scalar1=scalar1, scalar2=scalar2,
                                                         op0=op0, op1=op1), reads=rd, writes=[out])

    def copy(self, eng, out, in_):
        return self.S.add(eng, lambda e: e.tensor_copy(out=out, in_=in_), reads=[in_], writes=[out])

    def recip(self, out, in_):
        return self.S.add("dve", lambda e: e.reciprocal(out=out, in_=in_), reads=[in_], writes=[out])

    def memset(self, eng, out, val):
        return self.S.add(eng, lambda e: e.memset(out, val), writes=[out])


def rmsnorm_tile(C, xsrc, hdst, n, gcol, ones, sq, rt, rstd):
    C.act(sq[:, :, :n], xsrc, AF.Square)
    ps = C.psum()
    C.mm(ps[:, :n], [(ones, sq[:, c, :n]) for c in range(DC)])
    C.act(rt[:, :n], ps[:, :n], AF.Sqrt, bias=EPS, scale=1.0 / D)
    C.recip(rstd[:, :n], rt[:, :n])
    for c in range(DC):
        C.stt("dve", hdst[:, c, :], xsrc[:, c, :], gcol(c), rstd[:, :n], ALU.mult, ALU.mult)


def ffn_layer(C, A, PB, xT, hT, wg, wu, wd):
    aT = A.view(PB, [FH, NT], BF16)
    wdh = A.view(PB + 45056, [FH, D], BF16)
    wgu = [(A.view(PB + 67584 + i * 8192, [DC, 256], BF16),
            A.view(PB + 67584 + i * 8192 + 4096, [DC, 256], BF16)) for i in range(2)]
    stmp = [A.view(PB + 83968 + i * 2048, [TT], F32) for i in range(2)]
    wg_v = wg.rearrange("(c p) n -> p c n", p=128)
    wu_v = wu.rearrange("(c p) n -> p c n", p=128)
    wd_v = wd.rearrange("(j p) n -> p j n", p=128)
    gi = 0
    si = 0
    for half in range(2):
        j0 = half * FH
        C.dma("pool", wdh[:, 0:6, :], wd_v[:, j0:j0 + 6, :])
        C.dma("pool", wdh[:, 6:FH, :], wd_v[:, j0 + 6:j0 + FH, :])
        jl = 0
        while jl < FH:
            nj = min(2, FH - jl)
            bg, bu = wgu[gi % 2]
            gi += 1
            c0 = (j0 + jl) * 128
            C.dma("pool", bg[:, :, 0:nj * 128], wg_v[:, :, c0:c0 + nj * 128])
            C.dma("pool", bu[:, :, 0:nj * 128], wu_v[:, :, c0:c0 + nj * 128])
            for jj in range(nj):
                for t in range(NTT):
                    tsl = slice(t * TT, (t + 1) * TT)
                    pg = C.psum()
                    C.mm(pg, [(bg[:, k, jj * 128:(jj + 1) * 128], hT[:, k, tsl]) for k in range(DC)])
                    pu = C.psum()
                    C.mm(pu, [(bu[:, k, jj * 128:(jj + 1) * 128], hT[:, k, tsl]) for k in range(DC)])
                    st = stmp[si % 2]
                    si += 1
                    C.act(st, pg, AF.Silu)
                    C.tt("dve", aT[:, jl + jj, tsl], st, pu, ALU.mult)
            jl += nj
        for m in range(DC):
            for t in range(NTT):
                tsl = slice(t * TT, (t + 1) * TT)
                py = C.psum()
                C.mm(py, [(wdh[:, j, m * 128:(m + 1) * 128], aT[:, j, tsl]) for j in range(FH)])
                C.tt("dve", xT[:, m, tsl], xT[:, m, tsl], py, ALU.add)


PB = 118784


def common_views(A):
    v = {}
    v["xT"] = A.view(0, [DC, NT], F32)
    v["hT"] = A.view(65536, [DC, NT], BF16)
    v["sq"] = A.view(98304, [DC, TT], BF16)
    v["rt"] = A.view(106496, [TT], F32)
    v["rstd"] = A.view(108544, [TT], F32)
    v["vecs"] = A.view(110592, [48], F32)
    v["ones"] = A.view(110784, [128], BF16)
    v["invc"] = A.view(111040, [DC, HALO], F32)
    v["xh"] = A.view(111552, [DC, HALO], F32)
    v["hh"] = A.view(112064, [DC, HALO], BF16)
    v["t16"] = A.view(112320, [HALO], F32)
    v["hvt"] = A.view(112384, [16], F32)
    return v


def pool_layer(C, A, V, w_in, w_grp, w_out):
    xT, hT, hh, invc, vecs = V["xT"], V["hT"], V["hh"], V["invc"], V["vecs"]
    pT = A.view(PB, [DC, NT], BF16)
    win = A.view(PB + 32768, [DC, D], BF16)
    wgr = A.view(PB + 49152, [4, 2, 256], BF16)
    NU = NT + HALO
    ub = [A.view(PB + 53248 + i * 8256, [NU], F32) for i in range(2)]
    wa = A.view(PB + 69760, [NU], F32)
    wb = A.view(PB + 69760 + 8256, [NU], F32)
    wout = A.view(PB + 53248, [DC, D], BF16)
    zT = hT
    C.dma("pool", win[:, 0:4, :], w_in.rearrange("(c p) n -> p c n", p=128)[:, 0:4, :])
    C.dma("pool", win[:, 4:8, :], w_in.rearrange("(c p) n -> p c n", p=128)[:, 4:8, :])
    C.dma("pool", wgr, w_grp.rearrange("g (c p) n -> p g c n", p=128))
    for m in range(DC):
        u = ub[m % 2]
        msl = slice(m * 128, (m + 1) * 128)
        ps = C.psum()
        C.mm(ps[:, :HALO], [(win[:, k, msl], hh[:, k, :]) for k in range(DC)])
        C.act(u[:, 0:HALO], ps[:, :HALO], AF.Copy)
        for t in range(NTT):
            tsl = slice(t * TT, (t + 1) * TT)
            ps = C.psum()
            C.mm(ps, [(win[:, k, msl], hT[:, k, tsl]) for k in range(DC)])
            C.act(u[:, HALO + t * TT:HALO + (t + 1) * TT], ps, AF.Copy)
        g = m // 2
        L = g + 1
        w = 2 ** L
        src = u
        bufs = [wa, wb]
        for l in range(1, L + 1):
            sh = 2 ** (l - 1)
            lo = 2 ** l - 1
            dst = bufs[(l - 1) % 2]
            C.tt("dve", dst[:, lo:NU], src[:, lo:NU], src[:, lo - sh:NU - sh], ALU.add)
            src = dst
        C.stt("dve", pT[:, m, :], src[:, HALO:NU], 1.0 / w, u[:, HALO:NU], ALU.mult, ALU.subtract)
        t16 = V["t16"]
        C.tt("dve", t16, src[:, HALO:2 * HALO], invc[:, m, :], ALU.mult)
        C.tt("dve", pT[:, m, 0:HALO], t16, u[:, HALO:2 * HALO], ALU.subtract)
    C.dma("pool", wout[:, 0:4, :], w_out.rearrange("(c p) n -> p c n", p=128)[:, 0:4, :])
    C.dma("pool", wout[:, 4:8, :], w_out.rearrange("(c p) n -> p c n", p=128)[:, 4:8, :])
    for m in range(DC):
        g, mo = m // 2, m % 2
        for t in range(NTT):
            tsl = slice(t * TT, (t + 1) * TT)
            ps = C.psum()
            C.mm(ps, [(wgr[:, g, ki, mo * 128:(mo + 1) * 128], pT[:, 2 * g + ki, tsl]) for ki in range(2)])
            C.act(zT[:, m, tsl], ps, AF.Copy, scale=vecs[:, V_PSC + m:V_PSC + m + 1])
    for m in range(DC):
        msl = slice(m * 128, (m + 1) * 128)
        for t in range(NTT):
            tsl = slice(t * TT, (t + 1) * TT)
            ps = C.psum()
            C.mm(ps, [(wout[:, k, msl], zT[:, k, tsl]) for k in range(DC)])
            C.tt("dve", xT[:, m, tsl], xT[:, m, tsl], ps, ALU.add)


def build_layer0():
    nc = bass.Bass("TRN2", target_bir_lowering=False)
    dt_in = lambda n, s: nc.dram_tensor(n, s, F32, kind="ExternalInput").ap()
    xT_d = dt_in("xT", [D, NT])
    xh_d = dt_in("xh", [D, HALO])
    vecs_d = dt_in("vecs", [128, 48])
    invc_d = dt_in("invc", [128, DC * HALO])
    w_in = dt_in("w_in", [D, D])
    w_grp = dt_in("w_grp", [4, 256, 256])
    w_out = dt_in("w_out", [D, D])
    wg = dt_in("wg0", [D, FF])
    wu = dt_in("wu0", [D, FF])
    wd = dt_in("wd0", [FF, D])
    x1_d = nc.dram_tensor("x1T", [D, NT], F32, kind="ExternalOutput").ap()
    with ExitStack() as es:
        A = Arena(nc, es)
        C = Ctx(nc, es)
        V = common_views(A)
        layer0_body(C, A, V, xT_d, xh_d, vecs_d, invc_d, w_in, w_grp, w_out, wg, wu, wd)
        toks = []
        x1v = x1_d.rearrange("(c p) t -> p c t", p=128)
        for t in range(NTT):
            tsl = slice(t * TT, (t + 1) * TT)
            toks.append(C.dma("sp", x1v[:, :, tsl], V["xT"][:, :, tsl]))
        C.S.emit(es, toks)
    return nc


def layer0_body(C, A, V, xT_d, xh_d, vecs_d, invc_d, w_in, w_grp, w_out, wg, wu, wd):
    xT, hT = V["xT"], V["hT"]
    C.dma("sp", V["vecs"], vecs_d)
    C.dma("sp", V["invc"], invc_d.rearrange("p (c h) -> p c h", c=DC))
    C.dma("sp", V["xh"], xh_d.rearrange("(c p) h -> p c h", p=128))
    xv = xT_d.rearrange("(c p) t -> p c t", p=128)
    for t in range(NTT):
        tsl = slice(t * TT, (t + 1) * TT)
        C.dma("sp", xT[:, :, tsl], xv[:, :, tsl])
    C.memset("dve", V["ones"], 1.0)
    gcol = lambda base: (lambda c: V["vecs"][:, base + c:base + c + 1])
    rmsnorm_tile(C, V["xh"], V["hh"], HALO, gcol(V_GMIX0), V["ones"], V["sq"], V["rt"], V["rstd"])
    for t in range(NTT):
        tsl = slice(t * TT, (t + 1) * TT)
        rmsnorm_tile(C, xT[:, :, tsl], hT[:, :, tsl], TT, gcol(V_GMIX0), V["ones"], V["sq"], V["rt"], V["rstd"])
    pool_layer(C, A, V, w_in, w_grp, w_out)
    for t in range(NTT):
        tsl = slice(t * TT, (t + 1) * TT)
        rmsnorm_tile(C, xT[:, :, tsl], hT[:, :, tsl], TT, gcol(V_GFFN0), V["ones"], V["sq"], V["rt"], V["rstd"])
    ffn_layer(C, A, PB, xT, hT, wg, wu, wd)


def _vec_layout(v):
    return np.ascontiguousarray(np.asarray(v, np.float32).reshape(DC, 128).T)


def _make_vecs(norm_mix, norm_ffn, norm_final, pool_scale):
    cols = [norm_mix[0], norm_ffn[0], pool_scale[0], norm_mix[1], norm_ffn[1], norm_final]
    return np.ascontiguousarray(np.concatenate([_vec_layout(c) for c in cols], axis=1))


def _invc_table(first):
    t = np.zeros((128, DC, HALO), np.float32)
    for c in range(DC):
        w = 2 ** (c // 2 + 1)
        for i in range(HALO):
            t[:, c, i] = 1.0 / (min(i + 1, w) if first else w)
    return np.ascontiguousarray(t.reshape(128, DC * HALO))


def _core_tokens(x):
    xs, hs = [], []
    cps = SEQ // NT
    for core in range(NCORES):
        b, c = core // cps, core % cps
        xs.append(np.ascontiguousarray(x[b, c * NT:(c + 1) * NT, :].T))
        if c == 0:
            hs.append(np.zeros((D, HALO), np.float32))
        else:
            hs.append(np.ascontiguousarray(x[b, c * NT - HALO:c * NT, :].T))
    return xs, hs


def dsl(start, d, n=128):
    return slice(start, start + d * (n - 1) + 1, d)


def head_info(h):
    for g, (h0, h1) in enumerate(GROUP_HEADS):
        if h0 <= h < h1:
            return g, h0, h1 - h0
    raise ValueError(h)


def layer1_views(A):
    L = {}
    L["kT"] = A.view(0, [DC, NT], BF16)
    L["xs"] = [A.view(i * 16384, [DC, TT], F32) for i in range(2)]
    L["khb"] = A.view(32768, [3, NT], BF16)
    L["khs"] = A.view(45056, [5, TT], BF16)
    L["vh"] = [A.view(50176, [1, 6, HD], BF16), A.view(50176 + 768, [4, 5, HD], BF16),
               A.view(50176 + 768 + 2560, [16, 5, HD], BF16)]
    L["qT"] = A.view(PB, [DC, NT], BF16)
    L["hTh"] = L["qT"]
    vo = PB + 32768
    L["v"] = [A.view(vo, [16, 6, HD], BF16), A.view(vo + 12288, [16, 5, HD], BF16),
              A.view(vo + 12288 + 10240, [16, 5, HD], BF16)]
    L["cos"] = A.view(vo, [2 * NT], F32)
    L["sin"] = A.view(vo + 16384, [2 * NT], F32)
    wo_ = PB + 65536
    L["wqk"] = [A.view(wo_ + i * 2048, [DC, 128], BF16) for i in range(2)]
    L["wv"] = A.view(wo_ + 4096, [DC, 384], BF16)
    L["R"] = A.view(wo_ + 10240, [128], BF16)
    L["Dg"] = A.view(wo_, [3, NT], F32)
    L["wo"] = A.view(PB, [DC, D], BF16)
    b = 98304
    L["qb"] = [A.view(b + i * 1024, [TT], BF16) for i in range(2)]
    L["t1"] = [A.view(b + 2048 + i * 2048, [TT], F32) for i in range(2)]
    L["t2"] = [A.view(b + 6144 + i * 2048, [TT], F32) for i in range(2)]
    L["pT"] = [A.view(b + i * 2048, [1024], BF16) for i in range(2)]
    L["rden"] = [A.view(b + 4096 + i * 2048, [TT], F32) for i in range(2)]
    L["masks"] = A.view(112448, [3, 1024], BF16)
    return L


def build_layer1_body(C, A, V, x1_own_d, x1_halo_d, tab_d, mask_d, R_d, hv_d,
                      w_qkv, w_o, wg, wu, wd, out_d, own_in_sbuf=False, gather=None):
    S = C.S
    L = layer1_views(A)
    xT, hT, vecs, ones = V["xT"], V["hT"], V["vecs"], V["ones"]
    import os as _os
    _stop = _os.environ.get("KSTOP", "")

    def stop_here(tag, src_f32):
        if _stop != tag:
            return None
        ov_ = out_d.rearrange("(c p) t -> p c t", p=128)
        return [C.dma("sp", ov_[:, :, 0:TT], src_f32)]
    gcol = lambda base: (lambda c: vecs[:, base + c:base + c + 1])
    nrm = lambda xsrc, hdst, n, base: rmsnorm_tile(C, xsrc, hdst, n, gcol(base), ones, V["sq"], V["rt"], V["rstd"])
    x1o = x1_own_d.rearrange("(c p) t -> p c t", p=128)
    x1h = None if gather is not None else x1_halo_d.rearrange("(c p) t -> p c t", p=128)
    wq_v = w_qkv.rearrange("(c p) n -> p c n", p=128)

    masks, Rm, hv = L["masks"], L["R"], V["hvt"]
    C.dma("sp", hv[:, 0:1], hv_d)
    C.dma("pool", Rm, R_d)
    C.dma("pool", masks[:, 0, :], mask_d)
    for i in range(4):
        C.dma("sp", L["cos"][:, i * 1024:(i + 1) * 1024], tab_d[0][:, i * 1024:(i + 1) * 1024])
        C.dma("sp", L["sin"][:, i * 1024:(i + 1) * 1024], tab_d[1][:, i * 1024:(i + 1) * 1024])
    C.copy("pool", masks[:, 1, :], masks[:, 0, :])
    C.ts("dve", masks[:, 1, 0:128], masks[:, 0, 0:128], hv[:, 0:1], ALU.mult)
    C.copy("pool", masks[:, 2, :], masks[:, 0, :])
    m2v = masks[:, 2, :].rearrange("p (e s) -> p e s", e=4)
    m0v = masks[:, 0, :].rearrange("p (e s) -> p e s", e=4)
    C.ts("dve", m2v[:, :, 0:128], m0v[:, :, 0:128], hv[:, 0:1], ALU.mult)

    if not own_in_sbuf:
        for t in range(NTT):
            tsl = slice(t * TT, (t + 1) * TT)
            C.dma("sp", xT[:, :, tsl], x1o[:, :, tsl])
    for t in range(NTT):
        tsl = slice(t * TT, (t + 1) * TT)
        nrm(xT[:, :, tsl], hT[:, :, tsl], TT, V_GMIX1)

    hTh = L["hTh"]
    cnt = {"w": 0, "r": 0}

    def proj_rope(wcol0, src, s_tsl, tab_off, dst):
        wbuf = L["wqk"][(cnt["w"] - 1) % 2]
        i = cnt["r"] % 2
        cnt["r"] += 1
        pk = C.psum()
        C.mm(pk, [(wbuf[:, k, :], src[:, k, s_tsl]) for k in range(DC)])
        qb, t1, t2 = L["qb"][i], L["t1"][i], L["t2"][i]
        C.act(qb, pk, AF.Copy)
        pr = C.psum()
        C.mm(pr, [(Rm, qb)])
        C.tt("dve", t1, pk, L["cos"][:, tab_off:tab_off + TT], ALU.mult)
        C.tt("dve", t2, pr, L["sin"][:, tab_off:tab_off + TT], ALU.mult)
        C.tt("pool", dst, t1, t2, ALU.add)

    def load_wqk(col0):
        wbuf = L["wqk"][cnt["w"] % 2]
        cnt["w"] += 1
        C.dma("pool", wbuf, wq_v[:, :, col0:col0 + 128])

    def v_blocks(src, dstv, g, blocks):
        h0, nh = GROUP_HEADS[g][0], GROUP_HEADS[g][1] - GROUP_HEADS[g][0]
        ncol = nh * HD
        for bi, sl in blocks:
            ps = C.psum()
            C.mm(ps[:, :ncol], [(src[:, k, sl], L["wv"][:, k, 0:ncol]) for k in range(DC)])
            C.act(dstv[:, bi, :, :], ps[:, :ncol].rearrange("p (h d) -> p h d", h=nh), AF.Copy)

    def halo_blocks(g):
        d = DIL[g]
        return [(r, dsl(NT - 128 * d + r, d)) for r in range(d)]

    def own_blocks(g):
        d = DIL[g]
        nb = 16 // d
        return [(r * nb + b, dsl(r + d * 128 * b, d)) for r in range(d) for b in range(nb)]

    def do_k_own():
        for c in range(DC):
            load_wqk(D + c * 128)
            for t in range(NTT):
                tsl = slice(t * TT, (t + 1) * TT)
                proj_rope(D + c * 128, hT, tsl, NT + t * TT, L["kT"][:, c, tsl])

    def do_halo():
        if gather is None:
            for t in range(NTT):
                tsl = slice(t * TT, (t + 1) * TT)
                xs = L["xs"][t % 2]
                C.dma("sp", xs, x1h[:, :, tsl])
                nrm(xs, hTh[:, :, tsl], TT, V_GMIX1)
        else:
            pass
        for c in range(DC):
            load_wqk(D + c * 128)
            tiles = range(NTT) if c >= 5 else [NTT - 1]
            for t in tiles:
                tsl = slice(t * TT, (t + 1) * TT)
                dst = L["khb"][:, c - 5, tsl] if c >= 5 else L["khs"][:, c, :]
                proj_rope(D + c * 128, hTh, tsl, t * TT, dst)
        for g in range(3):
            h0, h1 = GROUP_HEADS[g]
            C.dma("pool", L["wv"][:, :, 0:(h1 - h0) * HD], wq_v[:, :, 2 * D + h0 * HD:2 * D + h1 * HD])
            v_blocks(hTh, L["vh"][g], g, halo_blocks(g))

    if gather is not None:
        hx_in, hx_all = gather
        x1s_v = x1_own_d.rearrange("(c p) t -> p c t", p=128)
        hin_v = hx_in.ap().rearrange("(c p) t -> p c t", p=128)
        for t in range(NTT):
            tsl = slice(t * TT, (t + 1) * TT)
            C.dma("sp", hin_v[:, :, tsl], hT[:, :, tsl])
            C.dma("sp", x1s_v[:, :, tsl], xT[:, :, tsl])
        S.add("pool", lambda e: e.collective_compute("AllGather", ALU.bypass, replica_groups=[list(range(NCORES))],
                                                     ins=[hx_in.ap().opt()], outs=[hx_all.ap().opt()]),
              reads=[hx_in.ap()], writes=[hx_all.ap()], dma=True, cc=True)
        hall_v = hx_all.ap().rearrange("(r c p) t -> r p c t", r=NCORES, p=128)
        for t in range(NTT):
            tsl = slice(t * TT, (t + 1) * TT)

            def rd(e, tsl=tsl):
                prev = (e.partition_id() + (NCORES - 1)) % NCORES
                return e.dma_start(out=hTh[:, :, tsl],
                                   in_=hall_v[bass.ds(prev, 1), :, :, tsl].rearrange("r p c t -> (r p) c t"))
            S.add("pool", rd, reads=[hx_all.ap()], writes=[hTh[:, :, tsl]], dma=True)
        do_k_own()
        do_halo()
    else:
        do_halo()
        do_k_own()

    for c in range(DC):
        load_wqk(c * 128)
        for t in range(NTT):
            tsl = slice(t * TT, (t + 1) * TT)
            proj_rope(c * 128, hT, tsl, NT + t * TT, L["qT"][:, c, tsl])
    for g in range(3):
        h0, h1 = GROUP_HEADS[g]
        C.dma("pool", L["wv"][:, :, 0:(h1 - h0) * HD], wq_v[:, :, 2 * D + h0 * HD:2 * D + h1 * HD])
        v_blocks(hT, L["v"][g], g, own_blocks(g))

    _r = stop_here("qkv", L["xs"][1])
    if _r:
        return _r
    oT = hT
    Dg = L["Dg"]
    C.memset("pool", Dg[0:64, :, :], 0.0)
    psS = [C.ps[:, 0:2, :].rearrange("p a b -> p (a b)"), C.ps[:, 2:4, :].rearrange("p a b -> p (a b)")]
    psO = [C.ps[:, 4:6, :].rearrange("p a b -> p (a b)"), C.ps[:, 6:8, :].rearrange("p a b -> p (a b)")]
    bi = 0
    for h in range(NH):
        g, h0, nh = head_info(h)
        d = DIL[g]
        nb = 16 // d
        c, ro, hl = h // 2, (h % 2) * HD, h - h0
        prow = slice(ro, ro + HD)
        entries = [(r, b) for r in range(d) for b in range(nb)]
        for q in range(4):
            ents = entries[4 * q:4 * q + 4]
            pS, pO, pT, rden = psS[bi % 2], psO[bi % 2], L["pT"][bi % 2], L["rden"][bi % 2]
            bi += 1
            n_halo = sum(1 for (r, b) in ents if b == 0)
            mk = masks[:, 0, :] if n_halo == 0 else (masks[:, 1, :] if n_halo == 1 else masks[:, 2, :])
            assert n_halo in (0, 4) or (n_halo == 1 and ents[0][1] == 0)
            kv = []
            for e, (r, b) in enumerate(ents):
                start = r + d * 128 * b
                qsl = dsl(start, d)
                qa = L["qT"][prow, c, qsl]
                kcur = L["kT"][prow, c, qsl]
                vcur = L["v"][g][:, r * nb + b, hl, :]
                if b >= 1:
                    kprev = L["kT"][prow, c, dsl(start - d * 128, d)]
                    vprev = L["v"][g][:, r * nb + b - 1, hl, :]
                else:
                    hsl = dsl(NT - 128 * d + r, d)
                    if c >= 5:
                        kprev = L["khb"][prow, c - 5, hsl]
                    else:
                        kprev = L["khs"][prow, c, dsl(hsl.start - (NT - TT), d)]
                    vprev = L["vh"][g][:, r, hl, :]
                C.mm(pS[:, e * 256:e * 256 + 128], [(kprev, qa)])
                C.mm(pS[:, e * 256 + 128:e * 256 + 256], [(kcur, qa)])
                kv.append((vprev, vcur))
            C.act(pT, pS, AF.Exp, scale=HD ** -0.5)
            C.tt("pool", pT, pT, mk, ALU.mult)
            for e, (vprev, vcur) in enumerate(kv):
                pp, pc = pT[:, e * 256:e * 256 + 128], pT[:, e * 256 + 128:e * 256 + 256]
                C.mm(pO[0:64, e * 256:e * 256 + 128], [(vprev, pp), (vcur, pc)])
                C.mm(pO[0:64, e * 256 + 128:e * 256 + 256], [(ones[:, 0:64], pp), (ones[:, 0:64], pc)])
            pOv = pO[0:64, :].rearrange("p (e s) -> p e s", e=4)
            num, den = pOv[:, :, 0:128], pOv[:, :, 128:256]
            rdv = rden[0:64, :].rearrange("p (e s) -> p e s", e=4)
            if d == 1:
                tv = lambda ap3: ap3[:, q * TT:(q + 1) * TT].rearrange("p (e s) -> p e s", e=4)
            elif d == 4:
                r0 = ents[0][0]
                tv = lambda ap3: ap3[:, dsl(r0, 4, 512)].rearrange("p (e s) -> p e s", e=4)
            else:
                r0 = ents[0][0]
                tv = lambda ap3: ap3[:, :].rearrange("p (s r) -> p r s", r=16)[:, r0:r0 + 4, :]
            S.add("dve", (lambda e_, o=rdv, i=den: e_.reciprocal(out=o, in_=i)), reads=[den], writes=[rdv])
            C.tt("dve", tv(oT[prow, c, :]), num, rdv, ALU.mult)
            dgv = tv(Dg[0:64, g, :])
            C.tt("dve", dgv, den, dgv, ALU.add)

    _r = stop_here("attn", L["cos"][:, 0:4096].rearrange("p (c t) -> p c t", c=DC))
    if _r:
        return _r
    C.dma("pool", L["wo"][:, 0:4, :], w_o.rearrange("(c p) n -> p c n", p=128)[:, 0:4, :])
    C.dma("pool", L["wo"][:, 4:8, :], w_o.rearrange("(c p) n -> p c n", p=128)[:, 4:8, :])
    for t in range(NTT):
        tsl = slice(t * TT, (t + 1) * TT)
        C.dma("sp", xT[:, :, tsl], x1o[:, :, tsl])
    tot = A.view(98304, [NT], F32)
    ng = [GROUP_HEADS[g][1] - GROUP_HEADS[g][0] for g in range(3)]
    C.ts("dve", tot[0:64, :], Dg[0:64, 0, :], 1.0 / ng[0], ALU.mult)
    C.stt("dve", tot[0:64, :], Dg[0:64, 1, :], 1.0 / ng[1], tot[0:64, :], ALU.mult, ALU.add)
    C.stt("dve", tot[0:64, :], Dg[0:64, 2, :], 1.0 / ng[2], tot[0:64, :], ALU.mult, ALU.add)
    C.recip(tot[0:64, :], tot[0:64, :])
    for g in range(3):
        C.stt("dve", Dg[0:64, g, :], Dg[0:64, g, :], 3.0 / ng[g], tot[0:64, :], ALU.mult, ALU.mult)
    C.act(Dg[64:128, :, :], Dg[0:64, :, :], AF.Copy)
    for c in range(DC):
        ga, gb = head_info(2 * c)[0], head_info(2 * c + 1)[0]
        if ga == gb:
            C.tt("dve", oT[:, c, :], oT[:, c, :], Dg[:, ga, :], ALU.mult)
        else:
            C.tt("dve", oT[0:64, c, :], oT[0:64, c, :], Dg[0:64, ga, :], ALU.mult)
            C.tt("dve", oT[64:128, c, :], oT[64:128, c, :], Dg[64:128, gb, :], ALU.mult)
    for m in range(DC):
        msl = slice(m * 128, (m + 1) * 128)
        for t in range(NTT):
            tsl = slice(t * TT, (t + 1) * TT)
            ps = C.psum()
            C.mm(ps, [(L["wo"][:, k, msl], oT[:, k, tsl]) for k in range(DC)])
            C.tt("dve", xT[:, m, tsl], xT[:, m, tsl], ps, ALU.add)

    _r = stop_here("oproj", xT[:, :, 0:TT])
    if _r:
        return _r
    for t in range(NTT):
        tsl = slice(t * TT, (t + 1) * TT)
        nrm(xT[:, :, tsl], hT[:, :, tsl], TT, V_GFFN1)
    ffn_layer(C, A, PB, xT, hT, wg, wu, wd)

    ob = [A.view(PB + i * 16384, [DC, TT], F32) for i in range(2)]
    ov = out_d.rearrange("(c p) t -> p c t", p=128)
    toks = []
    for t in range(NTT):
        tsl = slice(t * TT, (t + 1) * TT)
        rmsnorm_tile(C, xT[:, :, tsl], ob[t % 2], TT, gcol(V_GFIN), ones, V["sq"], V["rt"], V["rstd"])
        toks.append(C.dma("sp", ov[:, :, tsl], ob[t % 2]))
    return toks


def build_layer1():
    nc = bass.Bass("TRN2", target_bir_lowering=False)
    dt_in = lambda n, s: nc.dram_tensor(n, s, F32, kind="ExternalInput").ap()
    x1o = dt_in("x1T", [D, NT])
    x1h = dt_in("x1h", [D, NT])
    vecs_d = dt_in("vecs", [128, 48])
    hv_d = dt_in("hv", [128, 1])
    tab_d = dt_in("tab", [2, 128, 2 * NT])
    mask_d = dt_in("mask", [128, 1024])
    R_d = dt_in("rmat", [128, 128])
    w_qkv = dt_in("w_qkv", [D, 3 * D])
    w_o = dt_in("w_o", [D, D])
    wg = dt_in("wg1", [D, FF])
    wu = dt_in("wu1", [D, FF])
    wd = dt_in("wd1", [FF, D])
    out_d = nc.dram_tensor("outT", [D, NT], F32, kind="ExternalOutput").ap()
    with ExitStack() as es:
        A = Arena(nc, es)
        C = Ctx(nc, es)
        V = common_views(A)
        C.dma("sp", V["vecs"], vecs_d)
        C.memset("dve", V["ones"], 1.0)
        toks = build_layer1_body(C, A, V, x1o, x1h, tab_d, mask_d, R_d, hv_d, w_qkv, w_o, wg, wu, wd, out_d)
        C.S.emit(es, toks)
    return nc


def build_fused():
    nc = bass.Bass("TRN2", target_bir_lowering=False)
    dt_in = lambda n, s: nc.dram_tensor(n, s, F32, kind="ExternalInput").ap()
    xT_d = dt_in("xT", [D, NT])
    xh_d = dt_in("xh", [D, HALO])
    vecs_d = dt_in("vecs", [128, 48])
    invc_d = dt_in("invc", [128, DC * HALO])
    hv_d = dt_in("hv", [128, 1])
    tab_d = dt_in("tab", [2, 128, 2 * NT])
    mask_d = dt_in("mask", [128, 1024])
    R_d = dt_in("rmat", [128, 128])
    w_in = dt_in("w_in", [D, D])
    w_grp = dt_in("w_grp", [4, 256, 256])
    w_out = dt_in("w_out", [D, D])
    wg0 = dt_in("wg0", [D, FF])
    wu0 = dt_in("wu0", [D, FF])
    wd0 = dt_in("wd0", [FF, D])
    w_qkv = dt_in("w_qkv", [D, 3 * D])
    w_o = dt_in("w_o", [D, D])
    wg1 = dt_in("wg1", [D, FF])
    wu1 = dt_in("wu1", [D, FF])
    wd1 = dt_in("wd1", [FF, D])
    out_d = nc.dram_tensor("outT", [D, NT], F32, kind="ExternalOutput").ap()
    x1s = nc.dram_tensor("x1s", [D, NT], F32)
    hx_in = nc.dram_tensor("hx_in", [D, NT], BF16)
    hx_all = nc.dram_tensor("hx_all", [NCORES * D, NT], BF16)
    with ExitStack() as es:
        A = Arena(nc, es)
        C = Ctx(nc, es)
        V = common_views(A)
        layer0_body(C, A, V, xT_d, xh_d, vecs_d, invc_d, w_in, w_grp, w_out, wg0, wu0, wd0)
        toks = build_layer1_body(C, A, V, x1s.ap(), None, tab_d, mask_d, R_d, hv_d, w_qkv, w_o, wg1, wu1, wd1,
                                 out_d, own_in_sbuf=True, gather=(hx_in, hx_all))
        C.S.emit(es, toks)
    return nc


def _rope_tables(pos0):
    inv_freq = (1.0 / (10000.0 ** (np.arange(0, HD, 2, dtype=np.float32) / np.float32(HD)))).astype(np.float32)
    pos = np.arange(pos0 - NT, pos0 + NT, dtype=np.float32)
    ang = (pos[None, :] * inv_freq[:, None]).astype(np.float32)
    cos = np.cos(ang).astype(np.float32)
    sin = np.sin(ang).astype(np.float32)
    rows = np.arange(128)
    ctab = cos[rows % 32]
    sgn = np.where((rows % 64) < 32, -1.0, 1.0).astype(np.float32)
    stab = sin[rows % 32] * sgn[:, None]
    return np.ascontiguousarray(np.stack([ctab, stab]).astype(np.float32))


def _mask_table():
    k = np.arange(128)[:, None]
    q = np.arange(128)[None, :]
    prev = (q <= k).astype(np.float32)
    cur = (k <= q).astype(np.float32)
    one = np.concatenate([prev, cur], axis=1)
    return np.ascontiguousarray(np.tile(one, (1, 4)))


def _rot_matrix():
    m = np.arange(128)
    partner = (m // 64) * 64 + ((m % 64) + 32) % 64
    R = np.zeros((128, 128), np.float32)
    R[partner, m] = 1.0
    return R


_NC_CACHE = {}


def _get_nc(name, builder):
    if name not in _NC_CACHE:
        _NC_CACHE[name] = builder()
    return _NC_CACHE[name]


def kernel_unfused(x, norm_mix, norm_ffn, norm_final, pool_w_in, pool_w_group, pool_scale, pool_w_out,
                   attn_w_qkv, attn_w_out, ffn_w_gate, ffn_w_up, ffn_w_down):
    f = lambda a: np.ascontiguousarray(np.asarray(a, dtype=np.float32))
    x = f(x)
    cps = SEQ // NT
    vecs = _make_vecs(f(norm_mix), f(norm_ffn), f(norm_final), f(pool_scale))
    xs, hs = _core_tokens(x)
    nc0 = _get_nc("l0", build_layer0)
    in0 = []
    for core in range(NCORES):
        in0.append({"xT": xs[core], "xh": hs[core], "vecs": vecs, "invc": _invc_table(core % cps == 0),
                    "w_in": f(pool_w_in[0]), "w_grp": f(pool_w_group[0]), "w_out": f(pool_w_out[0]),
                    "wg0": f(ffn_w_gate[0]), "wu0": f(ffn_w_up[0]), "wd0": f(ffn_w_down[0])})
    r0 = run_bass_kernel_spmd(nc0, in0, core_ids=list(range(NCORES)))
    x1 = [r["x1T"] for r in r0.results]
    nc1 = _get_nc("l1", build_layer1)
    mask = _mask_table()
    rmat = _rot_matrix()
    in1 = []
    for core in range(NCORES):
        first = core % cps == 0
        halo = np.zeros((D, NT), np.float32) if first else x1[core - 1]
        in1.append({"x1T": x1[core], "x1h": halo, "vecs": vecs,
                    "hv": np.full((128, 1), 0.0 if first else 1.0, np.float32),
                    "tab": _rope_tables((core % cps) * NT), "mask": mask, "rmat": rmat,
                    "w_qkv": f(attn_w_qkv[0]), "w_o": f(attn_w_out[0]),
                    "wg1": f(ffn_w_gate[1]), "wu1": f(ffn_w_up[1]), "wd1": f(ffn_w_down[1])})
    r1 = run_bass_kernel_spmd(nc1, in1, core_ids=list(range(NCORES)))
    out = np.stack([r["outT"].T for r in r1.results]).reshape(BATCH, SEQ, D)
    return np.ascontiguousarray(out.astype(np.float32))


def kernel(x, norm_mix, norm_ffn, norm_final, pool_w_in, pool_w_group, pool_scale, pool_w_out,
           attn_w_qkv, attn_w_out, ffn_w_gate, ffn_w_up, ffn_w_down):
    f = lambda a: np.ascontiguousarray(np.asarray(a, dtype=np.float32))
    x = f(x)
    cps = SEQ // NT
    vecs = _make_vecs(f(norm_mix), f(norm_ffn), f(norm_final), f(pool_scale))
    xs, hs = _core_tokens(x)
    nc = _get_nc("fused", build_fused)
    mask = _mask_table()
    rmat = _rot_matrix()
    shared = {"vecs": vecs, "mask": mask, "rmat": rmat,
              "w_in": f(pool_w_in[0]), "w_grp": f(pool_w_group[0]), "w_out": f(pool_w_out[0]),
              "wg0": f(ffn_w_gate[0]), "wu0": f(ffn_w_up[0]), "wd0": f(ffn_w_down[0]),
              "w_qkv": f(attn_w_qkv[0]), "w_o": f(attn_w_out[0]),
              "wg1": f(ffn_w_gate[1]), "wu1": f(ffn_w_up[1]), "wd1": f(ffn_w_down[1])}
    in_maps = []
    for core in range(NCORES):
        first = core % cps == 0
        m = dict(shared)
        m.update({"xT": xs[core], "xh": hs[core], "invc": _invc_table(first),
                  "hv": np.full((128, 1), 0.0 if first else 1.0, np.float32),
                  "tab": _rope_tables((core % cps) * NT)})
        in_maps.append(m)
    r = run_bass_kernel_spmd(nc, in_maps, core_ids=list(range(NCORES)))
    out = np.stack([rr["outT"].T for rr in r.results]).reshape(BATCH, SEQ, D)
    return np.ascontiguousarray(out.astype(np.float32))
```

```python
import numpy as np
from contextlib import ExitStack
import concourse.bass as bass
import concourse.mybir as mybir
from concourse.bass_utils import run_bass_kernel_spmd

F32 = mybir.dt.float32
BF16 = mybir.dt.bfloat16
ALU = mybir.AluOpType
AF = mybir.ActivationFunctionType
DSZ = {F32: 4, BF16: 2}

ENGS = ("pe", "act", "dve", "pool", "sp")
ENG_ATTR = {"pe": "tensor", "act": "scalar", "dve": "vector", "pool": "gpsimd", "sp": "sync"}


class _Op:
    __slots__ = ("fn", "waits", "dma", "dsem", "inc")

    def __init__(self, fn, waits, dma, dsem, inc=16):
        self.fn, self.waits, self.dma, self.dsem, self.inc = fn, waits, dma, dsem, inc


class Sched:
    def __init__(self, nc, n_dma_sems=48):
        self.nc = nc
        self.ops = {e: [] for e in ENGS}
        self.count = {e: 0 for e in ENGS}
        self.waited = {e: {} for e in ENGS}
        self.acc = {}
        self.n_dma = n_dma_sems
        self.dma_val = [0] * n_dma_sems
        half = n_dma_sems // 2
        self.dma_pools = {"sw": list(range(0, half - 2)), "hw": list(range(half, n_dma_sems)),
                          "cc": list(range(half - 2, half))}
        self.dma_rr = {"sw": 0, "hw": 0, "cc": 0}

    @staticmethod
    def rng(ap):
        t = ap.tensor
        name = t.name
        esz = DSZ[ap.dtype]
        pat = ap.ap
        off = ap.offset
        shape = list(t.shape)
        if type(t).__name__.startswith("DRam") or len(shape) < 2 or "DRam" in type(t).__name__:
            lo = off
            hi = off + sum((c - 1) * abs(s) for s, c in pat) + 1
            return (name, 0, 1, lo * esz, hi * esz)
        F = 1
        for s in shape[1:]:
            F *= s
        p0 = off // F
        c0 = off % F
        pcount = pat[0][1]
        hi = c0 + sum((c - 1) * abs(s) for s, c in pat[1:]) + 1
        if name == "ps":
            lo_b = (c0 * esz) // 2048
            hi_b = (hi * esz + 2047) // 2048
            return (name, 0, 128, lo_b * 2048, hi_b * 2048)
        return (name, p0, p0 + pcount, c0 * esz, hi * esz)

    def _deps(self, reads, writes):
        toks = []
        for ap in reads:
            n, p0, p1, lo, hi = self.rng(ap)
            a = self.acc.setdefault(n, {"w": [], "r": []})
            for (q0, q1, l, h, tok) in a["w"]:
                if q0 < p1 and p0 < q1 and l < hi and lo < h:
                    toks.append(tok)
        for ap in writes:
            n, p0, p1, lo, hi = self.rng(ap)
            a = self.acc.setdefault(n, {"w": [], "r": []})
            for (q0, q1, l, h, tok) in a["w"]:
                if q0 < p1 and p0 < q1 and l < hi and lo < h:
                    toks.append(("W",) + tok)
            for (q0, q1, l, h, tok) in a["r"]:
                if q0 < p1 and p0 < q1 and l < hi and lo < h:
                    toks.append(("W",) + tok)
        return toks

    def _record(self, reads, writes, tok):
        for ap in writes:
            n, p0, p1, lo, hi = self.rng(ap)
            a = self.acc.setdefault(n, {"w": [], "r": []})
            a["w"] = [e for e in a["w"] if not (p0 <= e[0] and e[1] <= p1 and lo <= e[2] and e[3] <= hi)]
            a["r"] = [e for e in a["r"] if not (p0 <= e[0] and e[1] <= p1 and lo <= e[2] and e[3] <= hi)]
            a["w"].append((p0, p1, lo, hi, tok))
        for ap in reads:
            n, p0, p1, lo, hi = self.rng(ap)
            a = self.acc.setdefault(n, {"w": [], "r": []})
            a["r"] = [e for e in a["r"] if not (e[4][0] == tok[0] and tok[0] != "dma" and p0 <= e[0]
                                                and e[1] <= p1 and lo <= e[2] and e[3] <= hi)]
            a["r"].append((p0, p1, lo, hi, tok))

    def add(self, eng, fn, reads=(), writes=(), dma=False, extra_waits=(), cc=False):
        ps_reads = [ap for ap in reads if ap.tensor.name == "ps"]
        if ps_reads:
            reads = [ap for ap in reads if ap.tensor.name != "ps"]
            writes = list(writes) + ps_reads
        toks = self._deps(reads, writes) + list(extra_waits)
        waits = []
        wd = self.waited[eng]
        for tok in toks:
            war = False
            if tok[0] == "W":
                war = True
                tok = tok[1:]
            if tok[0] == "dma":
                key, val = ("dma", tok[1]), tok[2]
            else:
                key, val = tok[0], tok[1]
                if key == eng:
                    if eng in ("pe", "sp"):
                        continue
            if wd.get(key, 0) >= val:
                continue
            wd[key] = val
            waits.append((key, val))
        dsem = None
        if dma:
            pk_ = "cc" if cc else ("sw" if eng == "pool" else "hw")
            pool_ = self.dma_pools[pk_]
            dsem = pool_[self.dma_rr[pk_] % len(pool_)]
            self.dma_rr[pk_] += 1
            prev = self.dma_val[dsem]
            key = ("dma", dsem)
            if prev > 0 and wd.get(key, 0) < prev:
                wd[key] = prev
                waits.append((key, prev))
            inc_ = 1 if cc else 16
            self.dma_val[dsem] = prev + inc_
            tok = ("dma", dsem, prev + inc_)
        else:
            self.count[eng] += 1
            tok = (eng, self.count[eng])
        best = {}
        for k, v in waits:
            best[k] = max(best.get(k, 0), v)
        self.ops[eng].append(_Op(fn, list(best.items()), dma, dsem, 1 if cc else 16))
        self._record(reads, writes, tok)
        return tok

    def barrier_tokens(self):
        toks = [(e, self.count[e]) for e in ENGS if self.count[e] > 0]
        toks += [("dma", i, v) for i, v in enumerate(self.dma_val) if v > 0]
        return toks

    def emit(self, es, final_waits):
        nc = self.nc
        sems = {e: es.enter_context(nc.semaphore("s_" + e)) for e in ENGS}
        dsems = [es.enter_context(nc.semaphore("d%d" % i)) for i in range(self.n_dma)]
        block = es.enter_context(nc.Block())

        def semobj(key):
            return dsems[key[1]] if isinstance(key, tuple) else sems[key]

        for e in ENGS:
            deco = getattr(block, ENG_ATTR[e])

            def body(eng, e=e):
                for op in self.ops[e]:
                    for key, val in op.waits:
                        eng.wait_ge(semobj(key), val)
                    ins = op.fn(eng)
                    if op.dma:
                        ins.then_inc(dsems[op.dsem], op.inc)
                    else:
                        ins.then_inc(sems[e], 1)
                if e == "sp":
                    for tok in final_waits:
                        if tok[0] == "dma":
                            eng.wait_ge(dsems[tok[1]], tok[2])
                        else:
                            eng.wait_ge(sems[tok[0]], tok[1])

            deco(body)


NCORES = 8
BATCH, SEQ, D = 2, 8192, 1024
NT = 2048
TT = 512
NTT = NT // TT
DC = D // 128
FF = 2816
FC = FF // 128
FH = 11
HALO = 16
EPS = 1e-6
NH, HD = 16, 64
GROUP_HEADS = ((0, 6), (6, 11), (11, 16))
DIL = (1, 4, 16)
V_GMIX0, V_GFFN0, V_PSC, V_GMIX1, V_GFFN1, V_GFIN = 0, 8, 16, 24, 32, 40

ARENA_BYTES = 212480


class Arena:
    def __init__(self, nc, es, name="arena", nbytes=ARENA_BYTES):
        self.t = es.enter_context(nc.sbuf_tensor(name, [128, nbytes // 2], BF16))
        self.nbytes = nbytes

    def view(self, off, shape, dtype):
        n = 1
        for s in shape:
            n *= s
        esz = DSZ[dtype]
        assert off % 4 == 0 and off + n * esz <= self.nbytes, (off, shape, dtype)
        ap = self.t[:, off // 2: off // 2 + n * esz // 2]
        if dtype != BF16:
            ap = ap.bitcast(dtype)
        if len(shape) == 2:
            ap = ap.rearrange("p (a b) -> p a b", a=shape[0])
        elif len(shape) == 3:
            ap = ap.rearrange("p (a b c) -> p a b c", a=shape[0], b=shape[1])
        return ap


class Ctx:
    def __init__(self, nc, es):
        self.nc = nc
        self.es = es
        self.S = Sched(nc)
        self.ps = es.enter_context(nc.psum_tensor("ps", [128, 8, 512], F32))
        self.bank = 0

    def psum(self):
        b = self.bank
        self.bank = (self.bank + 1) % 8
        return self.ps[:, b, :]

    def mm(self, out, pairs):
        pairs = list(pairs)
        n = len(pairs)

        def fn(e):
            ins = None
            for i, (l, r) in enumerate(pairs):
                ins = e.matmul(out, lhsT=l, rhs=r, start=(i == 0), stop=(i == n - 1))
            return ins
        reads = [p[0] for p in pairs] + [p[1] for p in pairs]
        return self.S.add("pe", fn, reads=reads, writes=[out])

    def dma(self, q, out, in_):
        return self.S.add(q, lambda e: e.dma_start(out=out, in_=in_), reads=[in_], writes=[out], dma=True)

    def act(self, out, in_, func, reads=None, **kw):
        rd = [in_] + [v for v in kw.values() if not isinstance(v, (int, float))]
        return self.S.add("act", lambda e: e.activation(out=out, in_=in_, func=func, **kw), reads=rd, writes=[out])

    def tt(self, eng, out, in0, in1, op):
        return self.S.add(eng, lambda e: e.tensor_tensor(out=out, in0=in0, in1=in1, op=op),
                          reads=[in0, in1], writes=[out])

    def stt(self, eng, out, in0, scalar, in1, op0, op1):
        rd = [in0, in1] + ([] if isinstance(scalar, (int, float)) else [scalar])
        return self.S.add(eng, lambda e: e.scalar_tensor_tensor(out=out, in0=in0, scalar=scalar, in1=in1,
                                                                op0=op0, op1=op1), reads=rd, writes=[out])

    def ts(self, eng, out, in0, scalar1, op0, scalar2=None, op1=None):
        rd = [in0] + [s for s in (scalar1, scalar2) if s is not None and not isinstance(s, (int, float))]
        if op1 is None:
            return self.S.add(eng, lambda e: e.tensor_scalar(out=out, in0=in0, scalar1=scalar1, scalar2=None,
                                                             op0=op0), reads=rd, writes=[out])
        return self.S.add(eng, lambda e: e.tensor_scalar(out=out, in0=in0, scalar1=scalar1, scalar2=scalar2,
                                                         op0=op0, op1=op1), reads=rd, writes=[out])

    def copy(self, eng, out, in_):
        return self.S.add(eng, lambda e: e.tensor_copy(out=out, in_=in_), reads=[in_], writes=[out])

    def recip(self, out, in_):
        return self.S.add("dve", lambda e: e.reciprocal(out=out, in_=in_), reads=[in_], writes=[out])

    def memset(self, eng, out, val):
        return self.S.add(eng, lambda e: e.memset(out, val), writes=[out])


def rmsnorm_tile(C, xsrc, hdst, n, gcol, ones, sq, rt, rstd):
    C.act(sq[:, :, :n], xsrc, AF.Square)
    ps = C.psum()
    C.mm(ps[:, :n], [(ones, sq[:, c, :n]) for c in range(DC)])
    C.act(rt[:, :n], ps[:, :n], AF.Sqrt, bias=EPS, scale=1.0 / D)
    C.recip(rstd[:, :n], rt[:, :n])
    for c in range(DC):
        C.stt("dve", hdst[:, c, :], xsrc[:, c, :], gcol(c), rstd[:, :n], ALU.mult, ALU.mult)


def tile_major_residual(C, xT, pairs_fn, after_tile=None):
    for t in range(NTT):
        tsl = slice(t * TT, (t + 1) * TT)
        for m in range(DC):
            ps = C.psum()
            C.mm(ps, pairs_fn(m, tsl))
            C.tt("dve", xT[:, m, tsl], xT[:, m, tsl], ps, ALU.add)
        if after_tile is not None and t >= 1:
            after_tile(t - 1)
    if after_tile is not None:
        after_tile(NTT - 1)


def ffn_layer(C, A, PB, xT, hT, wg, wu, wd, after_tile=None):
    aT = A.view(PB, [FH, NT], BF16)
    wdh = A.view(PB + 45056, [FH, D], BF16)
    wgu = [(A.view(PB + 67584 + i * 8192, [DC, 256], BF16),
            A.view(PB + 67584 + i * 8192 + 4096, [DC, 256], BF16)) for i in range(2)]
    stmp = [A.view(PB + 83968 + i * 2048, [TT], F32) for i in range(2)]
    wg_v = wg.rearrange("(c p) n -> p c n", p=128)
    wu_v = wu.rearrange("(c p) n -> p c n", p=128)
    wd_v = wd.rearrange("(j p) n -> p j n", p=128)
    gi = 0
    si = 0
    for half in range(2):
        j0 = half * FH
        C.dma("pool", wdh[:, 0:6, :], wd_v[:, j0:j0 + 6, :])
        C.dma("pool", wdh[:, 6:FH, :], wd_v[:, j0 + 6:j0 + FH, :])
        jl = 0
        while jl < FH:
            nj = min(2, FH - jl)
            bg, bu = wgu[gi % 2]
            gi += 1
            c0 = (j0 + jl) * 128
            C.dma("pool", bg[:, :, 0:nj * 128], wg_v[:, :, c0:c0 + nj * 128])
            C.dma("pool", bu[:, :, 0:nj * 128], wu_v[:, :, c0:c0 + nj * 128])
            for jj in range(nj):
                for t in range(NTT):
                    tsl = slice(t * TT, (t + 1) * TT)
                    pg = C.psum()
                    C.mm(pg, [(bg[:, k, jj * 128:(jj + 1) * 128], hT[:, k, tsl]) for k in range(DC)])
                    pu = C.psum()
                    C.mm(pu, [(bu[:, k, jj * 128:(jj + 1) * 128], hT[:, k, tsl]) for k in range(DC)])
                    st = stmp[si % 2]
                    si += 1
                    C.act(st, pg, AF.Silu)
                    C.tt("dve", aT[:, jl + jj, tsl], st, pu, ALU.mult)
            jl += nj
        tile_major_residual(C, xT, lambda m, tsl: [(wdh[:, j, m * 128:(m + 1) * 128], aT[:, j, tsl])
                                                   for j in range(FH)],
                            after_tile if half == 1 else None)


PB = 118784


def common_views(A):
    v = {}
    v["xT"] = A.view(0, [DC, NT], F32)
    v["hT"] = A.view(65536, [DC, NT], BF16)
    v["sq"] = A.view(98304, [DC, TT], BF16)
    v["rt"] = A.view(106496, [TT], F32)
    v["rstd"] = A.view(108544, [TT], F32)
    v["vecs"] = A.view(110592, [48], F32)
    v["ones"] = A.view(110784, [128], BF16)
    v["invc"] = A.view(111040, [DC, HALO], F32)
    v["xh"] = A.view(111552, [DC, HALO], F32)
    v["hh"] = A.view(112064, [DC, HALO], BF16)
    v["t16"] = A.view(112320, [HALO], F32)
    v["hvt"] = A.view(112384, [16], F32)
    return v


def pool_layer(C, A, V, w_in, w_grp, w_out, after_tile=None):
    xT, hT, hh, invc, vecs = V["xT"], V["hT"], V["hh"], V["invc"], V["vecs"]
    pT = A.view(PB, [DC, NT], BF16)
    win = A.view(PB + 32768, [DC, D], BF16)
    wgr = A.view(PB + 49152, [4, 2, 256], BF16)
    NU = NT + HALO
    ub = [A.view(PB + 53248 + i * 8256, [NU], F32) for i in range(2)]
    wa = A.view(PB + 69760, [NU], F32)
    wb = A.view(PB + 69760 + 8256, [NU], F32)
    wout = A.view(PB + 53248, [DC, D], BF16)
    zT = hT
    C.dma("pool", win[:, 0:4, :], w_in.rearrange("(c p) n -> p c n", p=128)[:, 0:4, :])
    C.dma("pool", win[:, 4:8, :], w_in.rearrange("(c p) n -> p c n", p=128)[:, 4:8, :])
    C.dma("pool", wgr, w_grp.rearrange("g (c p) n -> p g c n", p=128))
    for m in range(DC):
        u = ub[m % 2]
        msl = slice(m * 128, (m + 1) * 128)
        ps = C.psum()
        C.mm(ps[:, :HALO], [(win[:, k, msl], hh[:, k, :]) for k in range(DC)])
        C.act(u[:, 0:HALO], ps[:, :HALO], AF.Copy)
        for t in range(NTT):
            tsl = slice(t * TT, (t + 1) * TT)
            ps = C.psum()
            C.mm(ps, [(win[:, k, msl], hT[:, k, tsl]) for k in range(DC)])
            C.act(u[:, HALO + t * TT:HALO + (t + 1) * TT], ps, AF.Copy)
        g = m // 2
        L = g + 1
        w = 2 ** L
        src = u
        bufs = [wa, wb]
        for l in range(1, L + 1):
            sh = 2 ** (l - 1)
            lo = 2 ** l - 1
            dst = bufs[(l - 1) % 2]
            C.tt("dve", dst[:, lo:NU], src[:, lo:NU], src[:, lo - sh:NU - sh], ALU.add)
            src = dst
        C.stt("dve", pT[:, m, :], src[:, HALO:NU], 1.0 / w, u[:, HALO:NU], ALU.mult, ALU.subtract)
        t16 = V["t16"]
        C.tt("dve", t16, src[:, HALO:2 * HALO], invc[:, m, :], ALU.mult)
        C.tt("dve", pT[:, m, 0:HALO], t16, u[:, HALO:2 * HALO], ALU.subtract)
    C.dma("pool", wout[:, 0:4, :], w_out.rearrange("(c p) n -> p c n", p=128)[:, 0:4, :])
    C.dma("pool", wout[:, 4:8, :], w_out.rearrange("(c p) n -> p c n", p=128)[:, 4:8, :])
    for m in range(DC):
        g, mo = m // 2, m % 2
        for t in range(NTT):
            tsl = slice(t * TT, (t + 1) * TT)
            ps = C.psum()
            C.mm(ps, [(wgr[:, g, ki, mo * 128:(mo + 1) * 128], pT[:, 2 * g + ki, tsl]) for ki in range(2)])
            C.act(zT[:, m, tsl], ps, AF.Copy, scale=vecs[:, V_PSC + m:V_PSC + m + 1])
    tile_major_residual(C, xT, lambda m, tsl: [(wout[:, k, m * 128:(m + 1) * 128], zT[:, k, tsl])
                                               for k in range(DC)], after_tile)


def build_layer0():
    nc = bass.Bass("TRN2", target_bir_lowering=False)
    dt_in = lambda n, s: nc.dram_tensor(n, s, F32, kind="ExternalInput").ap()
    xT_d = dt_in("xT", [D, NT])
    xh_d = dt_in("xh", [D, HALO])
    vecs_d = dt_in("vecs", [128, 48])
    invc_d = dt_in("invc", [128, DC * HALO])
    w_in = dt_in("w_in", [D, D])
    w_grp = dt_in("w_grp", [4, 256, 256])
    w_out = dt_in("w_out", [D, D])
    wg = dt_in("wg0", [D, FF])
    wu = dt_in("wu0", [D, FF])
    wd = dt_in("wd0", [FF, D])
    x1_d = nc.dram_tensor("x1T", [D, NT], F32, kind="ExternalOutput").ap()
    with ExitStack() as es:
        A = Arena(nc, es)
        C = Ctx(nc, es)
        V = common_views(A)
        layer0_body(C, A, V, xT_d, xh_d, vecs_d, invc_d, w_in, w_grp, w_out, wg, wu, wd)
        toks = []
        x1v = x1_d.rearrange("(c p) t -> p c t", p=128)
        for t in range(NTT):
            tsl = slice(t * TT, (t + 1) * TT)
            toks.append(C.dma("sp", x1v[:, :, tsl], V["xT"][:, :, tsl]))
        C.S.emit(es, toks)
    return nc


def layer0_body(C, A, V, xT_d, xh_d, vecs_d, invc_d, w_in, w_grp, w_out, wg, wu, wd, next_norm=None):
    xT, hT = V["xT"], V["hT"]
    C.dma("sp", V["vecs"], vecs_d)
    C.dma("sp", V["invc"], invc_d.rearrange("p (c h) -> p c h", c=DC))
    C.dma("sp", V["xh"], xh_d.rearrange("(c p) h -> p c h", p=128))
    xv = xT_d.rearrange("(c p) t -> p c t", p=128)
    for t in range(NTT):
        tsl = slice(t * TT, (t + 1) * TT)
        C.dma("sp", xT[:, :, tsl], xv[:, :, tsl])
    C.memset("dve", V["ones"], 1.0)
    gcol = lambda base: (lambda c: V["vecs"][:, base + c:base + c + 1])
    rmsnorm_tile(C, V["xh"], V["hh"], HALO, gcol(V_GMIX0), V["ones"], V["sq"], V["rt"], V["rstd"])
    for t in range(NTT):
        tsl = slice(t * TT, (t + 1) * TT)
        rmsnorm_tile(C, xT[:, :, tsl], hT[:, :, tsl], TT, gcol(V_GMIX0), V["ones"], V["sq"], V["rt"], V["rstd"])
    def norm_ffn0(t):
        tsl = slice(t * TT, (t + 1) * TT)
        rmsnorm_tile(C, xT[:, :, tsl], hT[:, :, tsl], TT, gcol(V_GFFN0), V["ones"], V["sq"], V["rt"], V["rstd"])
    pool_layer(C, A, V, w_in, w_grp, w_out, after_tile=norm_ffn0)
    ffn_layer(C, A, PB, xT, hT, wg, wu, wd, after_tile=next_norm)


def _vec_layout(v):
    return np.ascontiguousarray(np.asarray(v, np.float32).reshape(DC, 128).T)


def _make_vecs(norm_mix, norm_ffn, norm_final, pool_scale):
    cols = [norm_mix[0], norm_ffn[0], pool_scale[0], norm_mix[1], norm_ffn[1], norm_final]
    return np.ascontiguousarray(np.concatenate([_vec_layout(c) for c in cols], axis=1))


def _invc_table(first):
    t = np.zeros((128, DC, HALO), np.float32)
    for c in range(DC):
        w = 2 ** (c // 2 + 1)
        for i in range(HALO):
            t[:, c, i] = 1.0 / (min(i + 1, w) if first else w)
    return np.ascontiguousarray(t.reshape(128, DC * HALO))


def _core_tokens(x):
    xs, hs = [], []
    cps = SEQ // NT
    for core in range(NCORES):
        b, c = core // cps, core % cps
        xs.append(np.ascontiguousarray(x[b, c * NT:(c + 1) * NT, :].T))
        if c == 0:
            hs.append(np.zeros((D, HALO), np.float32))
        else:
            hs.append(np.ascontiguousarray(x[b, c * NT - HALO:c * NT, :].T))
    return xs, hs


def dsl(start, d, n=128):
    return slice(start, start + d * (n - 1) + 1, d)


def head_info(h):
    for g, (h0, h1) in enumerate(GROUP_HEADS):
        if h0 <= h < h1:
            return g, h0, h1 - h0
    raise ValueError(h)


def layer1_views(A):
    L = {}
    L["kT"] = A.view(0, [DC, NT], BF16)
    L["xs"] = [A.view(i * 16384, [DC, TT], F32) for i in range(2)]
    L["khb"] = A.view(32768, [3, NT], BF16)
    L["khs"] = A.view(45056, [5, TT], BF16)
    L["vh"] = [A.view(50176, [1, 6, HD], BF16), A.view(50176 + 768, [4, 5, HD], BF16),
               A.view(50176 + 768 + 2560, [16, 5, HD], BF16)]
    L["qT"] = A.view(PB, [DC, NT], BF16)
    L["hTh"] = L["qT"]
    vo = PB + 32768
    L["v"] = [A.view(vo, [16, 6, HD], BF16), A.view(vo + 12288, [16, 5, HD], BF16),
              A.view(vo + 12288 + 10240, [16, 5, HD], BF16)]
    L["cos"] = A.view(vo, [2 * NT], F32)
    L["sin"] = A.view(vo + 16384, [2 * NT], F32)
    wo_ = PB + 65536
    L["wqk"] = [A.view(wo_ + i * 2048, [DC, 128], BF16) for i in range(2)]
    L["wv"] = A.view(wo_ + 4096, [DC, 384], BF16)
    L["R"] = A.view(wo_ + 10240, [128], BF16)
    L["Dg"] = A.view(wo_, [3, NT], F32)
    L["wo"] = A.view(PB, [DC, D], BF16)
    b = 98304
    L["qb"] = [A.view(b + i * 1024, [TT], BF16) for i in range(2)]
    L["t1"] = [A.view(b + 2048 + i * 2048, [TT], F32) for i in range(2)]
    L["t2"] = [A.view(b + 6144 + i * 2048, [TT], F32) for i in range(2)]
    L["pT"] = [A.view(b + i * 2048, [1024], BF16) for i in range(2)]
    L["rden"] = [A.view(b + 4096 + i * 2048, [TT], F32) for i in range(2)]
    L["masks"] = A.view(112448, [3, 1024], BF16)
    return L


def build_layer1_body(C, A, V, x1_own_d, x1_halo_d, tab_d, mask_d, R_d, hv_d,
                      w_qkv, w_o, wg, wu, wd, out_d, own_in_sbuf=False, gather=None, own_normed=False):
    S = C.S
    L = layer1_views(A)
    xT, hT, vecs, ones = V["xT"], V["hT"], V["vecs"], V["ones"]
    import os as _os
    _stop = _os.environ.get("KSTOP", "")

    def stop_here(tag, src_f32):
        if _stop != tag:
            return None
        ov_ = out_d.rearrange("(c p) t -> p c t", p=128)
        return [C.dma("sp", ov_[:, :, 0:TT], src_f32)]
    gcol = lambda base: (lambda c: vecs[:, base + c:base + c + 1])
    nrm = lambda xsrc, hdst, n, base: rmsnorm_tile(C, xsrc, hdst, n, gcol(base), ones, V["sq"], V["rt"], V["rstd"])
    x1o = x1_own_d.rearrange("(c p) t -> p c t", p=128)
    x1h = None if gather is not None else x1_halo_d.rearrange("(c p) t -> p c t", p=128)
    wq_v = w_qkv.rearrange("(c p) n -> p c n", p=128)

    masks, Rm, hv = L["masks"], L["R"], V["hvt"]
    C.dma("sp", hv[:, 0:1], hv_d)
    C.dma("pool", Rm, R_d)
    C.dma("pool", masks[:, 0, :], mask_d)
    for i in range(4):
        C.dma("sp", L["cos"][:, i * 1024:(i + 1) * 1024], tab_d[0][:, i * 1024:(i + 1) * 1024])
        C.dma("sp", L["sin"][:, i * 1024:(i + 1) * 1024], tab_d[1][:, i * 1024:(i + 1) * 1024])
    C.copy("pool", masks[:, 1, :], masks[:, 0, :])
    C.ts("dve", masks[:, 1, 0:128], masks[:, 0, 0:128], hv[:, 0:1], ALU.mult)
    C.copy("pool", masks[:, 2, :], masks[:, 0, :])
    m2v = masks[:, 2, :].rearrange("p (e s) -> p e s", e=4)
    m0v = masks[:, 0, :].rearrange("p (e s) -> p e s", e=4)
    C.ts("dve", m2v[:, :, 0:128], m0v[:, :, 0:128], hv[:, 0:1], ALU.mult)

    if not own_in_sbuf:
        for t in range(NTT):
            tsl = slice(t * TT, (t + 1) * TT)
            C.dma("sp", xT[:, :, tsl], x1o[:, :, tsl])
    if not own_normed:
        for t in range(NTT):
            tsl = slice(t * TT, (t + 1) * TT)
            nrm(xT[:, :, tsl], hT[:, :, tsl], TT, V_GMIX1)

    hTh = L["hTh"]
    cnt = {"w": 0, "r": 0}

    pend = []

    def proj_rope(wcol0, src, s_tsl, tab_off, dst):
        wbuf = L["wqk"][(cnt["w"] - 1) % 2]
        i = cnt["r"] % 2
        cnt["r"] += 1
        pk = C.psum()
        C.mm(pk, [(wbuf[:, k, :], src[:, k, s_tsl]) for k in range(DC)])
        qb, t1, t2 = L["qb"][i], L["t1"][i], L["t2"][i]
        C.act(qb, pk, AF.Copy)

        def stage_b():
            pr = C.psum()
            C.mm(pr, [(Rm, qb)])
            C.tt("dve", t1, pk, L["cos"][:, tab_off:tab_off + TT], ALU.mult)
            C.tt("dve", t2, pr, L["sin"][:, tab_off:tab_off + TT], ALU.mult)
            C.tt("pool", dst, t1, t2, ALU.add)
        pend.append(stage_b)
        if len(pend) > 1:
            pend.pop(0)()

    def rope_flush():
        while pend:
            pend.pop(0)()

    def load_wqk(col0):
        wbuf = L["wqk"][cnt["w"] % 2]
        cnt["w"] += 1
        C.dma("pool", wbuf, wq_v[:, :, col0:col0 + 128])

    def v_blocks(src, dstv, g, blocks):
        h0, nh = GROUP_HEADS[g][0], GROUP_HEADS[g][1] - GROUP_HEADS[g][0]
        ncol = nh * HD
        for bi, sl in blocks:
            ps = C.psum()
            C.mm(ps[:, :ncol], [(src[:, k, sl], L["wv"][:, k, 0:ncol]) for k in range(DC)])
            C.act(dstv[:, bi, :, :], ps[:, :ncol].rearrange("p (h d) -> p h d", h=nh), AF.Copy)

    def halo_blocks(g):
        d = DIL[g]
        return [(r, dsl(NT - 128 * d + r, d)) for r in range(d)]

    def own_blocks(g):
        d = DIL[g]
        nb = 16 // d
        return [(r * nb + b, dsl(r + d * 128 * b, d)) for r in range(d) for b in range(nb)]

    def do_k_own():
        for c in range(DC):
            load_wqk(D + c * 128)
            for t in range(NTT):
                tsl = slice(t * TT, (t + 1) * TT)
                proj_rope(D + c * 128, hT, tsl, NT + t * TT, L["kT"][:, c, tsl])
        rope_flush()

    def do_halo():
        if gather is None:
            for t in range(NTT):
                tsl = slice(t * TT, (t + 1) * TT)
                xs = L["xs"][t % 2]
                C.dma("sp", xs, x1h[:, :, tsl])
                nrm(xs, hTh[:, :, tsl], TT, V_GMIX1)
        else:
            pass
        for c in range(DC):
            load_wqk(D + c * 128)
            tiles = range(NTT) if c >= 5 else [NTT - 1]
            for t in tiles:
                tsl = slice(t * TT, (t + 1) * TT)
                dst = L["khb"][:, c - 5, tsl] if c >= 5 else L["khs"][:, c, :]
                proj_rope(D + c * 128, hTh, tsl, t * TT, dst)
        rope_flush()
        for g in range(3):
            h0, h1 = GROUP_HEADS[g]
            C.dma("pool", L["wv"][:, :, 0:(h1 - h0) * HD], wq_v[:, :, 2 * D + h0 * HD:2 * D + h1 * HD])
            v_blocks(hTh, L["vh"][g], g, halo_blocks(g))

    if gather is not None:
        hx_in, hx_all = gather
        x1s_v = x1_own_d.rearrange("(c p) t -> p c t", p=128)
        hin_v = hx_in.ap().rearrange("(c p) t -> p c t", p=128)
        for t in range(NTT):
            tsl = slice(t * TT, (t + 1) * TT)
            C.dma("sp", hin_v[:, :, tsl], hT[:, :, tsl])
            C.dma("sp", x1s_v[:, :, tsl], xT[:, :, tsl])
        S.add("pool", lambda e: e.collective_compute("AllGather", ALU.bypass, replica_groups=[list(range(NCORES))],
                                                     ins=[hx_in.ap().opt()], outs=[hx_all.ap().opt()]),
              reads=[hx_in.ap()], writes=[hx_all.ap()], dma=True, cc=True)
        hall_v = hx_all.ap().rearrange("(r c p) t -> r p c t", r=NCORES, p=128)
        for t in range(NTT):
            tsl = slice(t * TT, (t + 1) * TT)

            def rd(e, tsl=tsl):
                prev = (e.partition_id() + (NCORES - 1)) % NCORES
                return e.dma_start(out=hTh[:, :, tsl],
                                   in_=hall_v[bass.ds(prev, 1), :, :, tsl].rearrange("r p c t -> (r p) c t"))
            S.add("pool", rd, reads=[hx_all.ap()], writes=[hTh[:, :, tsl]], dma=True)
        do_k_own()
        do_halo()
    else:
        do_halo()
        do_k_own()

    for c in range(DC):
        load_wqk(c * 128)
        for t in range(NTT):
            tsl = slice(t * TT, (t + 1) * TT)
            proj_rope(c * 128, hT, tsl, NT + t * TT, L["qT"][:, c, tsl])
    rope_flush()
    for g in range(3):
        h0, h1 = GROUP_HEADS[g]
        C.dma("pool", L["wv"][:, :, 0:(h1 - h0) * HD], wq_v[:, :, 2 * D + h0 * HD:2 * D + h1 * HD])
        v_blocks(hT, L["v"][g], g, own_blocks(g))

    _r = stop_here("qkv", L["xs"][1])
    if _r:
        return _r
    oT = hT
    Dg = L["Dg"]
    C.memset("pool", Dg[0:64, :, :], 0.0)
    psS = [C.ps[:, 0:2, :].rearrange("p a b -> p (a b)"), C.ps[:, 2:4, :].rearrange("p a b -> p (a b)")]
    psO = [C.ps[:, 4:6, :].rearrange("p a b -> p (a b)"), C.ps[:, 6:8, :].rearrange("p a b -> p (a b)")]
    bi = 0
    for h in range(NH):
        g, h0, nh = head_info(h)
        d = DIL[g]
        nb = 16 // d
        c, ro, hl = h // 2, (h % 2) * HD, h - h0
        prow = slice(ro, ro + HD)
        entries = [(r, b) for r in range(d) for b in range(nb)]
        for q in range(4):
            ents = entries[4 * q:4 * q + 4]
            pS, pO, pT, rden = psS[bi % 2], psO[bi % 2], L["pT"][bi % 2], L["rden"][bi % 2]
            bi += 1
            n_halo = sum(1 for (r, b) in ents if b == 0)
            mk = masks[:, 0, :] if n_halo == 0 else (masks[:, 1, :] if n_halo == 1 else masks[:, 2, :])
            assert n_halo in (0, 4) or (n_halo == 1 and ents[0][1] == 0)
            kv = []
            for e, (r, b) in enumerate(ents):
                start = r + d * 128 * b
                qsl = dsl(start, d)
                qa = L["qT"][prow, c, qsl]
                kcur = L["kT"][prow, c, qsl]
                vcur = L["v"][g][:, r * nb + b, hl, :]
                if b >= 1:
                    kprev = L["kT"][prow, c, dsl(start - d * 128, d)]
                    vprev = L["v"][g][:, r * nb + b - 1, hl, :]
                else:
                    hsl = dsl(NT - 128 * d + r, d)
                    if c >= 5:
                        kprev = L["khb"][prow, c - 5, hsl]
                    else:
                        kprev = L["khs"][prow, c, dsl(hsl.start - (NT - TT), d)]
                    vprev = L["vh"][g][:, r, hl, :]
                C.mm(pS[:, e * 256:e * 256 + 128], [(kprev, qa)])
                C.mm(pS[:, e * 256 + 128:e * 256 + 256], [(kcur, qa)])
                kv.append((vprev, vcur))
            C.act(pT, pS, AF.Exp, scale=HD ** -0.5)
            C.tt("pool", pT, pT, mk, ALU.mult)
            for e, (vprev, vcur) in enumerate(kv):
                pp, pc = pT[:, e * 256:e * 256 + 128], pT[:, e * 256 + 128:e * 256 + 256]
                C.mm(pO[0:64, e * 256:e * 256 + 128], [(vprev, pp), (vcur, pc)])
                C.mm(pO[0:64, e * 256 + 128:e * 256 + 256], [(ones[:, 0:64], pp), (ones[:, 0:64], pc)])
            pOv = pO[0:64, :].rearrange("p (e s) -> p e s", e=4)
            num, den = pOv[:, :, 0:128], pOv[:, :, 128:256]
            rdv = rden[0:64, :].rearrange("p (e s) -> p e s", e=4)
            if d == 1:
                tv = lambda ap3: ap3[:, q * TT:(q + 1) * TT].rearrange("p (e s) -> p e s", e=4)
            elif d == 4:
                r0 = ents[0][0]
                tv = lambda ap3: ap3[:, dsl(r0, 4, 512)].rearrange("p (e s) -> p e s", e=4)
            else:
                r0 = ents[0][0]
                tv = lambda ap3: ap3[:, :].rearrange("p (s r) -> p r s", r=16)[:, r0:r0 + 4, :]
            C.act(rdv, den, AF.Ln)
            C.act(rdv, rdv, AF.Exp, scale=-1.0)
            C.tt("dve", tv(oT[prow, c, :]), num, rdv, ALU.mult)
            dgv = tv(Dg[0:64, g, :])
            C.tt("dve", dgv, den, dgv, ALU.add)

    _r = stop_here("attn", L["cos"][:, 0:4096].rearrange("p (c t) -> p c t", c=DC))
    if _r:
        return _r
    C.dma("pool", L["wo"][:, 0:4, :], w_o.rearrange("(c p) n -> p c n", p=128)[:, 0:4, :])
    C.dma("pool", L["wo"][:, 4:8, :], w_o.rearrange("(c p) n -> p c n", p=128)[:, 4:8, :])
    for t in range(NTT):
        tsl = slice(t * TT, (t + 1) * TT)
        C.dma("sp", xT[:, :, tsl], x1o[:, :, tsl])
    tot = A.view(98304, [NT], F32)
    ng = [GROUP_HEADS[g][1] - GROUP_HEADS[g][0] for g in range(3)]
    C.ts("dve", tot[0:64, :], Dg[0:64, 0, :], 1.0 / ng[0], ALU.mult)
    C.stt("dve", tot[0:64, :], Dg[0:64, 1, :], 1.0 / ng[1], tot[0:64, :], ALU.mult, ALU.add)
    C.stt("dve", tot[0:64, :], Dg[0:64, 2, :], 1.0 / ng[2], tot[0:64, :], ALU.mult, ALU.add)
    C.recip(tot[0:64, :], tot[0:64, :])
    for g in range(3):
        C.stt("dve", Dg[0:64, g, :], Dg[0:64, g, :], 3.0 / ng[g], tot[0:64, :], ALU.mult, ALU.mult)
    C.act(Dg[64:128, :, :], Dg[0:64, :, :], AF.Copy)
    for c in range(DC):
        ga, gb = head_info(2 * c)[0], head_info(2 * c + 1)[0]
        if ga == gb:
            C.tt("dve", oT[:, c, :], oT[:, c, :], Dg[:, ga, :], ALU.mult)
        else:
            C.tt("dve", oT[0:64, c, :], oT[0:64, c, :], Dg[0:64, ga, :], ALU.mult)
            C.tt("dve", oT[64:128, c, :], oT[64:128, c, :], Dg[64:128, gb, :], ALU.mult)
    def norm_ffn1(t):
        tsl = slice(t * TT, (t + 1) * TT)
        nrm(xT[:, :, tsl], hT[:, :, tsl], TT, V_GFFN1)
    tile_major_residual(C, xT, lambda m, tsl: [(L["wo"][:, k, m * 128:(m + 1) * 128], oT[:, k, tsl])
                                               for k in range(DC)], norm_ffn1)

    _r = stop_here("oproj", xT[:, :, 0:TT])
    if _r:
        return _r
    ob = [A.view(65536 + i * 16384, [DC, TT], F32) for i in range(2)]
    ov = out_d.rearrange("(c p) t -> p c t", p=128)
    toks = []

    def norm_final(t):
        tsl = slice(t * TT, (t + 1) * TT)
        rmsnorm_tile(C, xT[:, :, tsl], ob[t % 2], TT, gcol(V_GFIN), ones, V["sq"], V["rt"], V["rstd"])
        toks.append(C.dma("sp", ov[:, :, tsl], ob[t % 2]))
    ffn_layer(C, A, PB, xT, hT, wg, wu, wd, after_tile=norm_final)
    return toks


def build_layer1():
    nc = bass.Bass("TRN2", target_bir_lowering=False)
    dt_in = lambda n, s: nc.dram_tensor(n, s, F32, kind="ExternalInput").ap()
    x1o = dt_in("x1T", [D, NT])
    x1h = dt_in("x1h", [D, NT])
    vecs_d = dt_in("vecs", [128, 48])
    hv_d = dt_in("hv", [128, 1])
    tab_d = dt_in("tab", [2, 128, 2 * NT])
    mask_d = dt_in("mask", [128, 1024])
    R_d = dt_in("rmat", [128, 128])
    w_qkv = dt_in("w_qkv", [D, 3 * D])
    w_o = dt_in("w_o", [D, D])
    wg = dt_in("wg1", [D, FF])
    wu = dt_in("wu1", [D, FF])
    wd = dt_in("wd1", [FF, D])
    out_d = nc.dram_tensor("outT", [D, NT], F32, kind="ExternalOutput").ap()
    with ExitStack() as es:
        A = Arena(nc, es)
        C = Ctx(nc, es)
        V = common_views(A)
        C.dma("sp", V["vecs"], vecs_d)
        C.memset("dve", V["ones"], 1.0)
        toks = build_layer1_body(C, A, V, x1o, x1h, tab_d, mask_d, R_d, hv_d, w_qkv, w_o, wg, wu, wd, out_d)
        C.S.emit(es, toks)
    return nc


def build_fused():
    nc = bass.Bass("TRN2", target_bir_lowering=False)
    dt_in = lambda n, s: nc.dram_tensor(n, s, F32, kind="ExternalInput").ap()
    xT_d = dt_in("xT", [D, NT])
    xh_d = dt_in("xh", [D, HALO])
    vecs_d = dt_in("vecs", [128, 48])
    invc_d = dt_in("invc", [128, DC * HALO])
    hv_d = dt_in("hv", [128, 1])
    tab_d = dt_in("tab", [2, 128, 2 * NT])
    mask_d = dt_in("mask", [128, 1024])
    R_d = dt_in("rmat", [128, 128])
    w_in = dt_in("w_in", [D, D])
    w_grp = dt_in("w_grp", [4, 256, 256])
    w_out = dt_in("w_out", [D, D])
    wg0 = dt_in("wg0", [D, FF])
    wu0 = dt_in("wu0", [D, FF])
    wd0 = dt_in("wd0", [FF, D])
    w_qkv = dt_in("w_qkv", [D, 3 * D])
    w_o = dt_in("w_o", [D, D])
    wg1 = dt_in("wg1", [D, FF])
    wu1 = dt_in("wu1", [D, FF])
    wd1 = dt_in("wd1", [FF, D])
    out_d = nc.dram_tensor("outT", [D, NT], F32, kind="ExternalOutput").ap()
    x1s = nc.dram_tensor("x1s", [D, NT], F32)
    hx_in = nc.dram_tensor("hx_in", [D, NT], BF16)
    hx_all = nc.dram_tensor("hx_all", [NCORES * D, NT], BF16)
    with ExitStack() as es:
        A = Arena(nc, es)
        C = Ctx(nc, es)
        V = common_views(A)
        def norm_mix1(t):
            tsl = slice(t * TT, (t + 1) * TT)
            rmsnorm_tile(C, V["xT"][:, :, tsl], V["hT"][:, :, tsl], TT,
                         lambda c: V["vecs"][:, V_GMIX1 + c:V_GMIX1 + c + 1], V["ones"], V["sq"], V["rt"], V["rstd"])
        layer0_body(C, A, V, xT_d, xh_d, vecs_d, invc_d, w_in, w_grp, w_out, wg0, wu0, wd0, next_norm=norm_mix1)
        toks = build_layer1_body(C, A, V, x1s.ap(), None, tab_d, mask_d, R_d, hv_d, w_qkv, w_o, wg1, wu1, wd1,
                                 out_d, own_in_sbuf=True, gather=(hx_in, hx_all), own_normed=True)
        C.S.emit(es, toks)
    return nc


def _rope_tables(pos0):
    inv_freq = (1.0 / (10000.0 ** (np.arange(0, HD, 2, dtype=np.float32) / np.float32(HD)))).astype(np.float32)
    pos = np.arange(pos0 - NT, pos0 + NT, dtype=np.float32)
    ang = (pos[None, :] * inv_freq[:, None]).astype(np.float32)
    cos = np.cos(ang).astype(np.float32)
    sin = np.sin(ang).astype(np.float32)
    rows = np.arange(128)
    ctab = cos[rows % 32]
    sgn = np.where((rows % 64) < 32, -1.0, 1.0).astype(np.float32)
    stab = sin[rows % 32] * sgn[:, None]
    return np.ascontiguousarray(np.stack([ctab, stab]).astype(np.float32))


def _mask_table():
    k = np.arange(128)[:, None]
    q = np.arange(128)[None, :]
    prev = (q <= k).astype(np.float32)
    cur = (k <= q).astype(np.float32)
    one = np.concatenate([prev, cur], axis=1)
    return np.ascontiguousarray(np.tile(one, (1, 4)))


def _rot_matrix():
    m = np.arange(128)
    partner = (m // 64) * 64 + ((m % 64) + 32) % 64
    R = np.zeros((128, 128), np.float32)
    R[partner, m] = 1.0
    return R


_NC_CACHE = {}


def _get_nc(name, builder):
    if name not in _NC_CACHE:
        _NC_CACHE[name] = builder()
    return _NC_CACHE[name]


def kernel_unfused(x, norm_mix, norm_ffn, norm_final, pool_w_in, pool_w_group, pool_scale, pool_w_out,
                   attn_w_qkv, attn_w_out, ffn_w_gate, ffn_w_up, ffn_w_down):
    f = lambda a: np.ascontiguousarray(np.asarray(a, dtype=np.float32))
    x = f(x)
    cps = SEQ // NT
    vecs = _make_vecs(f(norm_mix), f(norm_ffn), f(norm_final), f(pool_scale))
    xs, hs = _core_tokens(x)
    nc0 = _get_nc("l0", build_layer0)
    in0 = []
    for core in range(NCORES):
        in0.append({"xT": xs[core], "xh": hs[core], "vecs": vecs, "invc": _invc_table(core % cps == 0),
                    "w_in": f(pool_w_in[0]), "w_grp": f(pool_w_group[0]), "w_out": f(pool_w_out[0]),
                    "wg0": f(ffn_w_gate[0]), "wu0": f(ffn_w_up[0]), "wd0": f(ffn_w_down[0])})
    r0 = run_bass_kernel_spmd(nc0, in0, core_ids=list(range(NCORES)))
    x1 = [r["x1T"] for r in r0.results]
    nc1 = _get_nc("l1", build_layer1)
    mask = _mask_table()
    rmat = _rot_matrix()
    in1 = []
    for core in range(NCORES):
        first = core % cps == 0
        halo = np.zeros((D, NT), np.float32) if first else x1[core - 1]
        in1.append({"x1T": x1[core], "x1h": halo, "vecs": vecs,
                    "hv": np.full((128, 1), 0.0 if first else 1.0, np.float32),
                    "tab": _rope_tables((core % cps) * NT), "mask": mask, "rmat": rmat,
                    "w_qkv": f(attn_w_qkv[0]), "w_o": f(attn_w_out[0]),
                    "wg1": f(ffn_w_gate[1]), "wu1": f(ffn_w_up[1]), "wd1": f(ffn_w_down[1])})
    r1 = run_bass_kernel_spmd(nc1, in1, core_ids=list(range(NCORES)))
    out = np.stack([r["outT"].T for r in r1.results]).reshape(BATCH, SEQ, D)
    return np.ascontiguousarray(out.astype(np.float32))


def kernel(x, norm_mix, norm_ffn, norm_final, pool_w_in, pool_w_group, pool_scale, pool_w_out,
           attn_w_qkv, attn_w_out, ffn_w_gate, ffn_w_up, ffn_w_down):
    f = lambda a: np.ascontiguousarray(np.asarray(a, dtype=np.float32))
    x = f(x)
    cps = SEQ // NT
    vecs = _make_vecs(f(norm_mix), f(norm_ffn), f(norm_final), f(pool_scale))
    xs, hs = _core_tokens(x)
    nc = _get_nc("fused", build_fused)
    mask = _mask_table()
    rmat = _rot_matrix()
    shared = {"vecs": vecs, "mask": mask, "rmat": rmat,
              "w_in": f(pool_w_in[0]), "w_grp": f(pool_w_group[0]), "w_out": f(pool_w_out[0]),
              "wg0": f(ffn_w_gate[0]), "wu0": f(ffn_w_up[0]), "wd0": f(ffn_w_down[0]),
              "w_qkv": f(attn_w_qkv[0]), "w_o": f(attn_w_out[0]),
              "wg1": f(ffn_w_gate[1]), "wu1": f(ffn_w_up[1]), "wd1": f(ffn_w_down[1])}
    in_maps = []
    for core in range(NCORES):
        first = core % cps == 0
        m = dict(shared)
        m.update({"xT": xs[core], "xh": hs[core], "invc": _invc_table(first),
                  "hv": np.full((128, 1), 0.0 if first else 1.0, np.float32),
                  "tab": _rope_tables((core % cps) * NT)})
        in_maps.append(m)
    r = run_bass_kernel_spmd(nc, in_maps, core_ids=list(range(NCORES)))
    out = np.stack([rr["outT"].T for rr in r.results]).reshape(BATCH, SEQ, D)
    return np.ascontiguousarray(out.astype(np.float32))
```

```python
import numpy as np
from contextlib import ExitStack
import concourse.bass as bass
import concourse.mybir as mybir
from concourse.bass_utils import run_bass_kernel_spmd

F32 = mybir.dt.float32
BF16 = mybir.dt.bfloat16
ALU = mybir.AluOpType
AF = mybir.ActivationFunctionType
DSZ = {F32: 4, BF16: 2}

ENGS = ("pe", "act", "dve", "pool", "sp")
ENG_ATTR = {"pe": "tensor", "act": "scalar", "dve": "vector", "pool": "gpsimd", "sp": "sync"}


class _Op:
    __slots__ = ("fn", "waits", "dma", "dsem", "inc")

    def __init__(self, fn, waits, dma, dsem, inc=16):
        self.fn, self.waits, self.dma, self.dsem, self.inc = fn, waits, dma, dsem, inc


class Sched:
    def __init__(self, nc, n_dma_sems=48):
        self.nc = nc
        self.ops = {e: [] for e in ENGS}
        self.count = {e: 0 for e in ENGS}
        self.waited = {e: {} for e in ENGS}
        self.acc = {}
        self.n_dma = n_dma_sems
        self.dma_val = [0] * n_dma_sems
        half = n_dma_sems // 2
        self.dma_pools = {"sw": list(range(0, half - 2)), "hw": list(range(half, n_dma_sems)),
                          "cc": list(range(half - 2, half))}
        self.dma_rr = {"sw": 0, "hw": 0, "cc": 0}

    @staticmethod
    def rng(ap):
        t = ap.tensor
        name = t.name
        esz = DSZ[ap.dtype]
        pat = ap.ap
        off = ap.offset
        shape = list(t.shape)
        if type(t).__name__.startswith("DRam") or len(shape) < 2 or "DRam" in type(t).__name__:
            lo = off
            hi = off + sum((c - 1) * abs(s) for s, c in pat) + 1
            return (name, 0, 1, lo * esz, hi * esz)
        F = 1
        for s in shape[1:]:
            F *= s
        p0 = off // F
        c0 = off % F
        pcount = pat[0][1]
        hi = c0 + sum((c - 1) * abs(s) for s, c in pat[1:]) + 1
        if name == "ps":
            lo_b = (c0 * esz) // 2048
            hi_b = (hi * esz + 2047) // 2048
            return (name, 0, 128, lo_b * 2048, hi_b * 2048)
        return (name, p0, p0 + pcount, c0 * esz, hi * esz)

    def _deps(self, reads, writes):
        toks = []
        for ap in reads:
            n, p0, p1, lo, hi = self.rng(ap)
            a = self.acc.setdefault(n, {"w": [], "r": []})
            for (q0, q1, l, h, tok) in a["w"]:
                if q0 < p1 and p0 < q1 and l < hi and lo < h:
                    toks.append(tok)
        for ap in writes:
            n, p0, p1, lo, hi = self.rng(ap)
            a = self.acc.setdefault(n, {"w": [], "r": []})
            for (q0, q1, l, h, tok) in a["w"]:
                if q0 < p1 and p0 < q1 and l < hi and lo < h:
                    toks.append(("W",) + tok)
            for (q0, q1, l, h, tok) in a["r"]:
                if q0 < p1 and p0 < q1 and l < hi and lo < h:
                    toks.append(("W",) + tok)
        return toks

    def _record(self, reads, writes, tok):
        for ap in writes:
            n, p0, p1, lo, hi = self.rng(ap)
            a = self.acc.setdefault(n, {"w": [], "r": []})
            a["w"] = [e for e in a["w"] if not (p0 <= e[0] and e[1] <= p1 and lo <= e[2] and e[3] <= hi)]
            a["r"] = [e for e in a["r"] if not (p0 <= e[0] and e[1] <= p1 and lo <= e[2] and e[3] <= hi)]
            a["w"].append((p0, p1, lo, hi, tok))
        for ap in reads:
            n, p0, p1, lo, hi = self.rng(ap)
            a = self.acc.setdefault(n, {"w": [], "r": []})
            a["r"] = [e for e in a["r"] if not (e[4][0] == tok[0] and tok[0] != "dma" and p0 <= e[0]
                                                and e[1] <= p1 and lo <= e[2] and e[3] <= hi)]
            a["r"].append((p0, p1, lo, hi, tok))

    def add(self, eng, fn, reads=(), writes=(), dma=False, extra_waits=(), cc=False):
        ps_reads = [ap for ap in reads if ap.tensor.name == "ps"]
        if ps_reads:
            reads = [ap for ap in reads if ap.tensor.name != "ps"]
            writes = list(writes) + ps_reads
        toks = self._deps(reads, writes) + list(extra_waits)
        waits = []
        wd = self.waited[eng]
        for tok in toks:
            war = False
            if tok[0] == "W":
                war = True
                tok = tok[1:]
            if tok[0] == "dma":
                key, val = ("dma", tok[1]), tok[2]
            else:
                key, val = tok[0], tok[1]
                if key == eng:
                    if eng in ("pe", "sp"):
                        continue
            if wd.get(key, 0) >= val:
                continue
            wd[key] = val
            waits.append((key, val))
        dsem = None
        if dma:
            pk_ = "cc" if cc else ("sw" if eng == "pool" else "hw")
            pool_ = self.dma_pools[pk_]
            dsem = pool_[self.dma_rr[pk_] % len(pool_)]
            self.dma_rr[pk_] += 1
            prev = self.dma_val[dsem]
            key = ("dma", dsem)
            if prev > 0 and wd.get(key, 0) < prev:
                wd[key] = prev
                waits.append((key, prev))
            inc_ = 1 if cc else 16
            self.dma_val[dsem] = prev + inc_
            tok = ("dma", dsem, prev + inc_)
        else:
            self.count[eng] += 1
            tok = (eng, self.count[eng])
        best = {}
        for k, v in waits:
            best[k] = max(best.get(k, 0), v)
        self.ops[eng].append(_Op(fn, list(best.items()), dma, dsem, 1 if cc else 16))
        self._record(reads, writes, tok)
        return tok

    def barrier_tokens(self):
        toks = [(e, self.count[e]) for e in ENGS if self.count[e] > 0]
        toks += [("dma", i, v) for i, v in enumerate(self.dma_val) if v > 0]
        return toks

    def emit(self, es, final_waits):
        nc = self.nc
        sems = {e: es.enter_context(nc.semaphore("s_" + e)) for e in ENGS}
        dsems = [es.enter_context(nc.semaphore("d%d" % i)) for i in range(self.n_dma)]
        block = es.enter_context(nc.Block())

        def semobj(key):
            return dsems[key[1]] if isinstance(key, tuple) else sems[key]

        for e in ENGS:
            deco = getattr(block, ENG_ATTR[e])

            def body(eng, e=e):
                for op in self.ops[e]:
                    for key, val in op.waits:
                        eng.wait_ge(semobj(key), val)
                    ins = op.fn(eng)
                    if op.dma:
                        ins.then_inc(dsems[op.dsem], op.inc)
                    else:
                        ins.then_inc(sems[e], 1)
                if e == "sp":
                    for tok in final_waits:
                        if tok[0] == "dma":
                            eng.wait_ge(dsems[tok[1]], tok[2])
                        else:
                            eng.wait_ge(sems[tok[0]], tok[1])

            deco(body)


NCORES = 8
BATCH, SEQ, D = 2, 8192, 1024
NT = 2048
TT = 512
NTT = NT // TT
DC = D // 128
FF = 2816
FC = FF // 128
FH = 11
HALO = 16
EPS = 1e-6
NH, HD = 16, 64
GROUP_HEADS = ((0, 6), (6, 11), (11, 16))
DIL = (1, 4, 16)
V_GMIX0, V_GFFN0, V_PSC, V_GMIX1, V_GFFN1, V_GFIN = 0, 8, 16, 24, 32, 40

ARENA_BYTES = 212480


class Arena:
    def __init__(self, nc, es, name="arena", nbytes=ARENA_BYTES):
        self.t = es.enter_context(nc.sbuf_tensor(name, [128, nbytes // 2], BF16))
        self.nbytes = nbytes

    def view(self, off, shape, dtype):
        n = 1
        for s in shape:
            n *= s
        esz = DSZ[dtype]
        assert off % 4 == 0 and off + n * esz <= self.nbytes, (off, shape, dtype)
        ap = self.t[:, off // 2: off // 2 + n * esz // 2]
        if dtype != BF16:
            ap = ap.bitcast(dtype)
        if len(shape) == 2:
            ap = ap.rearrange("p (a b) -> p a b", a=shape[0])
        elif len(shape) == 3:
            ap = ap.rearrange("p (a b c) -> p a b c", a=shape[0], b=shape[1])
        return ap


class Ctx:
    def __init__(self, nc, es):
        self.nc = nc
        self.es = es
        self.S = Sched(nc)
        self.ps = es.enter_context(nc.psum_tensor("ps", [128, 8, 512], F32))
        self.bank = 0

    def psum(self):
        b = self.bank
        self.bank = (self.bank + 1) % 8
        return self.ps[:, b, :]

    def mm(self, out, pairs):
        pairs = list(pairs)
        n = len(pairs)

        def fn(e):
            ins = None
            for i, (l, r) in enumerate(pairs):
                ins = e.matmul(out, lhsT=l, rhs=r, start=(i == 0), stop=(i == n - 1))
            return ins
        reads = [p[0] for p in pairs] + [p[1] for p in pairs]
        return self.S.add("pe", fn, reads=reads, writes=[out])

    def dma(self, q, out, in_):
        return self.S.add(q, lambda e: e.dma_start(out=out, in_=in_), reads=[in_], writes=[out], dma=True)

    def act(self, out, in_, func, reads=None, **kw):
        rd = [in_] + [v for v in kw.values() if not isinstance(v, (int, float))]
        return self.S.add("act", lambda e: e.activation(out=out, in_=in_, func=func, **kw), reads=rd, writes=[out])

    def tt(self, eng, out, in0, in1, op):
        return self.S.add(eng, lambda e: e.tensor_tensor(out=out, in0=in0, in1=in1, op=op),
                          reads=[in0, in1], writes=[out])

    def stt(self, eng, out, in0, scalar, in1, op0, op1):
        rd = [in0, in1] + ([] if isinstance(scalar, (int, float)) else [scalar])
        return self.S.add(eng, lambda e: e.scalar_tensor_tensor(out=out, in0=in0, scalar=scalar, in1=in1,
                                                                op0=op0, op1=op1), reads=rd, writes=[out])

    def ts(self, eng, out, in0, scalar1, op0, scalar2=None, op1=None):
        rd = [in0] + [s for s in (scalar1, scalar2) if s is not None and not isinstance(s, (int, float))]
        if op1 is None:
            return self.S.add(eng, lambda e: e.tensor_scalar(out=out, in0=in0, scalar1=scalar1, scalar2=None,
                                                             op0=op0), reads=rd, writes=[out])
        return self.S.add(eng, lambda e: e.tensor_scalar(out=out, in0=in0, scalar1=scalar1, scalar2=scalar2,
                                                         op0=op0, op1=op1), reads=rd, writes=[out])

    def copy(self, eng, out, in_):
        return self.S.add(eng, lambda e: e.tensor_copy(out=out, in_=in_), reads=[in_], writes=[out])

    def recip(self, out, in_):
        return self.S.add("dve", lambda e: e.reciprocal(out=out, in_=in_), reads=[in_], writes=[out])

    def memset(self, eng, out, val):
        return self.S.add(eng, lambda e: e.memset(out, val), writes=[out])


def rmsnorm_tile(C, xsrc, hdst, n, gcol, ones, sq, rt, rstd):
    C.act(sq[:, :, :n], xsrc, AF.Square)
    ps = C.psum()
    C.mm(ps[:, :n], [(ones, sq[:, c, :n]) for c in range(DC)])
    C.act(rt[:, :n], ps[:, :n], AF.Sqrt, bias=EPS, scale=1.0 / D)
    C.recip(rstd[:, :n], rt[:, :n])
    for c in range(DC):
        C.stt("dve", hdst[:, c, :], xsrc[:, c, :], gcol(c), rstd[:, :n], ALU.mult, ALU.mult)


def tile_major_residual(C, xT, pairs_fn, after_tile=None):
    for t in range(NTT):
        tsl = slice(t * TT, (t + 1) * TT)
        for m in range(DC):
            ps = C.psum()
            C.mm(ps, pairs_fn(m, tsl))
            C.tt("dve", xT[:, m, tsl], xT[:, m, tsl], ps, ALU.add)
        if after_tile is not None and t >= 1:
            after_tile(t - 1)
    if after_tile is not None:
        after_tile(NTT - 1)


def ffn_layer(C, A, PB, xT, hT, wg, wu, wd, after_tile=None):
    aT = A.view(PB, [FH, NT], BF16)
    wdh = A.view(PB + 45056, [FH, D], BF16)
    wgu = [(A.view(PB + 67584 + i * 8192, [DC, 256], BF16),
            A.view(PB + 67584 + i * 8192 + 4096, [DC, 256], BF16)) for i in range(2)]
    stmp = [A.view(PB + 83968 + i * 2048, [TT], F32) for i in range(2)]
    wg_v = wg.rearrange("(c p) n -> p c n", p=128)
    wu_v = wu.rearrange("(c p) n -> p c n", p=128)
    wd_v = wd.rearrange("(j p) n -> p j n", p=128)
    gi = 0
    si = 0
    for half in range(2):
        j0 = half * FH
        C.dma("pool", wdh[:, 0:6, :], wd_v[:, j0:j0 + 6, :])
        C.dma("pool", wdh[:, 6:FH, :], wd_v[:, j0 + 6:j0 + FH, :])
        jl = 0
        while jl < FH:
            nj = min(2, FH - jl)
            bg, bu = wgu[gi % 2]
            gi += 1
            c0 = (j0 + jl) * 128
            C.dma("pool", bg[:, :, 0:nj * 128], wg_v[:, :, c0:c0 + nj * 128])
            C.dma("pool", bu[:, :, 0:nj * 128], wu_v[:, :, c0:c0 + nj * 128])
            for jj in range(nj):
                for t in range(NTT):
                    tsl = slice(t * TT, (t + 1) * TT)
                    pg = C.psum()
                    C.mm(pg, [(bg[:, k, jj * 128:(jj + 1) * 128], hT[:, k, tsl]) for k in range(DC)])
                    pu = C.psum()
                    C.mm(pu, [(bu[:, k, jj * 128:(jj + 1) * 128], hT[:, k, tsl]) for k in range(DC)])
                    st = stmp[si % 2]
                    si += 1
                    C.act(st, pg, AF.Silu)
                    C.tt("dve", aT[:, jl + jj, tsl], st, pu, ALU.mult)
            jl += nj
        tile_major_residual(C, xT, lambda m, tsl: [(wdh[:, j, m * 128:(m + 1) * 128], aT[:, j, tsl])
                                                   for j in range(FH)],
                            after_tile if half == 1 else None)


PB = 118784


def common_views(A):
    v = {}
    v["xT"] = A.view(0, [DC, NT], F32)
    v["hT"] = A.view(65536, [DC, NT], BF16)
    v["sq"] = A.view(98304, [DC, TT], BF16)
    v["rt"] = A.view(106496, [TT], F32)
    v["rstd"] = A.view(108544, [TT], F32)
    v["vecs"] = A.view(110592, [48], F32)
    v["ones"] = A.view(110784, [128], BF16)
    v["invc"] = A.view(111040, [DC, HALO], F32)
    v["xh"] = A.view(111552, [DC, HALO], F32)
    v["hh"] = A.view(112064, [DC, HALO], BF16)
    v["t16"] = A.view(112320, [HALO], F32)
    v["hvt"] = A.view(112384, [16], F32)
    return v


def pool_layer(C, A, V, w_in, w_grp, w_out, after_tile=None):
    xT, hT, hh, invc, vecs = V["xT"], V["hT"], V["hh"], V["invc"], V["vecs"]
    pT = A.view(PB, [DC, NT], BF16)
    win = A.view(PB + 32768, [DC, D], BF16)
    wgr = A.view(PB + 49152, [4, 2, 256], BF16)
    NU = NT + HALO
    ub = [A.view(PB + 53248 + i * 8256, [NU], F32) for i in range(2)]
    wa = A.view(PB + 69760, [NU], F32)
    wb = A.view(PB + 69760 + 8256, [NU], F32)
    wout = A.view(PB + 53248, [DC, D], BF16)
    zT = hT
    C.dma("pool", win[:, 0:4, :], w_in.rearrange("(c p) n -> p c n", p=128)[:, 0:4, :])
    C.dma("pool", win[:, 4:8, :], w_in.rearrange("(c p) n -> p c n", p=128)[:, 4:8, :])
    C.dma("pool", wgr, w_grp.rearrange("g (c p) n -> p g c n", p=128))
    for m in range(DC):
        u = ub[m % 2]
        msl = slice(m * 128, (m + 1) * 128)
        ps = C.psum()
        C.mm(ps[:, :HALO], [(win[:, k, msl], hh[:, k, :]) for k in range(DC)])
        C.act(u[:, 0:HALO], ps[:, :HALO], AF.Copy)
        for t in range(NTT):
            tsl = slice(t * TT, (t + 1) * TT)
            ps = C.psum()
            C.mm(ps, [(win[:, k, msl], hT[:, k, tsl]) for k in range(DC)])
            C.act(u[:, HALO + t * TT:HALO + (t + 1) * TT], ps, AF.Copy)
        g = m // 2
        L = g + 1
        w = 2 ** L
        src = u
        bufs = [wa, wb]
        for l in range(1, L + 1):
            sh = 2 ** (l - 1)
            lo = 2 ** l - 1
            dst = bufs[(l - 1) % 2]
            C.tt("dve", dst[:, lo:NU], src[:, lo:NU], src[:, lo - sh:NU - sh], ALU.add)
            src = dst
        C.stt("dve", pT[:, m, :], src[:, HALO:NU], 1.0 / w, u[:, HALO:NU], ALU.mult, ALU.subtract)
        t16 = V["t16"]
        C.tt("dve", t16, src[:, HALO:2 * HALO], invc[:, m, :], ALU.mult)
        C.tt("dve", pT[:, m, 0:HALO], t16, u[:, HALO:2 * HALO], ALU.subtract)
    C.dma("pool", wout[:, 0:4, :], w_out.rearrange("(c p) n -> p c n", p=128)[:, 0:4, :])
    C.dma("pool", wout[:, 4:8, :], w_out.rearrange("(c p) n -> p c n", p=128)[:, 4:8, :])
    for m in range(DC):
        g, mo = m // 2, m % 2
        for t in range(NTT):
            tsl = slice(t * TT, (t + 1) * TT)
            ps = C.psum()
            C.mm(ps, [(wgr[:, g, ki, mo * 128:(mo + 1) * 128], pT[:, 2 * g + ki, tsl]) for ki in range(2)])
            C.act(zT[:, m, tsl], ps, AF.Copy, scale=vecs[:, V_PSC + m:V_PSC + m + 1])
    tile_major_residual(C, xT, lambda m, tsl: [(wout[:, k, m * 128:(m + 1) * 128], zT[:, k, tsl])
                                               for k in range(DC)], after_tile)


def build_layer0():
    nc = bass.Bass("TRN2", target_bir_lowering=False)
    dt_in = lambda n, s: nc.dram_tensor(n, s, F32, kind="ExternalInput").ap()
    xT_d = dt_in("xT", [D, NT])
    xh_d = dt_in("xh", [D, HALO])
    vecs_d = dt_in("vecs", [128, 48])
    invc_d = dt_in("invc", [128, DC * HALO])
    w_in = dt_in("w_in", [D, D])
    w_grp = dt_in("w_grp", [4, 256, 256])
    w_out = dt_in("w_out", [D, D])
    wg = dt_in("wg0", [D, FF])
    wu = dt_in("wu0", [D, FF])
    wd = dt_in("wd0", [FF, D])
    x1_d = nc.dram_tensor("x1T", [D, NT], F32, kind="ExternalOutput").ap()
    with ExitStack() as es:
        A = Arena(nc, es)
        C = Ctx(nc, es)
        V = common_views(A)
        layer0_body(C, A, V, xT_d, xh_d, vecs_d, invc_d, w_in, w_grp, w_out, wg, wu, wd)
        toks = []
        x1v = x1_d.rearrange("(c p) t -> p c t", p=128)
        for t in range(NTT):
            tsl = slice(t * TT, (t + 1) * TT)
            toks.append(C.dma("sp", x1v[:, :, tsl], V["xT"][:, :, tsl]))
        C.S.emit(es, toks)
    return nc


def layer0_body(C, A, V, xT_d, xh_d, vecs_d, invc_d, w_in, w_grp, w_out, wg, wu, wd, next_norm=None):
    xT, hT = V["xT"], V["hT"]
    C.dma("sp", V["vecs"], vecs_d)
    C.dma("sp", V["invc"], invc_d.rearrange("p (c h) -> p c h", c=DC))
    C.dma("sp", V["xh"], xh_d.rearrange("(c p) h -> p c h", p=128))
    xv = xT_d.rearrange("(c p) t -> p c t", p=128)
    for t in range(NTT):
        tsl = slice(t * TT, (t + 1) * TT)
        C.dma("sp", xT[:, :, tsl], xv[:, :, tsl])
    C.memset("dve", V["ones"], 1.0)
    gcol = lambda base: (lambda c: V["vecs"][:, base + c:base + c + 1])
    rmsnorm_tile(C, V["xh"], V["hh"], HALO, gcol(V_GMIX0), V["ones"], V["sq"], V["rt"], V["rstd"])
    for t in range(NTT):
        tsl = slice(t * TT, (t + 1) * TT)
        rmsnorm_tile(C, xT[:, :, tsl], hT[:, :, tsl], TT, gcol(V_GMIX0), V["ones"], V["sq"], V["rt"], V["rstd"])
    def norm_ffn0(t):
        tsl = slice(t * TT, (t + 1) * TT)
        rmsnorm_tile(C, xT[:, :, tsl], hT[:, :, tsl], TT, gcol(V_GFFN0), V["ones"], V["sq"], V["rt"], V["rstd"])
    pool_layer(C, A, V, w_in, w_grp, w_out, after_tile=norm_ffn0)
    ffn_layer(C, A, PB, xT, hT, wg, wu, wd, after_tile=next_norm)


def _vec_layout(v):
    return np.ascontiguousarray(np.asarray(v, np.float32).reshape(DC, 128).T)


def _make_vecs(norm_mix, norm_ffn, norm_final, pool_scale):
    cols = [norm_mix[0], norm_ffn[0], pool_scale[0], norm_mix[1], norm_ffn[1], norm_final]
    return np.ascontiguousarray(np.concatenate([_vec_layout(c) for c in cols], axis=1))


def _invc_table(first):
    t = np.zeros((128, DC, HALO), np.float32)
    for c in range(DC):
        w = 2 ** (c // 2 + 1)
        for i in range(HALO):
            t[:, c, i] = 1.0 / (min(i + 1, w) if first else w)
    return np.ascontiguousarray(t.reshape(128, DC * HALO))


def _core_tokens(x):
    xs, hs = [], []
    cps = SEQ // NT
    for core in range(NCORES):
        b, c = core // cps, core % cps
        xs.append(np.ascontiguousarray(x[b, c * NT:(c + 1) * NT, :].T))
        if c == 0:
            hs.append(np.zeros((D, HALO), np.float32))
        else:
            hs.append(np.ascontiguousarray(x[b, c * NT - HALO:c * NT, :].T))
    return xs, hs


def dsl(start, d, n=128):
    return slice(start, start + d * (n - 1) + 1, d)


def head_info(h):
    for g, (h0, h1) in enumerate(GROUP_HEADS):
        if h0 <= h < h1:
            return g, h0, h1 - h0
    raise ValueError(h)


def layer1_views(A):
    L = {}
    L["kT"] = A.view(0, [DC, NT], BF16)
    L["xs"] = [A.view(i * 16384, [DC, TT], F32) for i in range(2)]
    L["khb"] = A.view(32768, [3, NT], BF16)
    L["khs"] = A.view(45056, [5, TT], BF16)
    L["vh"] = [A.view(50176, [1, 6, HD], BF16), A.view(50176 + 768, [4, 5, HD], BF16),
               A.view(50176 + 768 + 2560, [16, 5, HD], BF16)]
    L["qT"] = A.view(PB, [DC, NT], BF16)
    L["hTh"] = L["qT"]
    vo = PB + 32768
    L["v"] = [A.view(vo, [16, 6, HD], BF16), A.view(vo + 12288, [16, 5, HD], BF16),
              A.view(vo + 12288 + 10240, [16, 5, HD], BF16)]
    L["cos"] = A.view(vo, [2 * NT], F32)
    L["sin"] = A.view(vo + 16384, [2 * NT], F32)
    wo_ = PB + 65536
    L["wqk"] = [A.view(wo_ + i * 2048, [DC, 128], BF16) for i in range(3)]
    L["wv"] = A.view(wo_ + 6144, [DC, 384], BF16)
    L["R"] = A.view(wo_ + 12288, [128], BF16)
    L["wst"] = [A.view(wo_ + 12544 + i * 4096, [DC, 128], F32) for i in range(2)]
    L["Dg"] = A.view(wo_, [3, NT], F32)
    L["wo"] = A.view(PB, [DC, D], BF16)
    b = 98304
    L["qb"] = [A.view(b + i * 1024, [TT], BF16) for i in range(2)]
    L["t1"] = [A.view(b + 2048 + i * 2048, [TT], F32) for i in range(2)]
    L["t2"] = [A.view(b + 6144 + i * 2048, [TT], F32) for i in range(2)]
    L["pT"] = [A.view(b + i * 2048, [1024], BF16) for i in range(2)]
    L["rden"] = [A.view(b + 4096 + i * 2048, [TT], F32) for i in range(2)]
    L["masks"] = A.view(112448, [3, 1024], BF16)
    return L


def build_layer1_body(C, A, V, x1_own_d, x1_halo_d, tab_d, mask_d, R_d, hv_d,
                      w_qkv, w_o, wg, wu, wd, out_d, own_in_sbuf=False, gather=None, own_normed=False):
    S = C.S
    L = layer1_views(A)
    xT, hT, vecs, ones = V["xT"], V["hT"], V["vecs"], V["ones"]
    import os as _os
    _stop = _os.environ.get("KSTOP", "")

    def stop_here(tag, src_f32):
        if _stop != tag:
            return None
        ov_ = out_d.rearrange("(c p) t -> p c t", p=128)
        return [C.dma("sp", ov_[:, :, 0:TT], src_f32)]
    gcol = lambda base: (lambda c: vecs[:, base + c:base + c + 1])
    nrm = lambda xsrc, hdst, n, base: rmsnorm_tile(C, xsrc, hdst, n, gcol(base), ones, V["sq"], V["rt"], V["rstd"])
    x1o = x1_own_d.rearrange("(c p) t -> p c t", p=128)
    x1h = None if gather is not None else x1_halo_d.rearrange("(c p) t -> p c t", p=128)
    wq_v = w_qkv.rearrange("(c p) n -> p c n", p=128)

    masks, Rm, hv = L["masks"], L["R"], V["hvt"]
    C.dma("sp", hv[:, 0:1], hv_d)
    C.dma("pool", Rm, R_d)
    C.dma("pool", masks[:, 0, :], mask_d)
    for i in range(4):
        C.dma("sp", L["cos"][:, i * 1024:(i + 1) * 1024], tab_d[0][:, i * 1024:(i + 1) * 1024])
        C.dma("sp", L["sin"][:, i * 1024:(i + 1) * 1024], tab_d[1][:, i * 1024:(i + 1) * 1024])
    C.copy("pool", masks[:, 1, :], masks[:, 0, :])
    C.ts("dve", masks[:, 1, 0:128], masks[:, 0, 0:128], hv[:, 0:1], ALU.mult)
    C.copy("pool", masks[:, 2, :], masks[:, 0, :])
    m2v = masks[:, 2, :].rearrange("p (e s) -> p e s", e=4)
    m0v = masks[:, 0, :].rearrange("p (e s) -> p e s", e=4)
    C.ts("dve", m2v[:, :, 0:128], m0v[:, :, 0:128], hv[:, 0:1], ALU.mult)

    if not own_in_sbuf:
        for t in range(NTT):
            tsl = slice(t * TT, (t + 1) * TT)
            C.dma("sp", xT[:, :, tsl], x1o[:, :, tsl])
    if not own_normed:
        for t in range(NTT):
            tsl = slice(t * TT, (t + 1) * TT)
            nrm(xT[:, :, tsl], hT[:, :, tsl], TT, V_GMIX1)

    hTh = L["hTh"]
    cnt = {"w": 0, "r": 0}

    pend = []

    def proj_rope(wcol0, src, s_tsl, tab_off, dst):
        wbuf = L["wqk"][(cnt["w"] - 1) % 3]
        i = cnt["r"] % 2
        cnt["r"] += 1
        pk = C.psum()
        C.mm(pk, [(wbuf[:, k, :], src[:, k, s_tsl]) for k in range(DC)])
        qb, t1, t2 = L["qb"][i], L["t1"][i], L["t2"][i]
        C.act(qb, pk, AF.Copy)

        def stage_b():
            pr = C.psum()
            C.mm(pr, [(Rm, qb)])
            C.tt("dve", t1, pk, L["cos"][:, tab_off:tab_off + TT], ALU.mult)
            C.tt("dve", t2, pr, L["sin"][:, tab_off:tab_off + TT], ALU.mult)
            C.tt("dve", dst, t1, t2, ALU.add)
        pend.append(stage_b)
        if len(pend) > 1:
            pend.pop(0)()

    def rope_flush():
        while pend:
            pend.pop(0)()

    cnt["s"] = 0

    def stage_cast(dst_bf, col0, ncol):
        st = L["wst"][cnt["s"] % 2]
        cnt["s"] += 1
        C.dma("sp", st[:, :, 0:ncol], wq_v[:, :, col0:col0 + ncol])
        C.act(dst_bf, st[:, :, 0:ncol], AF.Copy)

    def load_wqk(col0):
        wbuf = L["wqk"][cnt["w"] % 3]
        cnt["w"] += 1
        stage_cast(wbuf, col0, 128)

    def load_wv(g):
        h0, h1 = GROUP_HEADS[g]
        ncol = (h1 - h0) * HD
        c0 = 0
        while c0 < ncol:
            n = min(128, ncol - c0)
            stage_cast(L["wv"][:, :, c0:c0 + n], 2 * D + h0 * HD + c0, n)
            c0 += n

    def v_blocks(src, dstv, g, blocks):
        h0, nh = GROUP_HEADS[g][0], GROUP_HEADS[g][1] - GROUP_HEADS[g][0]
        ncol = nh * HD
        for bi, sl in blocks:
            ps = C.psum()
            C.mm(ps[:, :ncol], [(src[:, k, sl], L["wv"][:, k, 0:ncol]) for k in range(DC)])
            C.act(dstv[:, bi, :, :], ps[:, :ncol].rearrange("p (h d) -> p h d", h=nh), AF.Copy)

    def halo_blocks(g):
        d = DIL[g]
        return [(r, dsl(NT - 128 * d + r, d)) for r in range(d)]

    def own_blocks(g):
        d = DIL[g]
        nb = 16 // d
        return [(r * nb + b, dsl(r + d * 128 * b, d)) for r in range(d) for b in range(nb)]

    def do_k_own():
        for c in range(DC):
            load_wqk(D + c * 128)
            for t in range(NTT):
                tsl = slice(t * TT, (t + 1) * TT)
                proj_rope(D + c * 128, hT, tsl, NT + t * TT, L["kT"][:, c, tsl])
        rope_flush()

    def do_halo():
        if gather is None:
            for t in range(NTT):
                tsl = slice(t * TT, (t + 1) * TT)
                xs = L["xs"][t % 2]
                C.dma("sp", xs, x1h[:, :, tsl])
                nrm(xs, hTh[:, :, tsl], TT, V_GMIX1)
        else:
            pass
        for c in range(DC):
            load_wqk(D + c * 128)
            tiles = range(NTT) if c >= 5 else [NTT - 1]
            for t in tiles:
                tsl = slice(t * TT, (t + 1) * TT)
                dst = L["khb"][:, c - 5, tsl] if c >= 5 else L["khs"][:, c, :]
                proj_rope(D + c * 128, hTh, tsl, t * TT, dst)
        rope_flush()
        for g in range(3):
            load_wv(g)
            v_blocks(hTh, L["vh"][g], g, halo_blocks(g))

    if gather is not None:
        hx_in, hx_all = gather
        x1s_v = x1_own_d.rearrange("(c p) t -> p c t", p=128)
        hin_v = hx_in.ap().rearrange("(c p) t -> p c t", p=128)
        for t in range(NTT):
            tsl = slice(t * TT, (t + 1) * TT)
            C.dma("sp", hin_v[:, :, tsl], hT[:, :, tsl])
            C.dma("sp", x1s_v[:, :, tsl], xT[:, :, tsl])
        S.add("pool", lambda e: e.collective_compute("AllGather", ALU.bypass, replica_groups=[list(range(NCORES))],
                                                     ins=[hx_in.ap().opt()], outs=[hx_all.ap().opt()]),
              reads=[hx_in.ap()], writes=[hx_all.ap()], dma=True, cc=True)
        hall_v = hx_all.ap().rearrange("(r c p) t -> r p c t", r=NCORES, p=128)
        for t in range(NTT):
            tsl = slice(t * TT, (t + 1) * TT)

            def rd(e, tsl=tsl):
                prev = (e.partition_id() + (NCORES - 1)) % NCORES
                return e.dma_start(out=hTh[:, :, tsl],
                                   in_=hall_v[bass.ds(prev, 1), :, :, tsl].rearrange("r p c t -> (r p) c t"))
            S.add("pool", rd, reads=[hx_all.ap()], writes=[hTh[:, :, tsl]], dma=True)
        do_k_own()
        do_halo()
    else:
        do_halo()
        do_k_own()

    for c in range(DC):
        load_wqk(c * 128)
        for t in range(NTT):
            tsl = slice(t * TT, (t + 1) * TT)
            proj_rope(c * 128, hT, tsl, NT + t * TT, L["qT"][:, c, tsl])
    rope_flush()
    for g in range(3):
        load_wv(g)
        v_blocks(hT, L["v"][g], g, own_blocks(g))

    _r = stop_here("qkv", L["xs"][1])
    if _r:
        return _r
    oT = hT
    Dg = L["Dg"]
    C.memset("pool", Dg[0:64, :, :], 0.0)
    psS = [C.ps[:, 0:2, :].rearrange("p a b -> p (a b)"), C.ps[:, 2:4, :].rearrange("p a b -> p (a b)")]
    psO = [C.ps[:, 4:6, :].rearrange("p a b -> p (a b)"), C.ps[:, 6:8, :].rearrange("p a b -> p (a b)")]
    bi = 0
    for h in range(NH):
        g, h0, nh = head_info(h)
        d = DIL[g]
        nb = 16 // d
        c, ro, hl = h // 2, (h % 2) * HD, h - h0
        prow = slice(ro, ro + HD)
        entries = [(r, b) for r in range(d) for b in range(nb)]
        for q in range(4):
            ents = entries[4 * q:4 * q + 4]
            pS, pO, pT, rden = psS[bi % 2], psO[bi % 2], L["pT"][bi % 2], L["rden"][bi % 2]
            bi += 1
            n_halo = sum(1 for (r, b) in ents if b == 0)
            mk = masks[:, 0, :] if n_halo == 0 else (masks[:, 1, :] if n_halo == 1 else masks[:, 2, :])
            assert n_halo in (0, 4) or (n_halo == 1 and ents[0][1] == 0)
            kv = []
            for e, (r, b) in enumerate(ents):
                start = r + d * 128 * b
                qsl = dsl(start, d)
                qa = L["qT"][prow, c, qsl]
                kcur = L["kT"][prow, c, qsl]
                vcur = L["v"][g][:, r * nb + b, hl, :]
                if b >= 1:
                    kprev = L["kT"][prow, c, dsl(start - d * 128, d)]
                    vprev = L["v"][g][:, r * nb + b - 1, hl, :]
                else:
                    hsl = dsl(NT - 128 * d + r, d)
                    if c >= 5:
                        kprev = L["khb"][prow, c - 5, hsl]
                    else:
                        kprev = L["khs"][prow, c, dsl(hsl.start - (NT - TT), d)]
                    vprev = L["vh"][g][:, r, hl, :]
                C.mm(pS[:, e * 256:e * 256 + 128], [(kprev, qa)])
                C.mm(pS[:, e * 256 + 128:e * 256 + 256], [(kcur, qa)])
                kv.append((vprev, vcur))
            C.act(pT, pS, AF.Exp, scale=HD ** -0.5)
            C.tt("pool", pT, pT, mk, ALU.mult)
            for e, (vprev, vcur) in enumerate(kv):
                pp, pc = pT[:, e * 256:e * 256 + 128], pT[:, e * 256 + 128:e * 256 + 256]
                C.mm(pO[0:64, e * 256:e * 256 + 128], [(vprev, pp), (vcur, pc)])
                C.mm(pO[0:64, e * 256 + 128:e * 256 + 256], [(ones[:, 0:64], pp), (ones[:, 0:64], pc)])
            pOv = pO[0:64, :].rearrange("p (e s) -> p e s", e=4)
            num, den = pOv[:, :, 0:128], pOv[:, :, 128:256]
            rdv = rden[0:64, :].rearrange("p (e s) -> p e s", e=4)
            if d == 1:
                tv = lambda ap3: ap3[:, q * TT:(q + 1) * TT].rearrange("p (e s) -> p e s", e=4)
            elif d == 4:
                r0 = ents[0][0]
                tv = lambda ap3: ap3[:, dsl(r0, 4, 512)].rearrange("p (e s) -> p e s", e=4)
            else:
                r0 = ents[0][0]
                tv = lambda ap3: ap3[:, :].rearrange("p (s r) -> p r s", r=16)[:, r0:r0 + 4, :]
            C.act(rdv, den, AF.Ln)
            C.act(rdv, rdv, AF.Exp, scale=-1.0)
            C.tt("dve", tv(oT[prow, c, :]), num, rdv, ALU.mult)
            dgv = tv(Dg[0:64, g, :])
            C.tt("dve", dgv, den, dgv, ALU.add)

    _r = stop_here("attn", L["cos"][:, 0:4096].rearrange("p (c t) -> p c t", c=DC))
    if _r:
        return _r
    C.dma("pool", L["wo"][:, 0:4, :], w_o.rearrange("(c p) n -> p c n", p=128)[:, 0:4, :])
    C.dma("pool", L["wo"][:, 4:8, :], w_o.rearrange("(c p) n -> p c n", p=128)[:, 4:8, :])
    for t in range(NTT):
        tsl = slice(t * TT, (t + 1) * TT)
        C.dma("sp", xT[:, :, tsl], x1o[:, :, tsl])
    tot = A.view(98304, [NT], F32)
    ng = [GROUP_HEADS[g][1] - GROUP_HEADS[g][0] for g in range(3)]
    C.ts("dve", tot[0:64, :], Dg[0:64, 0, :], 1.0 / ng[0], ALU.mult)
    C.stt("dve", tot[0:64, :], Dg[0:64, 1, :], 1.0 / ng[1], tot[0:64, :], ALU.mult, ALU.add)
    C.stt("dve", tot[0:64, :], Dg[0:64, 2, :], 1.0 / ng[2], tot[0:64, :], ALU.mult, ALU.add)
    C.recip(tot[0:64, :], tot[0:64, :])
    for g in range(3):
        C.stt("dve", Dg[0:64, g, :], Dg[0:64, g, :], 3.0 / ng[g], tot[0:64, :], ALU.mult, ALU.mult)
    C.act(Dg[64:128, :, :], Dg[0:64, :, :], AF.Copy)
    for c in range(DC):
        ga, gb = head_info(2 * c)[0], head_info(2 * c + 1)[0]
        if ga == gb:
            C.tt("dve", oT[:, c, :], oT[:, c, :], Dg[:, ga, :], ALU.mult)
        else:
            C.tt("dve", oT[0:64, c, :], oT[0:64, c, :], Dg[0:64, ga, :], ALU.mult)
            C.tt("dve", oT[64:128, c, :], oT[64:128, c, :], Dg[64:128, gb, :], ALU.mult)
    def norm_ffn1(t):
        tsl = slice(t * TT, (t + 1) * TT)
        nrm(xT[:, :, tsl], hT[:, :, tsl], TT, V_GFFN1)
    tile_major_residual(C, xT, lambda m, tsl: [(L["wo"][:, k, m * 128:(m + 1) * 128], oT[:, k, tsl])
                                               for k in range(DC)], norm_ffn1)

    _r = stop_here("oproj", xT[:, :, 0:TT])
    if _r:
        return _r
    ob = [A.view(65536 + i * 16384, [DC, TT], F32) for i in range(2)]
    ov = out_d.rearrange("(c p) t -> p c t", p=128)
    toks = []

    def norm_final(t):
        tsl = slice(t * TT, (t + 1) * TT)
        rmsnorm_tile(C, xT[:, :, tsl], ob[t % 2], TT, gcol(V_GFIN), ones, V["sq"], V["rt"], V["rstd"])
        toks.append(C.dma("sp", ov[:, :, tsl], ob[t % 2]))
    ffn_layer(C, A, PB, xT, hT, wg, wu, wd, after_tile=norm_final)
    return toks


def build_layer1():
    nc = bass.Bass("TRN2", target_bir_lowering=False)
    dt_in = lambda n, s: nc.dram_tensor(n, s, F32, kind="ExternalInput").ap()
    x1o = dt_in("x1T", [D, NT])
    x1h = dt_in("x1h", [D, NT])
    vecs_d = dt_in("vecs", [128, 48])
    hv_d = dt_in("hv", [128, 1])
    tab_d = dt_in("tab", [2, 128, 2 * NT])
    mask_d = dt_in("mask", [128, 1024])
    R_d = dt_in("rmat", [128, 128])
    w_qkv = dt_in("w_qkv", [D, 3 * D])
    w_o = dt_in("w_o", [D, D])
    wg = dt_in("wg1", [D, FF])
    wu = dt_in("wu1", [D, FF])
    wd = dt_in("wd1", [FF, D])
    out_d = nc.dram_tensor("outT", [D, NT], F32, kind="ExternalOutput").ap()
    with ExitStack() as es:
        A = Arena(nc, es)
        C = Ctx(nc, es)
        V = common_views(A)
        C.dma("sp", V["vecs"], vecs_d)
        C.memset("dve", V["ones"], 1.0)
        toks = build_layer1_body(C, A, V, x1o, x1h, tab_d, mask_d, R_d, hv_d, w_qkv, w_o, wg, wu, wd, out_d)
        C.S.emit(es, toks)
    return nc


def build_fused():
    nc = bass.Bass("TRN2", target_bir_lowering=False)
    dt_in = lambda n, s: nc.dram_tensor(n, s, F32, kind="ExternalInput").ap()
    xT_d = dt_in("xT", [D, NT])
    xh_d = dt_in("xh", [D, HALO])
    vecs_d = dt_in("vecs", [128, 48])
    invc_d = dt_in("invc", [128, DC * HALO])
    hv_d = dt_in("hv", [128, 1])
    tab_d = dt_in("tab", [2, 128, 2 * NT])
    mask_d = dt_in("mask", [128, 1024])
    R_d = dt_in("rmat", [128, 128])
    w_in = dt_in("w_in", [D, D])
    w_grp = dt_in("w_grp", [4, 256, 256])
    w_out = dt_in("w_out", [D, D])
    wg0 = dt_in("wg0", [D, FF])
    wu0 = dt_in("wu0", [D, FF])
    wd0 = dt_in("wd0", [FF, D])
    w_qkv = dt_in("w_qkv", [D, 3 * D])
    w_o = dt_in("w_o", [D, D])
    wg1 = dt_in("wg1", [D, FF])
    wu1 = dt_in("wu1", [D, FF])
    wd1 = dt_in("wd1", [FF, D])
    out_d = nc.dram_tensor("outT", [D, NT], F32, kind="ExternalOutput").ap()
    x1s = nc.dram_tensor("x1s", [D, NT], F32)
    hx_in = nc.dram_tensor("hx_in", [D, NT], BF16)
    hx_all = nc.dram_tensor("hx_all", [NCORES * D, NT], BF16)
    with ExitStack() as es:
        A = Arena(nc, es)
        C = Ctx(nc, es)
        V = common_views(A)
        def norm_mix1(t):
            tsl = slice(t * TT, (t + 1) * TT)
            rmsnorm_tile(C, V["xT"][:, :, tsl], V["hT"][:, :, tsl], TT,
                         lambda c: V["vecs"][:, V_GMIX1 + c:V_GMIX1 + c + 1], V["ones"], V["sq"], V["rt"], V["rstd"])
        layer0_body(C, A, V, xT_d, xh_d, vecs_d, invc_d, w_in, w_grp, w_out, wg0, wu0, wd0, next_norm=norm_mix1)
        toks = build_layer1_body(C, A, V, x1s.ap(), None, tab_d, mask_d, R_d, hv_d, w_qkv, w_o, wg1, wu1, wd1,
                                 out_d, own_in_sbuf=True, gather=(hx_in, hx_all), own_normed=True)
        C.S.emit(es, toks)
    return nc


def _rope_tables(pos0):
    inv_freq = (1.0 / (10000.0 ** (np.arange(0, HD, 2, dtype=np.float32) / np.float32(HD)))).astype(np.float32)
    pos = np.arange(pos0 - NT, pos0 + NT, dtype=np.float32)
    ang = (pos[None, :] * inv_freq[:, None]).astype(np.float32)
    cos = np.cos(ang).astype(np.float32)
    sin = np.sin(ang).astype(np.float32)
    rows = np.arange(128)
    ctab = cos[rows % 32]
    sgn = np.where((rows % 64) < 32, -1.0, 1.0).astype(np.float32)
    stab = sin[rows % 32] * sgn[:, None]
    return np.ascontiguousarray(np.stack([ctab, stab]).astype(np.float32))


def _mask_table():
    k = np.arange(128)[:, None]
    q = np.arange(128)[None, :]
    prev = (q <= k).astype(np.float32)
    cur = (k <= q).astype(np.float32)
    one = np.concatenate([prev, cur], axis=1)
    return np.ascontiguousarray(np.tile(one, (1, 4)))


def _rot_matrix():
    m = np.arange(128)
    partner = (m // 64) * 64 + ((m % 64) + 32) % 64
    R = np.zeros((128, 128), np.float32)
    R[partner, m] = 1.0
    return R


_NC_CACHE = {}


def _get_nc(name, builder):
    if name not in _NC_CACHE:
        _NC_CACHE[name] = builder()
    return _NC_CACHE[name]


def kernel_unfused(x, norm_mix, norm_ffn, norm_final, pool_w_in, pool_w_group, pool_scale, pool_w_out,
                   attn_w_qkv, attn_w_out, ffn_w_gate, ffn_w_up, ffn_w_down):
    f = lambda a: np.ascontiguousarray(np.asarray(a, dtype=np.float32))
    x = f(x)
    cps = SEQ // NT
    vecs = _make_vecs(f(norm_mix), f(norm_ffn), f(norm_final), f(pool_scale))
    xs, hs = _core_tokens(x)
    nc0 = _get_nc("l0", build_layer0)
    in0 = []
    for core in range(NCORES):
        in0.append({"xT": xs[core], "xh": hs[core], "vecs": vecs, "invc": _invc_table(core % cps == 0),
                    "w_in": f(pool_w_in[0]), "w_grp": f(pool_w_group[0]), "w_out": f(pool_w_out[0]),
                    "wg0": f(ffn_w_gate[0]), "wu0": f(ffn_w_up[0]), "wd0": f(ffn_w_down[0])})
    r0 = run_bass_kernel_spmd(nc0, in0, core_ids=list(range(NCORES)))
    x1 = [r["x1T"] for r in r0.results]
    nc1 = _get_nc("l1", build_layer1)
    mask = _mask_table()
    rmat = _rot_matrix()
    in1 = []
    for core in range(NCORES):
        first = core % cps == 0
        halo = np.zeros((D, NT), np.float32) if first else x1[core - 1]
        in1.append({"x1T": x1[core], "x1h": halo, "vecs": vecs,
                    "hv": np.full((128, 1), 0.0 if first else 1.0, np.float32),
                    "tab": _rope_tables((core % cps) * NT), "mask": mask, "rmat": rmat,
                    "w_qkv": f(attn_w_qkv[0]), "w_o": f(attn_w_out[0]),
                    "wg1": f(ffn_w_gate[1]), "wu1": f(ffn_w_up[1]), "wd1": f(ffn_w_down[1])})
    r1 = run_bass_kernel_spmd(nc1, in1, core_ids=list(range(NCORES)))
    out = np.stack([r["outT"].T for r in r1.results]).reshape(BATCH, SEQ, D)
    return np.ascontiguousarray(out.astype(np.float32))


def kernel(x, norm_mix, norm_ffn, norm_final, pool_w_in, pool_w_group, pool_scale, pool_w_out,
           attn_w_qkv, attn_w_out, ffn_w_gate, ffn_w_up, ffn_w_down):
    f = lambda a: np.ascontiguousarray(np.asarray(a, dtype=np.float32))
    x = f(x)
    cps = SEQ // NT
    vecs = _make_vecs(f(norm_mix), f(norm_ffn), f(norm_final), f(pool_scale))
    xs, hs = _core_tokens(x)
    nc = _get_nc("fused", build_fused)
    mask = _mask_table()
    rmat = _rot_matrix()
    shared = {"vecs": vecs, "mask": mask, "rmat": rmat,
              "w_in": f(pool_w_in[0]), "w_grp": f(pool_w_group[0]), "w_out": f(pool_w_out[0]),
              "wg0": f(ffn_w_gate[0]), "wu0": f(ffn_w_up[0]), "wd0": f(ffn_w_down[0]),
              "w_qkv": f(attn_w_qkv[0]), "w_o": f(attn_w_out[0]),
              "wg1": f(ffn_w_gate[1]), "wu1": f(ffn_w_up[1]), "wd1": f(ffn_w_down[1])}
    in_maps = []
    for core in range(NCORES):
        first = core % cps == 0
        m = dict(shared)
        m.update({"xT": xs[core], "xh": hs[core], "invc": _invc_table(first),
                  "hv": np.full((128, 1), 0.0 if first else 1.0, np.float32),
                  "tab": _rope_tables((core % cps) * NT)})
        in_maps.append(m)
    r = run_bass_kernel_spmd(nc, in_maps, core_ids=list(range(NCORES)))
    out = np.stack([rr["outT"].T for rr in r.results]).reshape(BATCH, SEQ, D)
    return np.ascontiguousarray(out.astype(np.float32))
```

```python
import numpy as np
from contextlib import ExitStack
import concourse.bass as bass
import concourse.mybir as mybir
from concourse.bass_utils import run_bass_kernel_spmd

F32 = mybir.dt.float32
BF16 = mybir.dt.bfloat16
ALU = mybir.AluOpType
AF = mybir.ActivationFunctionType
DSZ = {F32: 4, BF16: 2}

ENGS = ("pe", "act", "dve", "pool", "sp")
ENG_ATTR = {"pe": "tensor", "act": "scalar", "dve": "vector", "pool": "gpsimd", "sp": "sync"}


class _Op:
    __slots__ = ("fn", "waits", "dma", "dsem", "inc")

    def __init__(self, fn, waits, dma, dsem, inc=16):
        self.fn, self.waits, self.dma, self.dsem, self.inc = fn, waits, dma, dsem, inc


class Sched:
    def __init__(self, nc, n_dma_sems=48):
        self.nc = nc
        self.ops = {e: [] for e in ENGS}
        self.count = {e: 0 for e in ENGS}
        self.waited = {e: {} for e in ENGS}
        self.acc = {}
        self.n_dma = n_dma_sems
        self.dma_val = [0] * n_dma_sems
        half = n_dma_sems // 2
        self.dma_pools = {"sw": list(range(0, half - 2)), "hw": list(range(half, n_dma_sems)),
                          "cc": list(range(half - 2, half))}
        self.dma_rr = {"sw": 0, "hw": 0, "cc": 0}

    @staticmethod
    def rng(ap):
        t = ap.tensor
        name = t.name
        esz = DSZ[ap.dtype]
        pat = ap.ap
        off = ap.offset
        shape = list(t.shape)
        if "DRam" in type(t).__name__ or len(shape) < 2:
            lo = off
            hi = off + sum((c - 1) * abs(s) for s, c in pat) + 1
            return [(name, 0, 1, lo * esz, hi * esz)]
        F = 1
        for s in shape[1:]:
            F *= s
        p0 = off // F
        c0 = off % F
        pcount = pat[0][1]
        free = [(abs(s), c) for s, c in pat[1:] if c > 1]
        hi = c0 + sum((c - 1) * s for s, c in free) + 1
        if name == "ps":
            lo_b = (c0 * esz) // 2048
            hi_b = (hi * esz + 2047) // 2048
            return [(name, 0, 128, lo_b * 2048, hi_b * 2048)]
        if len(free) >= 2:
            free.sort(reverse=True)
            s0, n0 = free[0]
            inner = sum((c - 1) * s for s, c in free[1:]) + 1
            if n0 <= 32 and inner <= s0:
                return [(name, p0, p0 + pcount, (c0 + i * s0) * esz, (c0 + i * s0 + inner) * esz)
                        for i in range(n0)]
        return [(name, p0, p0 + pcount, c0 * esz, hi * esz)]

    def _deps(self, reads, writes):
        toks = []
        for ap in reads:
            for (n, p0, p1, lo, hi) in self.rng(ap):
                a = self.acc.setdefault(n, {"w": [], "r": []})
                for (q0, q1, l, h, tok) in a["w"]:
                    if q0 < p1 and p0 < q1 and l < hi and lo < h:
                        toks.append(tok)
        for ap in writes:
            for (n, p0, p1, lo, hi) in self.rng(ap):
                a = self.acc.setdefault(n, {"w": [], "r": []})
                for (q0, q1, l, h, tok) in a["w"]:
                    if q0 < p1 and p0 < q1 and l < hi and lo < h:
                        toks.append(("W",) + tok)
                for (q0, q1, l, h, tok) in a["r"]:
                    if q0 < p1 and p0 < q1 and l < hi and lo < h:
                        toks.append(("W",) + tok)
        return toks

    def _record(self, reads, writes, tok):
        for ap in writes:
            for (n, p0, p1, lo, hi) in self.rng(ap):
                a = self.acc.setdefault(n, {"w": [], "r": []})
                a["w"] = [e for e in a["w"] if not (p0 <= e[0] and e[1] <= p1 and lo <= e[2] and e[3] <= hi)]
                a["r"] = [e for e in a["r"] if not (p0 <= e[0] and e[1] <= p1 and lo <= e[2] and e[3] <= hi)]
                a["w"].append((p0, p1, lo, hi, tok))
        for ap in reads:
            for (n, p0, p1, lo, hi) in self.rng(ap):
                a = self.acc.setdefault(n, {"w": [], "r": []})
                a["r"] = [e for e in a["r"] if not (e[4][0] == tok[0] and tok[0] != "dma" and p0 <= e[0]
                                                    and e[1] <= p1 and lo <= e[2] and e[3] <= hi)]
                a["r"].append((p0, p1, lo, hi, tok))

    def add(self, eng, fn, reads=(), writes=(), dma=False, extra_waits=(), cc=False):
        ps_reads = [ap for ap in reads if ap.tensor.name == "ps"]
        if ps_reads:
            reads = [ap for ap in reads if ap.tensor.name != "ps"]
            writes = list(writes) + ps_reads
        toks = self._deps(reads, writes) + list(extra_waits)
        waits = []
        wd = self.waited[eng]
        for tok in toks:
            war = False
            if tok[0] == "W":
                war = True
                tok = tok[1:]
            if tok[0] == "dma":
                key, val = ("dma", tok[1]), tok[2]
            else:
                key, val = tok[0], tok[1]
                if key == eng:
                    if eng in ("pe", "sp"):
                        continue
            if wd.get(key, 0) >= val:
                continue
            wd[key] = val
            waits.append((key, val))
        dsem = None
        if dma:
            pk_ = "cc" if cc else ("sw" if eng == "pool" else "hw")
            pool_ = self.dma_pools[pk_]
            dsem = pool_[self.dma_rr[pk_] % len(pool_)]
            self.dma_rr[pk_] += 1
            prev = self.dma_val[dsem]
            key = ("dma", dsem)
            if prev > 0 and wd.get(key, 0) < prev:
                wd[key] = prev
                waits.append((key, prev))
            inc_ = 1 if cc else 16
            self.dma_val[dsem] = prev + inc_
            tok = ("dma", dsem, prev + inc_)
        else:
            self.count[eng] += 1
            tok = (eng, self.count[eng])
        best = {}
        for k, v in waits:
            best[k] = max(best.get(k, 0), v)
        self.ops[eng].append(_Op(fn, list(best.items()), dma, dsem, 1 if cc else 16))
        self._record(reads, writes, tok)
        return tok

    def barrier_tokens(self):
        toks = [(e, self.count[e]) for e in ENGS if self.count[e] > 0]
        toks += [("dma", i, v) for i, v in enumerate(self.dma_val) if v > 0]
        return toks

    def emit(self, es, final_waits):
        nc = self.nc
        sems = {e: es.enter_context(nc.semaphore("s_" + e)) for e in ENGS}
        dsems = [es.enter_context(nc.semaphore("d%d" % i)) for i in range(self.n_dma)]
        block = es.enter_context(nc.Block())

        def semobj(key):
            return dsems[key[1]] if isinstance(key, tuple) else sems[key]

        for e in ENGS:
            deco = getattr(block, ENG_ATTR[e])

            def body(eng, e=e):
                for op in self.ops[e]:
                    for key, val in op.waits:
                        eng.wait_ge(semobj(key), val)
                    ins = op.fn(eng)
                    if op.dma:
                        ins.then_inc(dsems[op.dsem], op.inc)
                    else:
                        ins.then_inc(sems[e], 1)
                if e == "sp":
                    for tok in final_waits:
                        if tok[0] == "dma":
                            eng.wait_ge(dsems[tok[1]], tok[2])
                        else:
                            eng.wait_ge(sems[tok[0]], tok[1])

            deco(body)


NCORES = 8
BATCH, SEQ, D = 2, 8192, 1024
NT = 2048
CPS = SEQ // NT
TT = 512
NTT = NT // TT
DC = D // 128
FF = 2816
FC = FF // 128
FH = 11
HALO = 16
EPS = 1e-6
NH, HD = 16, 64
GROUP_HEADS = ((0, 6), (6, 11), (11, 16))
DIL = (1, 4, 16)
V_GMIX0, V_GFFN0, V_PSC, V_GMIX1, V_GFFN1, V_GFIN = 0, 8, 16, 24, 32, 40

ARENA_BYTES = 212480


class Arena:
    def __init__(self, nc, es, name="arena", nbytes=ARENA_BYTES):
        self.t = es.enter_context(nc.sbuf_tensor(name, [128, nbytes // 2], BF16))
        self.nbytes = nbytes

    def view(self, off, shape, dtype):
        n = 1
        for s in shape:
            n *= s
        esz = DSZ[dtype]
        assert off % 4 == 0 and off + n * esz <= self.nbytes, (off, shape, dtype)
        ap = self.t[:, off // 2: off // 2 + n * esz // 2]
        if dtype != BF16:
            ap = ap.bitcast(dtype)
        if len(shape) == 2:
            ap = ap.rearrange("p (a b) -> p a b", a=shape[0])
        elif len(shape) == 3:
            ap = ap.rearrange("p (a b c) -> p a b c", a=shape[0], b=shape[1])
        return ap


class Ctx:
    def __init__(self, nc, es):
        self.nc = nc
        self.es = es
        self.S = Sched(nc)
        self.ps = es.enter_context(nc.psum_tensor("ps", [128, 8, 512], F32))
        self.bank = 0

    def psum(self):
        b = self.bank
        self.bank = (self.bank + 1) % 8
        return self.ps[:, b, :]

    def mm(self, out, pairs):
        pairs = list(pairs)
        n = len(pairs)

        def fn(e):
            ins = None
            for i, (l, r) in enumerate(pairs):
                ins = e.matmul(out, lhsT=l, rhs=r, start=(i == 0), stop=(i == n - 1))
            return ins
        reads = [p[0] for p in pairs] + [p[1] for p in pairs]
        return self.S.add("pe", fn, reads=reads, writes=[out])

    def dma(self, q, out, in_):
        return self.S.add(q, lambda e: e.dma_start(out=out, in_=in_), reads=[in_], writes=[out], dma=True)

    def act(self, out, in_, func, reads=None, **kw):
        rd = [in_] + [v for v in kw.values() if not isinstance(v, (int, float))]
        return self.S.add("act", lambda e: e.activation(out=out, in_=in_, func=func, **kw), reads=rd, writes=[out])

    def tt(self, eng, out, in0, in1, op):
        return self.S.add(eng, lambda e: e.tensor_tensor(out=out, in0=in0, in1=in1, op=op),
                          reads=[in0, in1], writes=[out])

    def stt(self, eng, out, in0, scalar, in1, op0, op1):
        rd = [in0, in1] + ([] if isinstance(scalar, (int, float)) else [scalar])
        return self.S.add(eng, lambda e: e.scalar_tensor_tensor(out=out, in0=in0, scalar=scalar, in1=in1,
                                                                op0=op0, op1=op1), reads=rd, writes=[out])

    def ts(self, eng, out, in0, scalar1, op0, scalar2=None, op1=None):
        rd = [in0] + [s for s in (scalar1, scalar2) if s is not None and not isinstance(s, (int, float))]
        if op1 is None:
            return self.S.add(eng, lambda e: e.tensor_scalar(out=out, in0=in0, scalar1=scalar1, scalar2=None,
                                                             op0=op0), reads=rd, writes=[out])
        return self.S.add(eng, lambda e: e.tensor_scalar(out=out, in0=in0, scalar1=scalar1, scalar2=scalar2,
                                                         op0=op0, op1=op1), reads=rd, writes=[out])

    def copy(self, eng, out, in_):
        return self.S.add(eng, lambda e: e.tensor_copy(out=out, in_=in_), reads=[in_], writes=[out])

    def recip(self, out, in_):
        return self.S.add("dve", lambda e: e.reciprocal(out=out, in_=in_), reads=[in_], writes=[out])

    def memset(self, eng, out, val):
        return self.S.add(eng, lambda e: e.memset(out, val), writes=[out])


def rmsnorm_tile(C, xsrc, hdst, n, gcol, ones, sq, rt, rstd):
    C.act(sq[:, :, :n], xsrc, AF.Square)
    ps = C.psum()
    C.mm(ps[:, :n], [(ones, sq[:, c, :n]) for c in range(DC)])
    C.act(rt[:, :n], ps[:, :n], AF.Sqrt, bias=EPS, scale=1.0 / D)
    C.recip(rstd[:, :n], rt[:, :n])
    for c in range(DC):
        C.stt("dve", hdst[:, c, :], xsrc[:, c, :], gcol(c), rstd[:, :n], ALU.mult, ALU.mult)


def tile_major_residual(C, xT, pairs_fn, after_tile=None, before_tile=None):
    for t in range(NTT):
        tsl = slice(t * TT, (t + 1) * TT)
        if before_tile is not None:
            before_tile(t)
        for m in range(DC):
            ps = C.psum()
            C.mm(ps, pairs_fn(m, tsl))
            C.tt("dve", xT[:, m, tsl], xT[:, m, tsl], ps, ALU.add)
        if after_tile is not None and t >= 1:
            after_tile(t - 1)
    if after_tile is not None:
        after_tile(NTT - 1)


def ffn_layer(C, A, PB, xT, hT, wg, wu, wd, after_tile=None):
    aT = A.view(PB, [FH, NT], BF16)
    wdh = A.view(PB + 45056, [FH, D], BF16)
    wgu = [(A.view(PB + 67584 + i * 8192, [DC, 256], BF16),
            A.view(PB + 67584 + i * 8192 + 4096, [DC, 256], BF16)) for i in range(2)]
    stmp = [A.view(PB + 83968 + i * 2048, [TT], F32) for i in range(2)]
    wg_v = wg.rearrange("(c p) n -> p c n", p=128)
    wu_v = wu.rearrange("(c p) n -> p c n", p=128)
    wd_v = wd.rearrange("(j p) n -> p j n", p=128)
    gi = 0
    si = 0
    for half in range(2):
        j0 = half * FH
        jl = 0
        while jl < FH:
            nj = min(2, FH - jl)
            bg, bu = wgu[gi % 2]
            gi += 1
            c0 = (j0 + jl) * 128
            C.dma("pool", bg[:, :, 0:nj * 128], wg_v[:, :, c0:c0 + nj * 128])
            C.dma("pool", bu[:, :, 0:nj * 128], wu_v[:, :, c0:c0 + nj * 128])
            if jl == 2:
                C.dma("pool", wdh[:, 0:6, :], wd_v[:, j0:j0 + 6, :])
                C.dma("pool", wdh[:, 6:FH, :], wd_v[:, j0 + 6:j0 + FH, :])
            for jj in range(nj):
                for t in range(NTT):
                    tsl = slice(t * TT, (t + 1) * TT)
                    pg = C.psum()
                    C.mm(pg, [(bg[:, k, jj * 128:(jj + 1) * 128], hT[:, k, tsl]) for k in range(DC)])
                    pu = C.psum()
                    C.mm(pu, [(bu[:, k, jj * 128:(jj + 1) * 128], hT[:, k, tsl]) for k in range(DC)])
                    st = stmp[si % 2]
                    si += 1
                    C.act(st, pg, AF.Silu)
                    C.tt("dve", aT[:, jl + jj, tsl], st, pu, ALU.mult)
            jl += nj
        tile_major_residual(C, xT, lambda m, tsl: [(wdh[:, j, m * 128:(m + 1) * 128], aT[:, j, tsl])
                                                   for j in range(FH)],
                            after_tile if half == 1 else None)


PB = 118784


def common_views(A):
    v = {}
    v["xT"] = A.view(0, [DC, NT], F32)
    v["hT"] = A.view(65536, [DC, NT], BF16)
    v["sq"] = A.view(98304, [DC, TT], BF16)
    v["rt"] = A.view(106496, [TT], F32)
    v["rstd"] = A.view(108544, [TT], F32)
    v["vecs"] = A.view(110592, [48], F32)
    v["ones"] = A.view(110784, [128], BF16)
    v["invc"] = A.view(111040, [DC, HALO], F32)
    v["xh"] = A.view(111552, [DC, HALO], F32)
    v["hh"] = A.view(112064, [DC, HALO], BF16)
    v["t16"] = A.view(112320, [HALO], F32)
    v["hvt"] = A.view(112384, [16], F32)
    return v


def pool_layer(C, A, V, w_in, w_grp, w_out, after_tile=None):
    xT, hT, hh, invc, vecs = V["xT"], V["hT"], V["hh"], V["invc"], V["vecs"]
    pT = A.view(PB, [DC, NT], BF16)
    win = A.view(PB + 32768, [DC, D], BF16)
    wgr = A.view(PB + 49152, [4, 2, 256], BF16)
    NU = NT + HALO
    ub = [A.view(PB + 53248 + i * 8256, [NU], F32) for i in range(2)]
    wa = A.view(PB + 69760, [NU], F32)
    wb = A.view(PB + 69760 + 8256, [NU], F32)
    wout = A.view(PB + 53248, [DC, D], BF16)
    zT = hT
    C.dma("pool", win[:, 0:4, :], w_in.rearrange("(c p) n -> p c n", p=128)[:, 0:4, :])
    C.dma("pool", win[:, 4:8, :], w_in.rearrange("(c p) n -> p c n", p=128)[:, 4:8, :])
    C.dma("pool", wgr, w_grp.rearrange("g (c p) n -> p g c n", p=128))
    for m in range(DC):
        u = ub[m % 2]
        msl = slice(m * 128, (m + 1) * 128)
        ps = C.psum()
        C.mm(ps[:, :HALO], [(win[:, k, msl], hh[:, k, :]) for k in range(DC)])
        C.act(u[:, 0:HALO], ps[:, :HALO], AF.Copy)
        for t in range(NTT):
            tsl = slice(t * TT, (t + 1) * TT)
            ps = C.psum()
            C.mm(ps, [(win[:, k, msl], hT[:, k, tsl]) for k in range(DC)])
            C.act(u[:, HALO + t * TT:HALO + (t + 1) * TT], ps, AF.Copy)
        g = m // 2
        L = g + 1
        w = 2 ** L
        src = u
        bufs = [wa, wb]
        for l in range(1, L + 1):
            sh = 2 ** (l - 1)
            lo = 2 ** l - 1
            dst = bufs[(l - 1) % 2]
            C.tt("dve", dst[:, lo:NU], src[:, lo:NU], src[:, lo - sh:NU - sh], ALU.add)
            src = dst
        C.stt("dve", pT[:, m, :], src[:, HALO:NU], 1.0 / w, u[:, HALO:NU], ALU.mult, ALU.subtract)
        t16 = V["t16"]
        C.tt("dve", t16, src[:, HALO:2 * HALO], invc[:, m, :], ALU.mult)
        C.tt("dve", pT[:, m, 0:HALO], t16, u[:, HALO:2 * HALO], ALU.subtract)
    C.dma("pool", wout[:, 0:4, :], w_out.rearrange("(c p) n -> p c n", p=128)[:, 0:4, :])
    C.dma("pool", wout[:, 4:8, :], w_out.rearrange("(c p) n -> p c n", p=128)[:, 4:8, :])
    for m in range(DC):
        g, mo = m // 2, m % 2
        for t in range(NTT):
            tsl = slice(t * TT, (t + 1) * TT)
            ps = C.psum()
            C.mm(ps, [(wgr[:, g, ki, mo * 128:(mo + 1) * 128], pT[:, 2 * g + ki, tsl]) for ki in range(2)])
            C.act(zT[:, m, tsl], ps, AF.Copy, scale=vecs[:, V_PSC + m:V_PSC + m + 1])
    tile_major_residual(C, xT, lambda m, tsl: [(wout[:, k, m * 128:(m + 1) * 128], zT[:, k, tsl])
                                               for k in range(DC)], after_tile)


def build_layer0():
    nc = bass.Bass("TRN2", target_bir_lowering=False)
    dt_in = lambda n, s: nc.dram_tensor(n, s, F32, kind="ExternalInput").ap()
    xT_d = dt_in("xT", [D, NT])
    xh_d = dt_in("xh", [D, HALO])
    vecs_d = dt_in("vecs", [128, 48])
    invc_d = dt_in("invc", [128, DC * HALO])
    w_in = dt_in("w_in", [D, D])
    w_grp = dt_in("w_grp", [4, 256, 256])
    w_out = dt_in("w_out", [D, D])
    wg = dt_in("wg0", [D, FF])
    wu = dt_in("wu0", [D, FF])
    wd = dt_in("wd0", [FF, D])
    x1_d = nc.dram_tensor("x1T", [D, NT], F32, kind="ExternalOutput").ap()
    with ExitStack() as es:
        A = Arena(nc, es)
        C = Ctx(nc, es)
        V = common_views(A)
        layer0_body(C, A, V, xT_d, xh_d, vecs_d, invc_d, w_in, w_grp, w_out, wg, wu, wd)
        toks = []
        x1v = x1_d.rearrange("(c p) t -> p c t", p=128)
        for t in range(NTT):
            tsl = slice(t * TT, (t + 1) * TT)
            toks.append(C.dma("sp", x1v[:, :, tsl], V["xT"][:, :, tsl]))
        C.S.emit(es, toks)
    return nc


def layer0_body(C, A, V, xT_d, xh_d, vecs_d, invc_d, w_in, w_grp, w_out, wg, wu, wd, next_norm=None):
    xT, hT = V["xT"], V["hT"]
    C.dma("sp", V["vecs"], vecs_d)
    C.dma("sp", V["invc"], invc_d.rearrange("p (c h) -> p c h", c=DC))
    C.dma("sp", V["xh"], xh_d.rearrange("(c p) h -> p c h", p=128))
    xv = xT_d.rearrange("(c p) t -> p c t", p=128)
    for t in range(NTT):
        tsl = slice(t * TT, (t + 1) * TT)
        C.dma("sp", xT[:, :, tsl], xv[:, :, tsl])
    C.memset("dve", V["ones"], 1.0)
    gcol = lambda base: (lambda c: V["vecs"][:, base + c:base + c + 1])
    rmsnorm_tile(C, V["xh"], V["hh"], HALO, gcol(V_GMIX0), V["ones"], V["sq"], V["rt"], V["rstd"])
    for t in range(NTT):
        tsl = slice(t * TT, (t + 1) * TT)
        rmsnorm_tile(C, xT[:, :, tsl], hT[:, :, tsl], TT, gcol(V_GMIX0), V["ones"], V["sq"], V["rt"], V["rstd"])
    def norm_ffn0(t):
        tsl = slice(t * TT, (t + 1) * TT)
        rmsnorm_tile(C, xT[:, :, tsl], hT[:, :, tsl], TT, gcol(V_GFFN0), V["ones"], V["sq"], V["rt"], V["rstd"])
    pool_layer(C, A, V, w_in, w_grp, w_out, after_tile=norm_ffn0)
    ffn_layer(C, A, PB, xT, hT, wg, wu, wd, after_tile=next_norm)


def _vec_layout(v):
    return np.ascontiguousarray(np.asarray(v, np.float32).reshape(DC, 128).T)


def _make_vecs(norm_mix, norm_ffn, norm_final, pool_scale):
    cols = [norm_mix[0], norm_ffn[0], pool_scale[0], norm_mix[1], norm_ffn[1], norm_final]
    return np.ascontiguousarray(np.concatenate([_vec_layout(c) for c in cols], axis=1))


def _invc_table(first):
    t = np.zeros((128, DC, HALO), np.float32)
    for c in range(DC):
        w = 2 ** (c // 2 + 1)
        for i in range(HALO):
            t[:, c, i] = 1.0 / (min(i + 1, w) if first else w)
    return np.ascontiguousarray(t.reshape(128, DC * HALO))


def _core_tokens(x):
    xs, hs = [], []
    cps = SEQ // NT
    for core in range(NCORES):
        b, c = core // cps, core % cps
        xs.append(np.ascontiguousarray(x[b, c * NT:(c + 1) * NT, :].T))
        if c == 0:
            hs.append(np.zeros((D, HALO), np.float32))
        else:
            hs.append(np.ascontiguousarray(x[b, c * NT - HALO:c * NT, :].T))
    return xs, hs


def dsl(start, d, n=128):
    return slice(start, start + d * (n - 1) + 1, d)


def head_info(h):
    for g, (h0, h1) in enumerate(GROUP_HEADS):
        if h0 <= h < h1:
            return g, h0, h1 - h0
    raise ValueError(h)


def layer1_views(A):
    L = {}
    L["kT"] = A.view(0, [DC, NT], BF16)
    L["xs"] = [A.view(i * 16384, [DC, TT], F32) for i in range(2)]
    L["khb"] = A.view(32768, [3, NT], BF16)
    L["khs"] = A.view(45056, [5, TT], BF16)
    L["vh"] = [A.view(50176, [1, 6, HD], BF16), A.view(50176 + 768, [4, 5, HD], BF16),
               A.view(50176 + 768 + 2560, [16, 5, HD], BF16)]
    L["qT"] = A.view(PB, [DC, NT], BF16)
    L["hTh"] = L["qT"]
    vo = PB + 32768
    L["v"] = [A.view(vo, [16, 6, HD], BF16), A.view(vo + 12288, [16, 5, HD], BF16),
              A.view(vo + 12288 + 10240, [16, 5, HD], BF16)]
    L["cos"] = A.view(vo, [2 * NT], F32)
    L["sin"] = A.view(vo + 16384, [2 * NT], F32)
    wo_ = PB + 65536
    L["wqk"] = [A.view(wo_ + i * 2048, [DC, 128], BF16) for i in range(3)]
    L["wv"] = A.view(wo_ + 6144, [DC, 384], BF16)
    L["R"] = A.view(wo_ + 12288, [128], BF16)
    L["wst"] = [A.view(wo_ + 12544 + i * 4096, [DC, 128], F32) for i in range(2)]
    L["Dg"] = A.view(wo_, [3, NT], F32)
    L["wo"] = A.view(PB, [DC, D], BF16)
    b = 98304
    L["qb"] = [A.view(b + i * 1024, [TT], BF16) for i in range(2)]
    L["t1"] = [A.view(b + 2048 + i * 2048, [TT], F32) for i in range(2)]
    L["t2"] = [A.view(b + 6144 + i * 2048, [TT], F32) for i in range(2)]
    L["pT"] = [A.view(b + i * 2048, [1024], BF16) for i in range(2)]
    L["rden"] = [A.view(b + 4096 + i * 2048, [TT], F32) for i in range(2)]
    L["masks"] = A.view(112448, [3, 1024], BF16)
    return L


def build_layer1_body(C, A, V, x1_own_d, x1_halo_d, tab_d, mask_d, R_d, hv_d,
                      w_qkv, w_o, wg, wu, wd, out_d, own_in_sbuf=False, gather=None, own_normed=False):
    S = C.S
    L = layer1_views(A)
    xT, hT, vecs, ones = V["xT"], V["hT"], V["vecs"], V["ones"]
    import os as _os
    _stop = _os.environ.get("KSTOP", "")

    def stop_here(tag, src_f32):
        if _stop != tag:
            return None
        ov_ = out_d.rearrange("(c p) t -> p c t", p=128)
        return [C.dma("sp", ov_[:, :, 0:TT], src_f32)]
    gcol = lambda base: (lambda c: vecs[:, base + c:base + c + 1])
    nrm = lambda xsrc, hdst, n, base: rmsnorm_tile(C, xsrc, hdst, n, gcol(base), ones, V["sq"], V["rt"], V["rstd"])
    x1o = x1_own_d.rearrange("(c p) t -> p c t", p=128)
    x1h = None if gather is not None else x1_halo_d.rearrange("(c p) t -> p c t", p=128)
    wq_v = w_qkv.rearrange("(c p) n -> p c n", p=128)

    masks, Rm, hv = L["masks"], L["R"], V["hvt"]
    C.dma("sp", hv[:, 0:1], hv_d)
    C.dma("pool", Rm, R_d)
    C.dma("pool", masks[:, 0, :], mask_d)
    for i in range(4):
        C.dma("sp", L["cos"][:, i * 1024:(i + 1) * 1024], tab_d[0][:, i * 1024:(i + 1) * 1024])
        C.dma("sp", L["sin"][:, i * 1024:(i + 1) * 1024], tab_d[1][:, i * 1024:(i + 1) * 1024])
    C.copy("pool", masks[:, 1, :], masks[:, 0, :])
    C.ts("dve", masks[:, 1, 0:128], masks[:, 0, 0:128], hv[:, 0:1], ALU.mult)
    C.copy("pool", masks[:, 2, :], masks[:, 0, :])
    m2v = masks[:, 2, :].rearrange("p (e s) -> p e s", e=4)
    m0v = masks[:, 0, :].rearrange("p (e s) -> p e s", e=4)
    C.ts("dve", m2v[:, :, 0:128], m0v[:, :, 0:128], hv[:, 0:1], ALU.mult)

    if not own_in_sbuf:
        for t in range(NTT):
            tsl = slice(t * TT, (t + 1) * TT)
            C.dma("sp", xT[:, :, tsl], x1o[:, :, tsl])
    if not own_normed:
        for t in range(NTT):
            tsl = slice(t * TT, (t + 1) * TT)
            nrm(xT[:, :, tsl], hT[:, :, tsl], TT, V_GMIX1)

    hTh = L["hTh"]
    cnt = {"w": 0, "r": 0}

    pend = []

    def proj_rope(wcol0, src, s_tsl, tab_off, dst):
        wbuf = L["wqk"][(cnt["w"] - 1) % 3]
        i = cnt["r"] % 2
        cnt["r"] += 1
        pk = C.psum()
        C.mm(pk, [(wbuf[:, k, :], src[:, k, s_tsl]) for k in range(DC)])
        qb, t1, t2 = L["qb"][i], L["t1"][i], L["t2"][i]
        C.act(qb, pk, AF.Copy)

        def stage_b():
            pr = C.psum()
            C.mm(pr, [(Rm, qb)])
            C.tt("dve", t1, pk, L["cos"][:, tab_off:tab_off + TT], ALU.mult)
            C.tt("dve", t2, pr, L["sin"][:, tab_off:tab_off + TT], ALU.mult)
            C.tt("dve", dst, t1, t2, ALU.add)
        pend.append(stage_b)
        if len(pend) > 1:
            pend.pop(0)()

    def rope_flush():
        while pend:
            pend.pop(0)()

    cnt["s"] = 0

    def stage_cast(dst_bf, col0, ncol):
        st = L["wst"][cnt["s"] % 2]
        cnt["s"] += 1
        C.dma("sp", st[:, :, 0:ncol], wq_v[:, :, col0:col0 + ncol])
        C.act(dst_bf, st[:, :, 0:ncol], AF.Copy)

    def load_wqk(col0):
        wbuf = L["wqk"][cnt["w"] % 3]
        cnt["w"] += 1
        stage_cast(wbuf, col0, 128)

    def load_wv(g):
        h0, h1 = GROUP_HEADS[g]
        ncol = (h1 - h0) * HD
        c0 = 0
        while c0 < ncol:
            n = min(128, ncol - c0)
            stage_cast(L["wv"][:, :, c0:c0 + n], 2 * D + h0 * HD + c0, n)
            c0 += n

    def v_blocks(src, dstv, g, blocks):
        h0, nh = GROUP_HEADS[g][0], GROUP_HEADS[g][1] - GROUP_HEADS[g][0]
        ncol = nh * HD
        for bi, sl in blocks:
            ps = C.psum()
            C.mm(ps[:, :ncol], [(src[:, k, sl], L["wv"][:, k, 0:ncol]) for k in range(DC)])
            C.act(dstv[:, bi, :, :], ps[:, :ncol].rearrange("p (h d) -> p h d", h=nh), AF.Copy)

    def halo_blocks(g):
        d = DIL[g]
        return [(r, dsl(NT - 128 * d + r, d)) for r in range(d)]

    def own_blocks(g):
        d = DIL[g]
        nb = 16 // d
        return [(r * nb + b, dsl(r + d * 128 * b, d)) for r in range(d) for b in range(nb)]

    def do_k_own():
        for c in range(DC):
            load_wqk(D + c * 128)
            for t in range(NTT):
                tsl = slice(t * TT, (t + 1) * TT)
                proj_rope(D + c * 128, hT, tsl, NT + t * TT, L["kT"][:, c, tsl])
        rope_flush()

    def do_halo():
        if gather is None:
            for t in range(NTT):
                tsl = slice(t * TT, (t + 1) * TT)
                xs = L["xs"][t % 2]
                C.dma("sp", xs, x1h[:, :, tsl])
                nrm(xs, hTh[:, :, tsl], TT, V_GMIX1)
        else:
            pass
        for c in range(DC):
            load_wqk(D + c * 128)
            tiles = range(NTT) if c >= 5 else [NTT - 1]
            for t in tiles:
                tsl = slice(t * TT, (t + 1) * TT)
                dst = L["khb"][:, c - 5, tsl] if c >= 5 else L["khs"][:, c, :]
                proj_rope(D + c * 128, hTh, tsl, t * TT, dst)
        rope_flush()
        for g in range(3):
            load_wv(g)
            v_blocks(hTh, L["vh"][g], g, halo_blocks(g))

    if gather is not None:
        hx_in, hx_all = gather
        x1s_v = x1_own_d.rearrange("(c p) t -> p c t", p=128)
        hin_v = hx_in.ap().rearrange("(c p) t -> p c t", p=128)
        for t in range(NTT):
            tsl = slice(t * TT, (t + 1) * TT)
            C.dma("sp", hin_v[:, :, tsl], hT[:, :, tsl])
            C.dma("sp", x1s_v[:, :, tsl], xT[:, :, tsl])
        groups = [list(range(NCORES))]
        S.add("pool", lambda e: e.collective_compute("AllGather", ALU.bypass, replica_groups=groups,
                                                     ins=[hx_in.ap().opt()], outs=[hx_all.ap().opt()]),
              reads=[hx_in.ap()], writes=[hx_all.ap()], dma=True, cc=True)
        hall_v = hx_all.ap().rearrange("(r c p) t -> r p c t", r=NCORES, p=128)
        for t in range(NTT):
            tsl = slice(t * TT, (t + 1) * TT)

            def rd(e, tsl=tsl):
                prev = (e.partition_id() + (NCORES - 1)) % NCORES
                return e.dma_start(out=hTh[:, :, tsl],
                                   in_=hall_v[bass.ds(prev, 1), :, :, tsl].rearrange("r p c t -> (r p) c t"))
            S.add("pool", rd, reads=[hx_all.ap()], writes=[hTh[:, :, tsl]], dma=True)
        do_k_own()
        do_halo()
    else:
        do_halo()
        do_k_own()

    for c in range(DC):
        load_wqk(c * 128)
        for t in range(NTT):
            tsl = slice(t * TT, (t + 1) * TT)
            proj_rope(c * 128, hT, tsl, NT + t * TT, L["qT"][:, c, tsl])
    rope_flush()
    for g in range(3):
        load_wv(g)
        v_blocks(hT, L["v"][g], g, own_blocks(g))

    _r = stop_here("qkv", L["xs"][1])
    if _r:
        return _r
    oT = hT
    Dg = L["Dg"]
    C.memset("pool", Dg[0:64, :, :], 0.0)
    psS = [C.ps[:, 0:2, :].rearrange("p a b -> p (a b)"), C.ps[:, 2:4, :].rearrange("p a b -> p (a b)")]
    psO = [C.ps[:, 4:6, :].rearrange("p a b -> p (a b)"), C.ps[:, 6:8, :].rearrange("p a b -> p (a b)")]
    bi = 0
    for h in range(NH):
        g, h0, nh = head_info(h)
        d = DIL[g]
        nb = 16 // d
        c, ro, hl = h // 2, (h % 2) * HD, h - h0
        prow = slice(ro, ro + HD)
        entries = [(r, b) for r in range(d) for b in range(nb)]
        for q in range(4):
            ents = entries[4 * q:4 * q + 4]
            pS, pO, pT, rden = psS[bi % 2], psO[bi % 2], L["pT"][bi % 2], L["rden"][bi % 2]
            bi += 1
            n_halo = sum(1 for (r, b) in ents if b == 0)
            mk = masks[:, 0, :] if n_halo == 0 else (masks[:, 1, :] if n_halo == 1 else masks[:, 2, :])
            assert n_halo in (0, 4) or (n_halo == 1 and ents[0][1] == 0)
            kv = []
            for e, (r, b) in enumerate(ents):
                start = r + d * 128 * b
                qsl = dsl(start, d)
                qa = L["qT"][prow, c, qsl]
                kcur = L["kT"][prow, c, qsl]
                vcur = L["v"][g][:, r * nb + b, hl, :]
                if b >= 1:
                    kprev = L["kT"][prow, c, dsl(start - d * 128, d)]
                    vprev = L["v"][g][:, r * nb + b - 1, hl, :]
                else:
                    hsl = dsl(NT - 128 * d + r, d)
                    if c >= 5:
                        kprev = L["khb"][prow, c - 5, hsl]
                    else:
                        kprev = L["khs"][prow, c, dsl(hsl.start - (NT - TT), d)]
                    vprev = L["vh"][g][:, r, hl, :]
                C.mm(pS[:, e * 256:e * 256 + 128], [(kprev, qa)])
                C.mm(pS[:, e * 256 + 128:e * 256 + 256], [(kcur, qa)])
                kv.append((vprev, vcur))
            C.act(pT, pS, AF.Exp, scale=HD ** -0.5)
            C.tt("pool", pT, pT, mk, ALU.mult)
            for e, (vprev, vcur) in enumerate(kv):
                pp, pc = pT[:, e * 256:e * 256 + 128], pT[:, e * 256 + 128:e * 256 + 256]
                C.mm(pO[0:64, e * 128:(e + 1) * 128], [(vprev, pp), (vcur, pc)])
            pTv = pT.rearrange("p (e s) -> p e s", e=4)
            num = pO[0:64, 0:512].rearrange("p (e s) -> p e s", e=4)
            den = pO[0:64, 512:1024].rearrange("p (e s) -> p e s", e=4)
            C.mm(den, [(ones[:, 0:64], pTv[:, :, 0:128]), (ones[:, 0:64], pTv[:, :, 128:256])])
            rdv = rden[0:64, :].rearrange("p (e s) -> p e s", e=4)
            if d == 1:
                tv = lambda ap3: ap3[:, q * TT:(q + 1) * TT].rearrange("p (e s) -> p e s", e=4)
            elif d == 4:
                r0 = ents[0][0]
                tv = lambda ap3: ap3[:, dsl(r0, 4, 512)].rearrange("p (e s) -> p e s", e=4)
            else:
                r0 = ents[0][0]
                tv = lambda ap3: ap3[:, :].rearrange("p (s r) -> p r s", r=16)[:, r0:r0 + 4, :]
            C.act(rdv, den, AF.Ln)
            C.act(rdv, rdv, AF.Exp, scale=-1.0)
            C.tt("dve", tv(oT[prow, c, :]), num, rdv, ALU.mult)
            dgv = tv(Dg[0:64, g, :])
            C.tt("dve", dgv, den, dgv, ALU.add)

    _r = stop_here("attn", L["cos"][:, 0:4096].rearrange("p (c t) -> p c t", c=DC))
    if _r:
        return _r
    C.dma("pool", L["wo"][:, 0:4, :], w_o.rearrange("(c p) n -> p c n", p=128)[:, 0:4, :])
    C.dma("pool", L["wo"][:, 4:8, :], w_o.rearrange("(c p) n -> p c n", p=128)[:, 4:8, :])
    for t in range(NTT):
        tsl = slice(t * TT, (t + 1) * TT)
        C.dma("sp", xT[:, :, tsl], x1o[:, :, tsl])
    tot = A.view(PB + 90112, [TT], F32)
    ng = [GROUP_HEADS[g][1] - GROUP_HEADS[g][0] for g in range(3)]

    def merge_tile(t):
        tsl = slice(t * TT, (t + 1) * TT)
        tv_ = tot[0:64, :]
        C.ts("dve", tv_, Dg[0:64, 0, tsl], 1.0 / ng[0], ALU.mult)
        C.stt("dve", tv_, Dg[0:64, 1, tsl], 1.0 / ng[1], tv_, ALU.mult, ALU.add)
        C.stt("dve", tv_, Dg[0:64, 2, tsl], 1.0 / ng[2], tv_, ALU.mult, ALU.add)
        C.recip(tv_, tv_)
        for g in range(3):
            C.stt("dve", Dg[0:64, g, tsl], Dg[0:64, g, tsl], 3.0 / ng[g], tv_, ALU.mult, ALU.mult)
        C.act(Dg[64:128, :, tsl], Dg[0:64, :, tsl], AF.Copy)
        for c in range(DC):
            ga, gb = head_info(2 * c)[0], head_info(2 * c + 1)[0]
            if ga == gb:
                C.tt("dve", oT[:, c, tsl], oT[:, c, tsl], Dg[:, ga, tsl], ALU.mult)
            else:
                C.tt("dve", oT[0:64, c, tsl], oT[0:64, c, tsl], Dg[0:64, ga, tsl], ALU.mult)
                C.tt("dve", oT[64:128, c, tsl], oT[64:128, c, tsl], Dg[64:128, gb, tsl], ALU.mult)

    def norm_ffn1(t):
        tsl = slice(t * TT, (t + 1) * TT)
        nrm(xT[:, :, tsl], hT[:, :, tsl], TT, V_GFFN1)
    tile_major_residual(C, xT, lambda m, tsl: [(L["wo"][:, k, m * 128:(m + 1) * 128], oT[:, k, tsl])
                                               for k in range(DC)], norm_ffn1, before_tile=merge_tile)

    _r = stop_here("oproj", xT[:, :, 0:TT])
    if _r:
        return _r
    ob = [A.view(65536 + i * 16384, [DC, TT], F32) for i in range(2)]
    ov = out_d.rearrange("(c p) t -> p c t", p=128)
    toks = []

    def norm_final(t):
        tsl = slice(t * TT, (t + 1) * TT)
        rmsnorm_tile(C, xT[:, :, tsl], ob[t % 2], TT, gcol(V_GFIN), ones, V["sq"], V["rt"], V["rstd"])
        toks.append(C.dma("sp", ov[:, :, tsl], ob[t % 2]))
    ffn_layer(C, A, PB, xT, hT, wg, wu, wd, after_tile=norm_final)
    return toks


def build_layer1():
    nc = bass.Bass("TRN2", target_bir_lowering=False)
    dt_in = lambda n, s: nc.dram_tensor(n, s, F32, kind="ExternalInput").ap()
    x1o = dt_in("x1T", [D, NT])
    x1h = dt_in("x1h", [D, NT])
    vecs_d = dt_in("vecs", [128, 48])
    hv_d = dt_in("hv", [128, 1])
    tab_d = dt_in("tab", [2, 128, 2 * NT])
    mask_d = dt_in("mask", [128, 1024])
    R_d = dt_in("rmat", [128, 128])
    w_qkv = dt_in("w_qkv", [D, 3 * D])
    w_o = dt_in("w_o", [D, D])
    wg = dt_in("wg1", [D, FF])
    wu = dt_in("wu1", [D, FF])
    wd = dt_in("wd1", [FF, D])
    out_d = nc.dram_tensor("outT", [D, NT], F32, kind="ExternalOutput").ap()
    with ExitStack() as es:
        A = Arena(nc, es)
        C = Ctx(nc, es)
        V = common_views(A)
        C.dma("sp", V["vecs"], vecs_d)
        C.memset("dve", V["ones"], 1.0)
        toks = build_layer1_body(C, A, V, x1o, x1h, tab_d, mask_d, R_d, hv_d, w_qkv, w_o, wg, wu, wd, out_d)
        C.S.emit(es, toks)
    return nc


def build_fused():
    nc = bass.Bass("TRN2", target_bir_lowering=False)
    dt_in = lambda n, s: nc.dram_tensor(n, s, F32, kind="ExternalInput").ap()
    xT_d = dt_in("xT", [D, NT])
    xh_d = dt_in("xh", [D, HALO])
    vecs_d = dt_in("vecs", [128, 48])
    invc_d = dt_in("invc", [128, DC * HALO])
    hv_d = dt_in("hv", [128, 1])
    tab_d = dt_in("tab", [2, 128, 2 * NT])
    mask_d = dt_in("mask", [128, 1024])
    R_d = dt_in("rmat", [128, 128])
    w_in = dt_in("w_in", [D, D])
    w_grp = dt_in("w_grp", [4, 256, 256])
    w_out = dt_in("w_out", [D, D])
    wg0 = dt_in("wg0", [D, FF])
    wu0 = dt_in("wu0", [D, FF])
    wd0 = dt_in("wd0", [FF, D])
    w_qkv = dt_in("w_qkv", [D, 3 * D])
    w_o = dt_in("w_o", [D, D])
    wg1 = dt_in("wg1", [D, FF])
    wu1 = dt_in("wu1", [D, FF])
    wd1 = dt_in("wd1", [FF, D])
    out_d = nc.dram_tensor("outT", [D, NT], F32, kind="ExternalOutput").ap()
    x1s = nc.dram_tensor("x1s", [D, NT], F32)
    hx_in = nc.dram_tensor("hx_in", [D, NT], BF16)
    hx_all = nc.dram_tensor("hx_all", [NCORES * D, NT], BF16)
    with ExitStack() as es:
        A = Arena(nc, es)
        C = Ctx(nc, es)
        V = common_views(A)
        def norm_mix1(t):
            tsl = slice(t * TT, (t + 1) * TT)
            rmsnorm_tile(C, V["xT"][:, :, tsl], V["hT"][:, :, tsl], TT,
                         lambda c: V["vecs"][:, V_GMIX1 + c:V_GMIX1 + c + 1], V["ones"], V["sq"], V["rt"], V["rstd"])
        layer0_body(C, A, V, xT_d, xh_d, vecs_d, invc_d, w_in, w_grp, w_out, wg0, wu0, wd0, next_norm=norm_mix1)
        toks = build_layer1_body(C, A, V, x1s.ap(), None, tab_d, mask_d, R_d, hv_d, w_qkv, w_o, wg1, wu1, wd1,
                                 out_d, own_in_sbuf=True, gather=(hx_in, hx_all), own_normed=True)
        C.S.emit(es, toks)
    return nc


def _rope_tables(pos0):
    inv_freq = (1.0 / (10000.0 ** (np.arange(0, HD, 2, dtype=np.float32) / np.float32(HD)))).astype(np.float32)
    pos = np.arange(pos0 - NT, pos0 + NT, dtype=np.float32)
    ang = (pos[None, :] * inv_freq[:, None]).astype(np.float32)
    cos = np.cos(ang).astype(np.float32)
    sin = np.sin(ang).astype(np.float32)
    rows = np.arange(128)
    ctab = cos[rows % 32]
    sgn = np.where((rows % 64) < 32, -1.0, 1.0).astype(np.float32)
    stab = sin[rows % 32] * sgn[:, None]
    return np.ascontiguousarray(np.stack([ctab, stab]).astype(np.float32))


def _mask_table():
    k = np.arange(128)[:, None]
    q = np.arange(128)[None, :]
    prev = (q <= k).astype(np.float32)
    cur = (k <= q).astype(np.float32)
    one = np.concatenate([prev, cur], axis=1)
    return np.ascontiguousarray(np.tile(one, (1, 4)))


def _rot_matrix():
    m = np.arange(128)
    partner = (m // 64) * 64 + ((m % 64) + 32) % 64
    R = np.zeros((128, 128), np.float32)
    R[partner, m] = 1.0
    return R


_NC_CACHE = {}


def _get_nc(name, builder):
    if name not in _NC_CACHE:
        _NC_CACHE[name] = builder()
    return _NC_CACHE[name]


def kernel_unfused(x, norm_mix, norm_ffn, norm_final, pool_w_in, pool_w_group, pool_scale, pool_w_out,
                   attn_w_qkv, attn_w_out, ffn_w_gate, ffn_w_up, ffn_w_down):
    f = lambda a: np.ascontiguousarray(np.asarray(a, dtype=np.float32))
    x = f(x)
    cps = SEQ // NT
    vecs = _make_vecs(f(norm_mix), f(norm_ffn), f(norm_final), f(pool_scale))
    xs, hs = _core_tokens(x)
    nc0 = _get_nc("l0", build_layer0)
    in0 = []
    for core in range(NCORES):
        in0.append({"xT": xs[core], "xh": hs[core], "vecs": vecs, "invc": _invc_table(core % cps == 0),
                    "w_in": f(pool_w_in[0]), "w_grp": f(pool_w_group[0]), "w_out": f(pool_w_out[0]),
                    "wg0": f(ffn_w_gate[0]), "wu0": f(ffn_w_up[0]), "wd0": f(ffn_w_down[0])})
    r0 = run_bass_kernel_spmd(nc0, in0, core_ids=list(range(NCORES)))
    x1 = [r["x1T"] for r in r0.results]
    nc1 = _get_nc("l1", build_layer1)
    mask = _mask_table()
    rmat = _rot_matrix()
    in1 = []
    for core in range(NCORES):
        first = core % cps == 0
        halo = np.zeros((D, NT), np.float32) if first else x1[core - 1]
        in1.append({"x1T": x1[core], "x1h": halo, "vecs": vecs,
                    "hv": np.full((128, 1), 0.0 if first else 1.0, np.float32),
                    "tab": _rope_tables((core % cps) * NT), "mask": mask, "rmat": rmat,
                    "w_qkv": f(attn_w_qkv[0]), "w_o": f(attn_w_out[0]),
                    "wg1": f(ffn_w_gate[1]), "wu1": f(ffn_w_up[1]), "wd1": f(ffn_w_down[1])})
    r1 = run_bass_kernel_spmd(nc1, in1, core_ids=list(range(NCORES)))
    out = np.stack([r["outT"].T for r in r1.results]).reshape(BATCH, SEQ, D)
    return np.ascontiguousarray(out.astype(np.float32))


def kernel(x, norm_mix, norm_ffn, norm_final, pool_w_in, pool_w_group, pool_scale, pool_w_out,
           attn_w_qkv, attn_w_out, ffn_w_gate, ffn_w_up, ffn_w_down):
    f = lambda a: np.ascontiguousarray(np.asarray(a, dtype=np.float32))
    x = f(x)
    cps = SEQ // NT
    vecs = _make_vecs(f(norm_mix), f(norm_ffn), f(norm_final), f(pool_scale))
    xs, hs = _core_tokens(x)
    nc = _get_nc("fused", build_fused)
    mask = _mask_table()
    rmat = _rot_matrix()
    shared = {"vecs": vecs, "mask": mask, "rmat": rmat,
              "w_in": f(pool_w_in[0]), "w_grp": f(pool_w_group[0]), "w_out": f(pool_w_out[0]),
              "wg0": f(ffn_w_gate[0]), "wu0": f(ffn_w_up[0]), "wd0": f(ffn_w_down[0]),
              "w_qkv": f(attn_w_qkv[0]), "w_o": f(attn_w_out[0]),
              "wg1": f(ffn_w_gate[1]), "wu1": f(ffn_w_up[1]), "wd1": f(ffn_w_down[1])}
    in_maps = []
    for core in range(NCORES):
        first = core % cps == 0
        m = dict(shared)
        m.update({"xT": xs[core], "xh": hs[core], "invc": _invc_table(first),
                  "hv": np.full((128, 1), 0.0 if first else 1.0, np.float32),
                  "tab": _rope_tables((core % cps) * NT)})
        in_maps.append(m)
    r = run_bass_kernel_spmd(nc, in_maps, core_ids=list(range(NCORES)))
    out = np.stack([rr["outT"].T for rr in r.results]).reshape(BATCH, SEQ, D)
    return np.ascontiguousarray(out.astype(np.float32))
```

```python
import numpy as np
from contextlib import ExitStack
import concourse.bass as bass
import concourse.mybir as mybir
from concourse.bass_utils import run_bass_kernel_spmd

F32 = mybir.dt.float32
BF16 = mybir.dt.bfloat16
ALU = mybir.AluOpType
AF = mybir.ActivationFunctionType
DSZ = {F32: 4, BF16: 2}

ENGS = ("pe", "act", "dve", "pool", "sp")
ENG_ATTR = {"pe": "tensor", "act": "scalar", "dve": "vector", "pool": "gpsimd", "sp": "sync"}


class _Op:
    __slots__ = ("fn", "waits", "dma", "dsem", "inc")

    def __init__(self, fn, waits, dma, dsem, inc=16):
        self.fn, self.waits, self.dma, self.dsem, self.inc = fn, waits, dma, dsem, inc


class Sched:
    def __init__(self, nc, n_dma_sems=48):
        self.nc = nc
        self.ops = {e: [] for e in ENGS}
        self.count = {e: 0 for e in ENGS}
        self.waited = {e: {} for e in ENGS}
        self.acc = {}
        self.n_dma = n_dma_sems
        self.dma_val = [0] * n_dma_sems
        half = n_dma_sems // 2
        self.dma_pools = {"sw": list(range(0, half - 2)), "hw": list(range(half, n_dma_sems)),
                          "cc": list(range(half - 2, half))}
        self.dma_rr = {"sw": 0, "hw": 0, "cc": 0}

    @staticmethod
    def rng(ap):
        t = ap.tensor
        name = t.name
        esz = DSZ[ap.dtype]
        pat = ap.ap
        off = ap.offset
        shape = list(t.shape)
        if "DRam" in type(t).__name__ or len(shape) < 2:
            lo = off
            hi = off + sum((c - 1) * abs(s) for s, c in pat) + 1
            return [(name, 0, 1, lo * esz, hi * esz)]
        F = 1
        for s in shape[1:]:
            F *= s
        p0 = off // F
        c0 = off % F
        pcount = pat[0][1]
        free = [(abs(s), c) for s, c in pat[1:] if c > 1]
        hi = c0 + sum((c - 1) * s for s, c in free) + 1
        if name == "ps":
            lo_b = (c0 * esz) // 2048
            hi_b = (hi * esz + 2047) // 2048
            return [(name, 0, 128, lo_b * 2048, hi_b * 2048)]
        if len(free) >= 2:
            free.sort(reverse=True)
            s0, n0 = free[0]
            inner = sum((c - 1) * s for s, c in free[1:]) + 1
            if n0 <= 32 and inner <= s0:
                return [(name, p0, p0 + pcount, (c0 + i * s0) * esz, (c0 + i * s0 + inner) * esz)
                        for i in range(n0)]
        return [(name, p0, p0 + pcount, c0 * esz, hi * esz)]

    def _deps(self, reads, writes):
        toks = []
        for ap in reads:
            for (n, p0, p1, lo, hi) in self.rng(ap):
                a = self.acc.setdefault(n, {"w": [], "r": []})
                for (q0, q1, l, h, tok) in a["w"]:
                    if q0 < p1 and p0 < q1 and l < hi and lo < h:
                        toks.append(tok)
        for ap in writes:
            for (n, p0, p1, lo, hi) in self.rng(ap):
                a = self.acc.setdefault(n, {"w": [], "r": []})
                for (q0, q1, l, h, tok) in a["w"]:
                    if q0 < p1 and p0 < q1 and l < hi and lo < h:
                        toks.append(("W",) + tok)
                for (q0, q1, l, h, tok) in a["r"]:
                    if q0 < p1 and p0 < q1 and l < hi and lo < h:
                        toks.append(("W",) + tok)
        return toks

    def _record(self, reads, writes, tok):
        for ap in writes:
            for (n, p0, p1, lo, hi) in self.rng(ap):
                a = self.acc.setdefault(n, {"w": [], "r": []})
                a["w"] = [e for e in a["w"] if not (p0 <= e[0] and e[1] <= p1 and lo <= e[2] and e[3] <= hi)]
                a["r"] = [e for e in a["r"] if not (p0 <= e[0] and e[1] <= p1 and lo <= e[2] and e[3] <= hi)]
                a["w"].append((p0, p1, lo, hi, tok))
        for ap in reads:
            for (n, p0, p1, lo, hi) in self.rng(ap):
                a = self.acc.setdefault(n, {"w": [], "r": []})
                a["r"] = [e for e in a["r"] if not (e[4][0] == tok[0] and tok[0] != "dma" and p0 <= e[0]
                                                    and e[1] <= p1 and lo <= e[2] and e[3] <= hi)]
                a["r"].append((p0, p1, lo, hi, tok))

    def add(self, eng, fn, reads=(), writes=(), dma=False, extra_waits=(), cc=False):
        ps_reads = [ap for ap in reads if ap.tensor.name == "ps"]
        if ps_reads:
            reads = [ap for ap in reads if ap.tensor.name != "ps"]
            writes = list(writes) + ps_reads
        toks = self._deps(reads, writes) + list(extra_waits)
        waits = []
        wd = self.waited[eng]
        for tok in toks:
            war = False
            if tok[0] == "W":
                war = True
                tok = tok[1:]
            if tok[0] == "dma":
                key, val = ("dma", tok[1]), tok[2]
            else:
                key, val = tok[0], tok[1]
                if key == eng:
                    if eng in ("pe", "sp"):
                        continue
            if wd.get(key, 0) >= val:
                continue
            wd[key] = val
            waits.append((key, val))
        dsem = None
        if dma:
            pk_ = "cc" if cc else ("sw" if eng == "pool" else "hw")
            pool_ = self.dma_pools[pk_]
            dsem = pool_[self.dma_rr[pk_] % len(pool_)]
            self.dma_rr[pk_] += 1
            prev = self.dma_val[dsem]
            key = ("dma", dsem)
            if prev > 0 and wd.get(key, 0) < prev:
                wd[key] = prev
                waits.append((key, prev))
            inc_ = 1 if cc else 16
            self.dma_val[dsem] = prev + inc_
            tok = ("dma", dsem, prev + inc_)
        else:
            self.count[eng] += 1
            tok = (eng, self.count[eng])
        best = {}
        for k, v in waits:
            best[k] = max(best.get(k, 0), v)
        self.ops[eng].append(_Op(fn, list(best.items()), dma, dsem, 1 if cc else 16))
        self._record(reads, writes, tok)
        return tok

    def barrier_tokens(self):
        toks = [(e, self.count[e]) for e in ENGS if self.count[e] > 0]
        toks += [("dma", i, v) for i, v in enumerate(self.dma_val) if v > 0]
        return toks

    def emit(self, es, final_waits):
        nc = self.nc
        sems = {e: es.enter_context(nc.semaphore("s_" + e)) for e in ENGS}
        dsems = [es.enter_context(nc.semaphore("d%d" % i)) for i in range(self.n_dma)]
        block = es.enter_context(nc.Block())

        def semobj(key):
            return dsems[key[1]] if isinstance(key, tuple) else sems[key]

        for e in ENGS:
            deco = getattr(block, ENG_ATTR[e])

            def body(eng, e=e):
                for op in self.ops[e]:
                    for key, val in op.waits:
                        eng.wait_ge(semobj(key), val)
                    ins = op.fn(eng)
                    if op.dma:
                        ins.then_inc(dsems[op.dsem], op.inc)
                    else:
                        ins.then_inc(sems[e], 1)
                if e == "sp":
                    for tok in final_waits:
                        if tok[0] == "dma":
                            eng.wait_ge(dsems[tok[1]], tok[2])
                        else:
                            eng.wait_ge(sems[tok[0]], tok[1])

            deco(body)


NCORES = 8
BATCH, SEQ, D = 2, 8192, 1024
NT = 2048
CPS = SEQ // NT
TT = 512
NTT = NT // TT
DC = D // 128
FF = 2816
FC = FF // 128
FH = 11
HALO = 16
EPS = 1e-6
NH, HD = 16, 64
GROUP_HEADS = ((0, 6), (6, 11), (11, 16))
DIL = (1, 4, 16)
V_GMIX0, V_GFFN0, V_PSC, V_GMIX1, V_GFFN1, V_GFIN = 0, 8, 16, 24, 32, 40

ARENA_BYTES = 212480


class Arena:
    def __init__(self, nc, es, name="arena", nbytes=ARENA_BYTES):
        self.t = es.enter_context(nc.sbuf_tensor(name, [128, nbytes // 2], BF16))
        self.nbytes = nbytes

    def view(self, off, shape, dtype):
        n = 1
        for s in shape:
            n *= s
        esz = DSZ[dtype]
        assert off % 4 == 0 and off + n * esz <= self.nbytes, (off, shape, dtype)
        ap = self.t[:, off // 2: off // 2 + n * esz // 2]
        if dtype != BF16:
            ap = ap.bitcast(dtype)
        if len(shape) == 2:
            ap = ap.rearrange("p (a b) -> p a b", a=shape[0])
        elif len(shape) == 3:
            ap = ap.rearrange("p (a b c) -> p a b c", a=shape[0], b=shape[1])
        return ap


class Ctx:
    def __init__(self, nc, es):
        self.nc = nc
        self.es = es
        self.S = Sched(nc)
        self.ps = es.enter_context(nc.psum_tensor("ps", [128, 8, 512], F32))
        self.bank = 0

    def psum(self):
        b = self.bank
        self.bank = (self.bank + 1) % 8
        return self.ps[:, b, :]

    def mm(self, out, pairs):
        pairs = list(pairs)
        n = len(pairs)

        def fn(e):
            ins = None
            for i, (l, r) in enumerate(pairs):
                ins = e.matmul(out, lhsT=l, rhs=r, start=(i == 0), stop=(i == n - 1))
            return ins
        reads = [p[0] for p in pairs] + [p[1] for p in pairs]
        return self.S.add("pe", fn, reads=reads, writes=[out])

    def dma(self, q, out, in_):
        return self.S.add(q, lambda e: e.dma_start(out=out, in_=in_), reads=[in_], writes=[out], dma=True)

    def act(self, out, in_, func, reads=None, **kw):
        rd = [in_] + [v for v in kw.values() if not isinstance(v, (int, float))]
        return self.S.add("act", lambda e: e.activation(out=out, in_=in_, func=func, **kw), reads=rd, writes=[out])

    def tt(self, eng, out, in0, in1, op):
        return self.S.add(eng, lambda e: e.tensor_tensor(out=out, in0=in0, in1=in1, op=op),
                          reads=[in0, in1], writes=[out])

    def stt(self, eng, out, in0, scalar, in1, op0, op1):
        rd = [in0, in1] + ([] if isinstance(scalar, (int, float)) else [scalar])
        return self.S.add(eng, lambda e: e.scalar_tensor_tensor(out=out, in0=in0, scalar=scalar, in1=in1,
                                                                op0=op0, op1=op1), reads=rd, writes=[out])

    def ts(self, eng, out, in0, scalar1, op0, scalar2=None, op1=None):
        rd = [in0] + [s for s in (scalar1, scalar2) if s is not None and not isinstance(s, (int, float))]
        if op1 is None:
            return self.S.add(eng, lambda e: e.tensor_scalar(out=out, in0=in0, scalar1=scalar1, scalar2=None,
                                                             op0=op0), reads=rd, writes=[out])
        return self.S.add(eng, lambda e: e.tensor_scalar(out=out, in0=in0, scalar1=scalar1, scalar2=scalar2,
                                                         op0=op0, op1=op1), reads=rd, writes=[out])

    def copy(self, eng, out, in_):
        return self.S.add(eng, lambda e: e.tensor_copy(out=out, in_=in_), reads=[in_], writes=[out])

    def recip(self, out, in_):
        return self.S.add("dve", lambda e: e.reciprocal(out=out, in_=in_), reads=[in_], writes=[out])

    def memset(self, eng, out, val):
        return self.S.add(eng, lambda e: e.memset(out, val), writes=[out])


def rmsnorm_tile(C, xsrc, hdst, n, gcol, ones, sq, rt, rstd):
    C.act(sq[:, :, :n], xsrc, AF.Square)
    ps = C.psum()
    C.mm(ps[:, :n], [(ones, sq[:, c, :n]) for c in range(DC)])
    C.act(rt[:, :n], ps[:, :n], AF.Sqrt, bias=EPS, scale=1.0 / D)
    C.recip(rstd[:, :n], rt[:, :n])
    for c in range(DC):
        C.stt("dve", hdst[:, c, :], xsrc[:, c, :], gcol(c), rstd[:, :n], ALU.mult, ALU.mult)


def tile_major_residual(C, xT, pairs_fn, after_tile=None, before_tile=None):
    if before_tile is not None:
        before_tile(0)
    for t in range(NTT):
        tsl = slice(t * TT, (t + 1) * TT)
        if before_tile is not None and t + 1 < NTT:
            before_tile(t + 1)
        for m in range(DC):
            ps = C.psum()
            C.mm(ps, pairs_fn(m, tsl))
            C.tt("dve", xT[:, m, tsl], xT[:, m, tsl], ps, ALU.add)
        if after_tile is not None and t >= 1:
            after_tile(t - 1)
    if after_tile is not None:
        after_tile(NTT - 1)


def ffn_layer(C, A, PB, xT, hT, wg, wu, wd, after_tile=None):
    aT = A.view(PB, [FH, NT], BF16)
    wdh = A.view(PB + 45056, [FH, D], BF16)
    wgu = [(A.view(PB + 67584 + i * 8192, [DC, 256], BF16),
            A.view(PB + 67584 + i * 8192 + 4096, [DC, 256], BF16)) for i in range(2)]
    stmp = [A.view(PB + 83968 + i * 2048, [TT], F32) for i in range(2)]
    wg_v = wg.rearrange("(c p) n -> p c n", p=128)
    wu_v = wu.rearrange("(c p) n -> p c n", p=128)
    wd_v = wd.rearrange("(j p) n -> p j n", p=128)
    gi = 0
    si = 0
    for half in range(2):
        j0 = half * FH
        jl = 0
        while jl < FH:
            nj = min(2, FH - jl)
            bg, bu = wgu[gi % 2]
            gi += 1
            c0 = (j0 + jl) * 128
            C.dma("pool", bg[:, :, 0:nj * 128], wg_v[:, :, c0:c0 + nj * 128])
            C.dma("pool", bu[:, :, 0:nj * 128], wu_v[:, :, c0:c0 + nj * 128])
            if jl == 2:
                C.dma("pool", wdh[:, 0:6, :], wd_v[:, j0:j0 + 6, :])
                C.dma("pool", wdh[:, 6:FH, :], wd_v[:, j0 + 6:j0 + FH, :])
            for jj in range(nj):
                for t in range(NTT):
                    tsl = slice(t * TT, (t + 1) * TT)
                    pg = C.psum()
                    C.mm(pg, [(bg[:, k, jj * 128:(jj + 1) * 128], hT[:, k, tsl]) for k in range(DC)])
                    pu = C.psum()
                    C.mm(pu, [(bu[:, k, jj * 128:(jj + 1) * 128], hT[:, k, tsl]) for k in range(DC)])
                    st = stmp[si % 2]
                    si += 1
                    C.act(st, pg, AF.Silu)
                    C.tt("dve", aT[:, jl + jj, tsl], st, pu, ALU.mult)
            jl += nj
        tile_major_residual(C, xT, lambda m, tsl: [(wdh[:, j, m * 128:(m + 1) * 128], aT[:, j, tsl])
                                                   for j in range(FH)],
                            after_tile if half == 1 else None)


PB = 118784


def common_views(A):
    v = {}
    v["xT"] = A.view(0, [DC, NT], F32)
    v["hT"] = A.view(65536, [DC, NT], BF16)
    v["sq"] = A.view(98304, [DC, TT], BF16)
    v["rt"] = A.view(106496, [TT], F32)
    v["rstd"] = A.view(108544, [TT], F32)
    v["vecs"] = A.view(110592, [48], F32)
    v["ones"] = A.view(110784, [128], BF16)
    v["invc"] = A.view(111040, [DC, HALO], F32)
    v["xh"] = A.view(111552, [DC, HALO], F32)
    v["hh"] = A.view(112064, [DC, HALO], BF16)
    v["t16"] = A.view(112320, [HALO], F32)
    v["hvt"] = A.view(112384, [16], F32)
    return v


def pool_layer(C, A, V, w_in, w_grp, w_out, after_tile=None):
    xT, hT, hh, invc, vecs = V["xT"], V["hT"], V["hh"], V["invc"], V["vecs"]
    pT = A.view(PB, [DC, NT], BF16)
    win = A.view(PB + 32768, [DC, D], BF16)
    wgr = A.view(PB + 49152, [4, 2, 256], BF16)
    NU = NT + HALO
    ub = [A.view(PB + 53248 + i * 8256, [NU], F32) for i in range(2)]
    wa = A.view(PB + 69760, [NU], F32)
    wb = A.view(PB + 69760 + 8256, [NU], F32)
    wout = A.view(PB + 53248, [DC, D], BF16)
    zT = hT
    C.dma("pool", win[:, 0:4, :], w_in.rearrange("(c p) n -> p c n", p=128)[:, 0:4, :])
    C.dma("pool", win[:, 4:8, :], w_in.rearrange("(c p) n -> p c n", p=128)[:, 4:8, :])
    C.dma("pool", wgr, w_grp.rearrange("g (c p) n -> p g c n", p=128))
    for m in range(DC):
        u = ub[m % 2]
        msl = slice(m * 128, (m + 1) * 128)
        ps = C.psum()
        C.mm(ps[:, :HALO], [(win[:, k, msl], hh[:, k, :]) for k in range(DC)])
        C.act(u[:, 0:HALO], ps[:, :HALO], AF.Copy)
        for t in range(NTT):
            tsl = slice(t * TT, (t + 1) * TT)
            ps = C.psum()
            C.mm(ps, [(win[:, k, msl], hT[:, k, tsl]) for k in range(DC)])
            C.act(u[:, HALO + t * TT:HALO + (t + 1) * TT], ps, AF.Copy)
        g = m // 2
        L = g + 1
        w = 2 ** L
        src = u
        bufs = [wa, wb]
        for l in range(1, L + 1):
            sh = 2 ** (l - 1)
            lo = 2 ** l - 1
            dst = bufs[(l - 1) % 2]
            C.tt("dve", dst[:, lo:NU], src[:, lo:NU], src[:, lo - sh:NU - sh], ALU.add)
            src = dst
        C.stt("dve", pT[:, m, :], src[:, HALO:NU], 1.0 / w, u[:, HALO:NU], ALU.mult, ALU.subtract)
        t16 = V["t16"]
        C.tt("dve", t16, src[:, HALO:2 * HALO], invc[:, m, :], ALU.mult)
        C.tt("dve", pT[:, m, 0:HALO], t16, u[:, HALO:2 * HALO], ALU.subtract)
    C.dma("pool", wout[:, 0:4, :], w_out.rearrange("(c p) n -> p c n", p=128)[:, 0:4, :])
    C.dma("pool", wout[:, 4:8, :], w_out.rearrange("(c p) n -> p c n", p=128)[:, 4:8, :])
    for m in range(DC):
        g, mo = m // 2, m % 2
        for t in range(NTT):
            tsl = slice(t * TT, (t + 1) * TT)
            ps = C.psum()
            C.mm(ps, [(wgr[:, g, ki, mo * 128:(mo + 1) * 128], pT[:, 2 * g + ki, tsl]) for ki in range(2)])
            C.act(zT[:, m, tsl], ps, AF.Copy, scale=vecs[:, V_PSC + m:V_PSC + m + 1])
    tile_major_residual(C, xT, lambda m, tsl: [(wout[:, k, m * 128:(m + 1) * 128], zT[:, k, tsl])
                                               for k in range(DC)], after_tile)


def build_layer0():
    nc = bass.Bass("TRN2", target_bir_lowering=False)
    dt_in = lambda n, s: nc.dram_tensor(n, s, F32, kind="ExternalInput").ap()
    xT_d = dt_in("xT", [D, NT])
    xh_d = dt_in("xh", [D, HALO])
    vecs_d = dt_in("vecs", [128, 48])
    invc_d = dt_in("invc", [128, DC * HALO])
    w_in = dt_in("w_in", [D, D])
    w_grp = dt_in("w_grp", [4, 256, 256])
    w_out = dt_in("w_out", [D, D])
    wg = dt_in("wg0", [D, FF])
    wu = dt_in("wu0", [D, FF])
    wd = dt_in("wd0", [FF, D])
    x1_d = nc.dram_tensor("x1T", [D, NT], F32, kind="ExternalOutput").ap()
    with ExitStack() as es:
        A = Arena(nc, es)
        C = Ctx(nc, es)
        V = common_views(A)
        layer0_body(C, A, V, xT_d, xh_d, vecs_d, invc_d, w_in, w_grp, w_out, wg, wu, wd)
        toks = []
        x1v = x1_d.rearrange("(c p) t -> p c t", p=128)
        for t in range(NTT):
            tsl = slice(t * TT, (t + 1) * TT)
            toks.append(C.dma("sp", x1v[:, :, tsl], V["xT"][:, :, tsl]))
        C.S.emit(es, toks)
    return nc


def layer0_body(C, A, V, xT_d, xh_d, vecs_d, invc_d, w_in, w_grp, w_out, wg, wu, wd, next_norm=None):
    xT, hT = V["xT"], V["hT"]
    C.dma("sp", V["vecs"], vecs_d)
    C.dma("sp", V["invc"], invc_d.rearrange("p (c h) -> p c h", c=DC))
    C.dma("sp", V["xh"], xh_d.rearrange("(c p) h -> p c h", p=128))
    xv = xT_d.rearrange("(c p) t -> p c t", p=128)
    for t in range(NTT):
        tsl = slice(t * TT, (t + 1) * TT)
        C.dma("sp", xT[:, :, tsl], xv[:, :, tsl])
    C.memset("dve", V["ones"], 1.0)
    gcol = lambda base: (lambda c: V["vecs"][:, base + c:base + c + 1])
    rmsnorm_tile(C, V["xh"], V["hh"], HALO, gcol(V_GMIX0), V["ones"], V["sq"], V["rt"], V["rstd"])
    for t in range(NTT):
        tsl = slice(t * TT, (t + 1) * TT)
        rmsnorm_tile(C, xT[:, :, tsl], hT[:, :, tsl], TT, gcol(V_GMIX0), V["ones"], V["sq"], V["rt"], V["rstd"])
    def norm_ffn0(t):
        tsl = slice(t * TT, (t + 1) * TT)
        rmsnorm_tile(C, xT[:, :, tsl], hT[:, :, tsl], TT, gcol(V_GFFN0), V["ones"], V["sq"], V["rt"], V["rstd"])
    pool_layer(C, A, V, w_in, w_grp, w_out, after_tile=norm_ffn0)
    ffn_layer(C, A, PB, xT, hT, wg, wu, wd, after_tile=next_norm)


def _vec_layout(v):
    return np.ascontiguousarray(np.asarray(v, np.float32).reshape(DC, 128).T)


def _make_vecs(norm_mix, norm_ffn, norm_final, pool_scale):
    cols = [norm_mix[0], norm_ffn[0], pool_scale[0], norm_mix[1], norm_ffn[1], norm_final]
    return np.ascontiguousarray(np.concatenate([_vec_layout(c) for c in cols], axis=1))


def _invc_table(first):
    t = np.zeros((128, DC, HALO), np.float32)
    for c in range(DC):
        w = 2 ** (c // 2 + 1)
        for i in range(HALO):
            t[:, c, i] = 1.0 / (min(i + 1, w) if first else w)
    return np.ascontiguousarray(t.reshape(128, DC * HALO))


def _core_tokens(x):
    xs, hs = [], []
    cps = SEQ // NT
    for core in range(NCORES):
        b, c = core // cps, core % cps
        xs.append(np.ascontiguousarray(x[b, c * NT:(c + 1) * NT, :].T))
        if c == 0:
            hs.append(np.zeros((D, HALO), np.float32))
        else:
            hs.append(np.ascontiguousarray(x[b, c * NT - HALO:c * NT, :].T))
    return xs, hs


def dsl(start, d, n=128):
    return slice(start, start + d * (n - 1) + 1, d)


def head_info(h):
    for g, (h0, h1) in enumerate(GROUP_HEADS):
        if h0 <= h < h1:
            return g, h0, h1 - h0
    raise ValueError(h)


def layer1_views(A):
    L = {}
    L["kT"] = A.view(0, [DC, NT], BF16)
    L["xs"] = [A.view(i * 16384, [DC, TT], F32) for i in range(2)]
    L["khb"] = A.view(32768, [3, NT], BF16)
    L["khs"] = A.view(45056, [5, TT], BF16)
    L["vh"] = [A.view(50176, [1, 6, HD], BF16), A.view(50176 + 768, [4, 5, HD], BF16),
               A.view(50176 + 768 + 2560, [16, 5, HD], BF16)]
    L["qT"] = A.view(PB, [DC, NT], BF16)
    L["hTh"] = L["qT"]
    vo = PB + 32768
    L["v"] = [A.view(vo, [16, 6, HD], BF16), A.view(vo + 12288, [16, 5, HD], BF16),
              A.view(vo + 12288 + 10240, [16, 5, HD], BF16)]
    L["cos"] = A.view(vo, [2 * NT], F32)
    L["sin"] = A.view(vo + 16384, [2 * NT], F32)
    wo_ = PB + 65536
    L["wqk"] = [A.view(wo_ + i * 2048, [DC, 128], BF16) for i in range(3)]
    L["wv"] = A.view(wo_ + 6144, [DC, 384], BF16)
    L["R"] = A.view(wo_ + 12288, [128], BF16)
    L["wst"] = [A.view(wo_ + 12544 + i * 4096, [DC, 128], F32) for i in range(2)]
    L["Dg"] = A.view(wo_, [3, NT], F32)
    L["wo"] = A.view(PB, [DC, D], BF16)
    b = 98304
    L["qb"] = [A.view(b + i * 1024, [TT], BF16) for i in range(2)]
    L["t1"] = [A.view(b + 2048 + i * 2048, [TT], F32) for i in range(2)]
    L["t2"] = [A.view(b + 6144 + i * 2048, [TT], F32) for i in range(2)]
    L["pT"] = [A.view(b + i * 2048, [1024], BF16) for i in range(2)]
    L["rden"] = [A.view(b + 4096 + i * 2048, [TT], F32) for i in range(2)]
    L["masks"] = A.view(112448, [3, 1024], BF16)
    return L


def build_layer1_body(C, A, V, x1_own_d, x1_halo_d, tab_d, mask_d, R_d, hv_d,
                      w_qkv, w_o, wg, wu, wd, out_d, own_in_sbuf=False, gather=None, own_normed=False):
    S = C.S
    L = layer1_views(A)
    xT, hT, vecs, ones = V["xT"], V["hT"], V["vecs"], V["ones"]
    import os as _os
    _stop = _os.environ.get("KSTOP", "")

    def stop_here(tag, src_f32):
        if _stop != tag:
            return None
        ov_ = out_d.rearrange("(c p) t -> p c t", p=128)
        return [C.dma("sp", ov_[:, :, 0:TT], src_f32)]
    gcol = lambda base: (lambda c: vecs[:, base + c:base + c + 1])
    nrm = lambda xsrc, hdst, n, base: rmsnorm_tile(C, xsrc, hdst, n, gcol(base), ones, V["sq"], V["rt"], V["rstd"])
    x1o = x1_own_d.rearrange("(c p) t -> p c t", p=128)
    x1h = None if gather is not None else x1_halo_d.rearrange("(c p) t -> p c t", p=128)
    wq_v = w_qkv.rearrange("(c p) n -> p c n", p=128)

    masks, Rm, hv = L["masks"], L["R"], V["hvt"]
    C.dma("sp", hv[:, 0:1], hv_d)
    C.dma("pool", Rm, R_d)
    C.dma("pool", masks[:, 0, :], mask_d)
    for i in range(4):
        C.dma("sp", L["cos"][:, i * 1024:(i + 1) * 1024], tab_d[0][:, i * 1024:(i + 1) * 1024])
        C.dma("sp", L["sin"][:, i * 1024:(i + 1) * 1024], tab_d[1][:, i * 1024:(i + 1) * 1024])
    C.copy("pool", masks[:, 1, :], masks[:, 0, :])
    C.ts("dve", masks[:, 1, 0:128], masks[:, 0, 0:128], hv[:, 0:1], ALU.mult)
    C.copy("pool", masks[:, 2, :], masks[:, 0, :])
    m2v = masks[:, 2, :].rearrange("p (e s) -> p e s", e=4)
    m0v = masks[:, 0, :].rearrange("p (e s) -> p e s", e=4)
    C.ts("dve", m2v[:, :, 0:128], m0v[:, :, 0:128], hv[:, 0:1], ALU.mult)

    if not own_in_sbuf:
        for t in range(NTT):
            tsl = slice(t * TT, (t + 1) * TT)
            C.dma("sp", xT[:, :, tsl], x1o[:, :, tsl])
    if not own_normed:
        for t in range(NTT):
            tsl = slice(t * TT, (t + 1) * TT)
            nrm(xT[:, :, tsl], hT[:, :, tsl], TT, V_GMIX1)

    hTh = L["hTh"]
    cnt = {"w": 0, "r": 0}

    pend = []

    def proj_rope(wcol0, src, s_tsl, tab_off, dst):
        wbuf = L["wqk"][(cnt["w"] - 1) % 3]
        i = cnt["r"] % 2
        cnt["r"] += 1
        pk = C.psum()
        C.mm(pk, [(wbuf[:, k, :], src[:, k, s_tsl]) for k in range(DC)])
        qb, t1, t2 = L["qb"][i], L["t1"][i], L["t2"][i]
        C.act(qb, pk, AF.Copy)

        def stage_b():
            pr = C.psum()
            C.mm(pr, [(Rm, qb)])
            C.tt("dve", t1, pk, L["cos"][:, tab_off:tab_off + TT], ALU.mult)
            C.tt("dve", t2, pr, L["sin"][:, tab_off:tab_off + TT], ALU.mult)
            C.tt("dve", dst, t1, t2, ALU.add)
        pend.append(stage_b)
        if len(pend) > 1:
            pend.pop(0)()

    def rope_flush():
        while pend:
            pend.pop(0)()

    cnt["s"] = 0

    def stage_cast(dst_bf, col0, ncol):
        st = L["wst"][cnt["s"] % 2]
        cnt["s"] += 1
        C.dma("sp", st[:, :, 0:ncol], wq_v[:, :, col0:col0 + ncol])
        C.act(dst_bf, st[:, :, 0:ncol], AF.Copy)

    def load_wqk(col0):
        wbuf = L["wqk"][cnt["w"] % 3]
        cnt["w"] += 1
        stage_cast(wbuf, col0, 128)

    def load_wv(g):
        h0, h1 = GROUP_HEADS[g]
        ncol = (h1 - h0) * HD
        c0 = 0
        while c0 < ncol:
            n = min(128, ncol - c0)
            stage_cast(L["wv"][:, :, c0:c0 + n], 2 * D + h0 * HD + c0, n)
            c0 += n

    def v_blocks(src, dstv, g, blocks):
        h0, nh = GROUP_HEADS[g][0], GROUP_HEADS[g][1] - GROUP_HEADS[g][0]
        ncol = nh * HD
        for bi, sl in blocks:
            ps = C.psum()
            C.mm(ps[:, :ncol], [(src[:, k, sl], L["wv"][:, k, 0:ncol]) for k in range(DC)])
            C.act(dstv[:, bi, :, :], ps[:, :ncol].rearrange("p (h d) -> p h d", h=nh), AF.Copy)

    def halo_blocks(g):
        d = DIL[g]
        return [(r, dsl(NT - 128 * d + r, d)) for r in range(d)]

    def own_blocks(g):
        d = DIL[g]
        nb = 16 // d
        return [(r * nb + b, dsl(r + d * 128 * b, d)) for r in range(d) for b in range(nb)]

    def do_k_own():
        for c in range(DC):
            load_wqk(D + c * 128)
            for t in range(NTT):
                tsl = slice(t * TT, (t + 1) * TT)
                proj_rope(D + c * 128, hT, tsl, NT + t * TT, L["kT"][:, c, tsl])
        rope_flush()

    def do_halo():
        if gather is None:
            for t in range(NTT):
                tsl = slice(t * TT, (t + 1) * TT)
                xs = L["xs"][t % 2]
                C.dma("sp", xs, x1h[:, :, tsl])
                nrm(xs, hTh[:, :, tsl], TT, V_GMIX1)
        else:
            pass
        for c in range(DC):
            load_wqk(D + c * 128)
            tiles = range(NTT) if c >= 5 else [NTT - 1]
            for t in tiles:
                tsl = slice(t * TT, (t + 1) * TT)
                dst = L["khb"][:, c - 5, tsl] if c >= 5 else L["khs"][:, c, :]
                proj_rope(D + c * 128, hTh, tsl, t * TT, dst)
        rope_flush()
        for g in range(3):
            load_wv(g)
            v_blocks(hTh, L["vh"][g], g, halo_blocks(g))

    if gather is not None:
        hx_in, hx_all = gather
        x1s_v = x1_own_d.rearrange("(c p) t -> p c t", p=128)
        hin_v = hx_in.ap().rearrange("(c p) t -> p c t", p=128)
        for t in range(NTT):
            tsl = slice(t * TT, (t + 1) * TT)
            C.dma("sp", hin_v[:, :, tsl], hT[:, :, tsl])
            C.dma("sp", x1s_v[:, :, tsl], xT[:, :, tsl])
        groups = [list(range(NCORES))]
        S.add("pool", lambda e: e.collective_compute("AllGather", ALU.bypass, replica_groups=groups,
                                                     ins=[hx_in.ap().opt()], outs=[hx_all.ap().opt()]),
              reads=[hx_in.ap()], writes=[hx_all.ap()], dma=True, cc=True)
        hall_v = hx_all.ap().rearrange("(r c p) t -> r p c t", r=NCORES, p=128)
        for t in range(NTT):
            tsl = slice(t * TT, (t + 1) * TT)

            def rd(e, tsl=tsl):
                prev = (e.partition_id() + (NCORES - 1)) % NCORES
                return e.dma_start(out=hTh[:, :, tsl],
                                   in_=hall_v[bass.ds(prev, 1), :, :, tsl].rearrange("r p c t -> (r p) c t"))
            S.add("pool", rd, reads=[hx_all.ap()], writes=[hTh[:, :, tsl]], dma=True)
        do_k_own()
        do_halo()
    else:
        do_halo()
        do_k_own()

    for c in range(DC):
        load_wqk(c * 128)
        for t in range(NTT):
            tsl = slice(t * TT, (t + 1) * TT)
            proj_rope(c * 128, hT, tsl, NT + t * TT, L["qT"][:, c, tsl])
    rope_flush()
    for g in range(3):
        load_wv(g)
        v_blocks(hT, L["v"][g], g, own_blocks(g))

    _r = stop_here("qkv", L["xs"][1])
    if _r:
        return _r
    oT = hT
    Dg = L["Dg"]
    C.memset("pool", Dg[0:64, :, :], 0.0)
    psS = [C.ps[:, 0:2, :].rearrange("p a b -> p (a b)"), C.ps[:, 2:4, :].rearrange("p a b -> p (a b)")]
    psO = [C.ps[:, 4:6, :].rearrange("p a b -> p (a b)"), C.ps[:, 6:8, :].rearrange("p a b -> p (a b)")]
    def make_batch(h, q, bi):
        g, h0, nh = head_info(h)
        d = DIL[g]
        nb = 16 // d
        c, ro, hl = h // 2, (h % 2) * HD, h - h0
        prow = slice(ro, ro + HD)
        entries = [(r, b) for r in range(d) for b in range(nb)]
        ents = entries[4 * q:4 * q + 4]
        pS, pO, pT, rden = psS[bi % 2], psO[bi % 2], L["pT"][bi % 2], L["rden"][bi % 2]
        n_halo = sum(1 for (r, b) in ents if b == 0)
        mk = masks[:, 0, :] if n_halo == 0 else (masks[:, 1, :] if n_halo == 1 else masks[:, 2, :])
        assert n_halo in (0, 4) or (n_halo == 1 and ents[0][1] == 0)
        kv = []

        def stage_s():
            for e, (r, b) in enumerate(ents):
                start_ = r + d * 128 * b
                qsl = dsl(start_, d)
                qa = L["qT"][prow, c, qsl]
                kcur = L["kT"][prow, c, qsl]
                vcur = L["v"][g][:, r * nb + b, hl, :]
                if b >= 1:
                    kprev = L["kT"][prow, c, dsl(start_ - d * 128, d)]
                    vprev = L["v"][g][:, r * nb + b - 1, hl, :]
                else:
                    hsl = dsl(NT - 128 * d + r, d)
                    if c >= 5:
                        kprev = L["khb"][prow, c - 5, hsl]
                    else:
                        kprev = L["khs"][prow, c, dsl(hsl.start - (NT - TT), d)]
                    vprev = L["vh"][g][:, r, hl, :]
                C.mm(pS[:, e * 256:e * 256 + 128], [(kprev, qa)])
                C.mm(pS[:, e * 256 + 128:e * 256 + 256], [(kcur, qa)])
                kv.append((vprev, vcur))
            C.act(pT, pS, AF.Exp, scale=HD ** -0.5)
            C.tt("dve", pT, pT, mk, ALU.mult)

        def stage_pv():
            for e, (vprev, vcur) in enumerate(kv):
                pp, pc = pT[:, e * 256:e * 256 + 128], pT[:, e * 256 + 128:e * 256 + 256]
                C.mm(pO[0:64, e * 128:(e + 1) * 128], [(vprev, pp), (vcur, pc)])
            pTv = pT.rearrange("p (e s) -> p e s", e=4)
            num = pO[0:64, 0:512].rearrange("p (e s) -> p e s", e=4)
            den = pO[0:64, 512:1024].rearrange("p (e s) -> p e s", e=4)
            C.mm(den, [(ones[:, 0:64], pTv[:, :, 0:128]), (ones[:, 0:64], pTv[:, :, 128:256])])
            rdv = rden[0:64, :].rearrange("p (e s) -> p e s", e=4)
            r0 = ents[0][0]
            if d == 1:
                tv = lambda ap3: ap3[:, q * TT:(q + 1) * TT].rearrange("p (e s) -> p e s", e=4)
            elif d == 4:
                tv = lambda ap3: ap3[:, dsl(r0, 4, 512)].rearrange("p (e s) -> p e s", e=4)
            else:
                tv = lambda ap3: ap3[:, :].rearrange("p (s r) -> p r s", r=16)[:, r0:r0 + 4, :]
            C.act(rdv, den, AF.Ln)
            C.act(rdv, rdv, AF.Exp, scale=-1.0)
            C.tt("dve", tv(oT[prow, c, :]), num, rdv, ALU.mult)
            dgv = tv(Dg[0:64, g, :])
            C.tt("dve", dgv, den, dgv, ALU.add)
        return stage_s, stage_pv

    batches = [make_batch(h, q, h * 4 + q) for h in range(NH) for q in range(4)]
    batches[0][0]()
    for i, (fs, fpv) in enumerate(batches):
        if i + 1 < len(batches):
            batches[i + 1][0]()
        fpv()

    _r = stop_here("attn", L["cos"][:, 0:4096].rearrange("p (c t) -> p c t", c=DC))
    if _r:
        return _r
    C.dma("pool", L["wo"][:, 0:4, :], w_o.rearrange("(c p) n -> p c n", p=128)[:, 0:4, :])
    C.dma("pool", L["wo"][:, 4:8, :], w_o.rearrange("(c p) n -> p c n", p=128)[:, 4:8, :])
    for t in range(NTT):
        tsl = slice(t * TT, (t + 1) * TT)
        C.dma("sp", xT[:, :, tsl], x1o[:, :, tsl])
    tot = A.view(PB + 90112, [TT], F32)
    ng = [GROUP_HEADS[g][1] - GROUP_HEADS[g][0] for g in range(3)]

    def merge_tile(t):
        tsl = slice(t * TT, (t + 1) * TT)
        tv_ = tot[0:64, :]
        C.ts("dve", tv_, Dg[0:64, 0, tsl], 1.0 / ng[0], ALU.mult)
        C.stt("dve", tv_, Dg[0:64, 1, tsl], 1.0 / ng[1], tv_, ALU.mult, ALU.add)
        C.stt("dve", tv_, Dg[0:64, 2, tsl], 1.0 / ng[2], tv_, ALU.mult, ALU.add)
        C.recip(tv_, tv_)
        for g in range(3):
            C.stt("dve", Dg[0:64, g, tsl], Dg[0:64, g, tsl], 3.0 / ng[g], tv_, ALU.mult, ALU.mult)
        C.act(Dg[64:128, :, tsl], Dg[0:64, :, tsl], AF.Copy)
        for c in range(DC):
            ga, gb = head_info(2 * c)[0], head_info(2 * c + 1)[0]
            if ga == gb:
                C.tt("pool", oT[:, c, tsl], oT[:, c, tsl], Dg[:, ga, tsl], ALU.mult)
            else:
                C.tt("pool", oT[0:64, c, tsl], oT[0:64, c, tsl], Dg[0:64, ga, tsl], ALU.mult)
                C.tt("pool", oT[64:128, c, tsl], oT[64:128, c, tsl], Dg[64:128, gb, tsl], ALU.mult)

    def norm_ffn1(t):
        tsl = slice(t * TT, (t + 1) * TT)
        nrm(xT[:, :, tsl], hT[:, :, tsl], TT, V_GFFN1)
    tile_major_residual(C, xT, lambda m, tsl: [(L["wo"][:, k, m * 128:(m + 1) * 128], oT[:, k, tsl])
                                               for k in range(DC)], norm_ffn1, before_tile=merge_tile)

    _r = stop_here("oproj", xT[:, :, 0:TT])
    if _r:
        return _r
    ob = [A.view(65536 + i * 16384, [DC, TT], F32) for i in range(2)]
    ov = out_d.rearrange("(c p) t -> p c t", p=128)
    toks = []

    def norm_final(t):
        tsl = slice(t * TT, (t + 1) * TT)
        rmsnorm_tile(C, xT[:, :, tsl], ob[t % 2], TT, gcol(V_GFIN), ones, V["sq"], V["rt"], V["rstd"])
        toks.append(C.dma("sp", ov[:, :, tsl], ob[t % 2]))
    ffn_layer(C, A, PB, xT, hT, wg, wu, wd, after_tile=norm_final)
    return toks


def build_layer1():
    nc = bass.Bass("TRN2", target_bir_lowering=False)
    dt_in = lambda n, s: nc.dram_tensor(n, s, F32, kind="ExternalInput").ap()
    x1o = dt_in("x1T", [D, NT])
    x1h = dt_in("x1h", [D, NT])
    vecs_d = dt_in("vecs", [128, 48])
    hv_d = dt_in("hv", [128, 1])
    tab_d = dt_in("tab", [2, 128, 2 * NT])
    mask_d = dt_in("mask", [128, 1024])
    R_d = dt_in("rmat", [128, 128])
    w_qkv = dt_in("w_qkv", [D, 3 * D])
    w_o = dt_in("w_o", [D, D])
    wg = dt_in("wg1", [D, FF])
    wu = dt_in("wu1", [D, FF])
    wd = dt_in("wd1", [FF, D])
    out_d = nc.dram_tensor("outT", [D, NT], F32, kind="ExternalOutput").ap()
    with ExitStack() as es:
        A = Arena(nc, es)
        C = Ctx(nc, es)
        V = common_views(A)
        C.dma("sp", V["vecs"], vecs_d)
        C.memset("dve", V["ones"], 1.0)
        toks = build_layer1_body(C, A, V, x1o, x1h, tab_d, mask_d, R_d, hv_d, w_qkv, w_o, wg, wu, wd, out_d)
        C.S.emit(es, toks)
    return nc


def build_fused():
    nc = bass.Bass("TRN2", target_bir_lowering=False)
    dt_in = lambda n, s: nc.dram_tensor(n, s, F32, kind="ExternalInput").ap()
    xT_d = dt_in("xT", [D, NT])
    xh_d = dt_in("xh", [D, HALO])
    vecs_d = dt_in("vecs", [128, 48])
    invc_d = dt_in("invc", [128, DC * HALO])
    hv_d = dt_in("hv", [128, 1])
    tab_d = dt_in("tab", [2, 128, 2 * NT])
    mask_d = dt_in("mask", [128, 1024])
    R_d = dt_in("rmat", [128, 128])
    w_in = dt_in("w_in", [D, D])
    w_grp = dt_in("w_grp", [4, 256, 256])
    w_out = dt_in("w_out", [D, D])
    wg0 = dt_in("wg0", [D, FF])
    wu0 = dt_in("wu0", [D, FF])
    wd0 = dt_in("wd0", [FF, D])
    w_qkv = dt_in("w_qkv", [D, 3 * D])
    w_o = dt_in("w_o", [D, D])
    wg1 = dt_in("wg1", [D, FF])
    wu1 = dt_in("wu1", [D, FF])
    wd1 = dt_in("wd1", [FF, D])
    out_d = nc.dram_tensor("outT", [D, NT], F32, kind="ExternalOutput").ap()
    x1s = nc.dram_tensor("x1s", [D, NT], F32)
    hx_in = nc.dram_tensor("hx_in", [D, NT], BF16)
    hx_all = nc.dram_tensor("hx_all", [NCORES * D, NT], BF16)
    with ExitStack() as es:
        A = Arena(nc, es)
        C = Ctx(nc, es)
        V = common_views(A)
        def norm_mix1(t):
            tsl = slice(t * TT, (t + 1) * TT)
            rmsnorm_tile(C, V["xT"][:, :, tsl], V["hT"][:, :, tsl], TT,
                         lambda c: V["vecs"][:, V_GMIX1 + c:V_GMIX1 + c + 1], V["ones"], V["sq"], V["rt"], V["rstd"])
        layer0_body(C, A, V, xT_d, xh_d, vecs_d, invc_d, w_in, w_grp, w_out, wg0, wu0, wd0, next_norm=norm_mix1)
        toks = build_layer1_body(C, A, V, x1s.ap(), None, tab_d, mask_d, R_d, hv_d, w_qkv, w_o, wg1, wu1, wd1,
                                 out_d, own_in_sbuf=True, gather=(hx_in, hx_all), own_normed=True)
        C.S.emit(es, toks)
    return nc


def _rope_tables(pos0):
    inv_freq = (1.0 / (10000.0 ** (np.arange(0, HD, 2, dtype=np.float32) / np.float32(HD)))).astype(np.float32)
    pos = np.arange(pos0 - NT, pos0 + NT, dtype=np.float32)
    ang = (pos[None, :] * inv_freq[:, None]).astype(np.float32)
    cos = np.cos(ang).astype(np.float32)
    sin = np.sin(ang).astype(np.float32)
    rows = np.arange(128)
    ctab = cos[rows % 32]
    sgn = np.where((rows % 64) < 32, -1.0, 1.0).astype(np.float32)
    stab = sin[rows % 32] * sgn[:, None]
    return np.ascontiguousarray(np.stack([ctab, stab]).astype(np.float32))


def _mask_table():
    k = np.arange(128)[:, None]
    q = np.arange(128)[None, :]
    prev = (q <= k).astype(np.float32)
    cur = (k <= q).astype(np.float32)
    one = np.concatenate([prev, cur], axis=1)
    return np.ascontiguousarray(np.tile(one, (1, 4)))


def _rot_matrix():
    m = np.arange(128)
    partner = (m // 64) * 64 + ((m % 64) + 32) % 64
    R = np.zeros((128, 128), np.float32)
    R[partner, m] = 1.0
    return R


_NC_CACHE = {}


def _get_nc(name, builder):
    if name not in _NC_CACHE:
        _NC_CACHE[name] = builder()
    return _NC_CACHE[name]


def kernel_unfused(x, norm_mix, norm_ffn, norm_final, pool_w_in, pool_w_group, pool_scale, pool_w_out,
                   attn_w_qkv, attn_w_out, ffn_w_gate, ffn_w_up, ffn_w_down):
    f = lambda a: np.ascontiguousarray(np.asarray(a, dtype=np.float32))
    x = f(x)
    cps = SEQ // NT
    vecs = _make_vecs(f(norm_mix), f(norm_ffn), f(norm_final), f(pool_scale))
    xs, hs = _core_tokens(x)
    nc0 = _get_nc("l0", build_layer0)
    in0 = []
    for core in range(NCORES):
        in0.append({"xT": xs[core], "xh": hs[core], "vecs": vecs, "invc": _invc_table(core % cps == 0),
                    "w_in": f(pool_w_in[0]), "w_grp": f(pool_w_group[0]), "w_out": f(pool_w_out[0]),
                    "wg0": f(ffn_w_gate[0]), "wu0": f(ffn_w_up[0]), "wd0": f(ffn_w_down[0])})
    r0 = run_bass_kernel_spmd(nc0, in0, core_ids=list(range(NCORES)))
    x1 = [r["x1T"] for r in r0.results]
    nc1 = _get_nc("l1", build_layer1)
    mask = _mask_table()
    rmat = _rot_matrix()
    in1 = []
    for core in range(NCORES):
        first = core % cps == 0
        halo = np.zeros((D, NT), np.float32) if first else x1[core - 1]
        in1.append({"x1T": x1[core], "x1h": halo, "vecs": vecs,
                    "hv": np.full((128, 1), 0.0 if first else 1.0, np.float32),
                    "tab": _rope_tables((core % cps) * NT), "mask": mask, "rmat": rmat,
                    "w_qkv": f(attn_w_qkv[0]), "w_o": f(attn_w_out[0]),
                    "wg1": f(ffn_w_gate[1]), "wu1": f(ffn_w_up[1]), "wd1": f(ffn_w_down[1])})
    r1 = run_bass_kernel_spmd(nc1, in1, core_ids=list(range(NCORES)))
    out = np.stack([r["outT"].T for r in r1.results]).reshape(BATCH, SEQ, D)
    return np.ascontiguousarray(out.astype(np.float32))


def kernel(x, norm_mix, norm_ffn, norm_final, pool_w_in, pool_w_group, pool_scale, pool_w_out,
           attn_w_qkv, attn_w_out, ffn_w_gate, ffn_w_up, ffn_w_down):
    f = lambda a: np.ascontiguousarray(np.asarray(a, dtype=np.float32))
    x = f(x)
    cps = SEQ // NT
    vecs = _make_vecs(f(norm_mix), f(norm_ffn), f(norm_final), f(pool_scale))
    xs, hs = _core_tokens(x)
    nc = _get_nc("fused", build_fused)
    mask = _mask_table()
    rmat = _rot_matrix()
    shared = {"vecs": vecs, "mask": mask, "rmat": rmat,
              "w_in": f(pool_w_in[0]), "w_grp": f(pool_w_group[0]), "w_out": f(pool_w_out[0]),
              "wg0": f(ffn_w_gate[0]), "wu0": f(ffn_w_up[0]), "wd0": f(ffn_w_down[0]),
              "w_qkv": f(attn_w_qkv[0]), "w_o": f(attn_w_out[0]),
              "wg1": f(ffn_w_gate[1]), "wu1": f(ffn_w_up[1]), "wd1": f(ffn_w_down[1])}
    in_maps = []
    for core in range(NCORES):
        first = core % cps == 0
        m = dict(shared)
        m.update({"xT": xs[core], "xh": hs[core], "invc": _invc_table(first),
                  "hv": np.full((128, 1), 0.0 if first else 1.0, np.float32),
                  "tab": _rope_tables((core % cps) * NT)})
        in_maps.append(m)
    r = run_bass_kernel_spmd(nc, in_maps, core_ids=list(range(NCORES)))
    out = np.stack([rr["outT"].T for rr in r.results]).reshape(BATCH, SEQ, D)
    return np.ascontiguousarray(out.astype(np.float32))
```
